# Optimizing a Trainium2 kernel written in Bass

```python
import math
import jax, jax.numpy as jnp
from jax import lax
import numpy as np

D_MODEL = 1024
BATCH = 4
SEQ = 4096
DEPTH = 2

HEAD_DIM = 64
ROPE_THETA = 500000.0
ROPE_FRAC = 4
EPS = 1e-6
Q_BLOCK = 128

A_WIDTH = D_MODEL // 2
A_GROUPS = 4
A_GDIM = A_WIDTH // A_GROUPS
A_CHUNK = 128

B_HEADS = 8
B_QK_DIM = 32
B_V_DIM = 2 * B_QK_DIM
B_WIDTH = B_HEADS * B_V_DIM

C_WIDTH = D_MODEL // 2
C_KERNEL = 31

D_HEADS = 8
D_KV_GROUPS = 2
D_WIDTH = D_HEADS * HEAD_DIM
CMP_BLOCK = 32
CMP_STRIDE = 16
CMP_HIDDEN = 2 * HEAD_DIM
SLC_BLOCK = 64
SLC_TOPK = 16
WINDOW = 512

MEM_TOKENS = 256
M_HEADS = 4
M_WIDTH = M_HEADS * HEAD_DIM

MIX_WIDTH = A_WIDTH + B_WIDTH + M_WIDTH
EVEN_SPLITS = (A_WIDTH, A_WIDTH, A_WIDTH,
               2 * B_HEADS * B_QK_DIM, 2 * B_HEADS * B_QK_DIM, B_WIDTH, B_WIDTH,
               M_WIDTH, M_WIDTH)
ODD_SPLITS = (C_WIDTH, C_WIDTH, C_WIDTH,
              D_WIDTH, 6 * D_KV_GROUPS * HEAD_DIM, 3 * D_HEADS, D_WIDTH,
              M_WIDTH, M_WIDTH)
EVEN_IN = sum(EVEN_SPLITS)
ODD_IN = sum(ODD_SPLITS)

kernel_name = "hybrid_sgu_diffattn_conformer_nsa_block"


def rms_norm(x, g):
    xf = x.astype(jnp.float32)
    y = xf * lax.rsqrt(jnp.mean(xf * xf, axis=-1, keepdims=True) + EPS)
    return (y * g.astype(jnp.float32)).astype(x.dtype)


def rope_partial(x, pos):
    d = x.shape[-1]
    rd = d // ROPE_FRAC
    half = rd // 2
    inv = ROPE_THETA ** (-jnp.arange(half, dtype=jnp.float32) / half)
    ang = pos.astype(jnp.float32)[:, None] * inv[None, :]
    cos = jnp.cos(ang).astype(x.dtype)
    sin = jnp.sin(ang).astype(x.dtype)
    x1, x2, rest = x[..., :half], x[..., half:rd], x[..., rd:]
    return jnp.concatenate([x1 * cos - x2 * sin, x1 * sin + x2 * cos, rest], axis=-1)


def masked_softmax(s, mask):
    s = jnp.where(mask, s, -jnp.inf)
    m = jnp.max(s, axis=-1, keepdims=True)
    m = jnp.where(jnp.isfinite(m), m, 0.0)
    e = jnp.where(mask, jnp.exp(s - m), 0.0)
    return e / jnp.maximum(jnp.sum(e, axis=-1, keepdims=True), jnp.finfo(jnp.float32).tiny)


def split_cols(x, sizes):
    return jnp.split(x, [int(v) for v in np.cumsum(sizes)[:-1]], axis=-1)


def chunked_sgu(u, v, w_s, b_s, v_gain):
    Bz, S, _ = v.shape
    v = rms_norm(v.reshape(Bz, S, A_GROUPS, A_GDIM), v_gain.reshape(A_GROUPS, A_GDIM))
    v = v.reshape(Bz, S // A_CHUNK, A_CHUNK, A_GROUPS, A_GDIM)
    causal = jnp.tril(jnp.ones((A_CHUNK, A_CHUNK), dtype=bool))
    w = jnp.where(causal[None], w_s, 0.0)
    z = jnp.einsum('gts,bcsgd->bctgd', w, v) + b_s.T[:, :, None]
    return u * z.reshape(Bz, S, A_WIDTH)


def diff_attention(q, k, v, lam):
    Bz, H, _, S, dk = q.shape
    dv = v.shape[-1]
    nblk = S // Q_BLOCK
    scale = dk ** -0.5
    kpos = jnp.arange(S)
    qb = q.reshape(Bz, H, 2, nblk, Q_BLOCK, dk).transpose(3, 0, 1, 2, 4, 5)

    def block(args):
        qi, c = args
        qpos = c * Q_BLOCK + jnp.arange(Q_BLOCK)
        s = jnp.einsum('bhmqd,bhmkd->bhmqk', qi, k).astype(jnp.float32) * scale
        s = jnp.where(kpos[None, :] <= qpos[:, None], s, -jnp.inf)
        p = jax.nn.softmax(s, axis=-1)
        pd = p[:, :, 0] - lam * p[:, :, 1]
        return jnp.einsum('bhqk,bhkd->bhqd', pd.astype(v.dtype), v)

    o = lax.map(block, (qb, jnp.arange(nblk)))
    return o.transpose(1, 0, 3, 2, 4).reshape(Bz, S, H, dv)


def memory_attention(q, mem_n, w_kv, qnorm, knorm):
    Bz, S, _ = q.shape
    k, v = jnp.split(mem_n @ w_kv, 2, axis=-1)
    q = rms_norm(q.reshape(Bz, S, M_HEADS, HEAD_DIM), qnorm)
    k = rms_norm(k.reshape(Bz, -1, M_HEADS, HEAD_DIM), knorm)
    v = v.reshape(Bz, -1, M_HEADS, HEAD_DIM)
    s = jnp.einsum('bshd,bmhd->bhsm', q, k).astype(jnp.float32) * (HEAD_DIM ** -0.5)
    p = jax.nn.softmax(s, axis=-1)
    return jnp.einsum('bhsm,bmhd->bshd', p.astype(v.dtype), v).reshape(Bz, S, M_WIDTH)


def conformer_conv(a, b, conv_w, conv_b, norm_g):
    h = a * jax.nn.sigmoid(b)
    hp = jnp.pad(h, ((0, 0), (C_KERNEL - 1, 0), (0, 0)))
    y = lax.conv_general_dilated(hp, conv_w[:, None, :], window_strides=(1,), padding='VALID',
                                 dimension_numbers=('NWC', 'WIO', 'NWC'),
                                 feature_group_count=C_WIDTH) + conv_b
    return jax.nn.silu(rms_norm(y, norm_g))


def compress(t, cmp_idx, pos_emb, w1, w2):
    Bz, _, G, hd = t.shape
    blocks = t[:, cmp_idx] + pos_emb[:, None, :]
    flat = blocks.transpose(0, 1, 3, 2, 4).reshape(Bz, cmp_idx.shape[0], G, CMP_BLOCK * hd)
    return jax.nn.silu(flat @ w1) @ w2


def cmp_to_slc_overlap(n_cmp, n_slc):
    start = np.arange(n_cmp) * CMP_STRIDE
    s0 = np.arange(n_slc) * SLC_BLOCK
    lo = np.maximum(start[:, None], s0[None, :])
    hi = np.minimum(start[:, None] + CMP_BLOCK, s0[None, :] + SLC_BLOCK)
    return (np.clip(hi - lo, 0, None) / CMP_BLOCK).astype(np.float32)


def nsa(q, kv, gates, pos, qnorm, knorm, cmp_pos_k, cmp_w1_k, cmp_w2_k,
        cmp_pos_v, cmp_w1_v, cmp_w2_v):
    Bz, S = q.shape[:2]
    G, R, hd = D_KV_GROUPS, D_HEADS // D_KV_GROUPS, HEAD_DIM
    nblk = S // Q_BLOCK
    n_cmp = (S - CMP_BLOCK) // CMP_STRIDE + 1
    n_slc = S // SLC_BLOCK
    n_sel = min(SLC_TOPK, n_slc)
    scale = hd ** -0.5

    q = rope_partial(rms_norm(q, qnorm).transpose(0, 2, 1, 3), pos).reshape(Bz, G, R, S, hd)
    k_cmp_raw, v_cmp_raw, k_slc, v_slc, k_win, v_win = [kv[:, :, i] for i in range(6)]

    cmp_idx = np.arange(n_cmp)[:, None] * CMP_STRIDE + np.arange(CMP_BLOCK)[None, :]
    cmp_end = jnp.asarray(cmp_idx[:, -1])
    kc = compress(k_cmp_raw, cmp_idx, cmp_pos_k, cmp_w1_k, cmp_w2_k)
    vc = compress(v_cmp_raw, cmp_idx, cmp_pos_v, cmp_w1_v, cmp_w2_v).transpose(0, 2, 1, 3)
    kc = rope_partial(rms_norm(kc, knorm).transpose(0, 2, 1, 3), cmp_end)

    ks = rope_partial(rms_norm(k_slc, knorm).transpose(0, 2, 1, 3), pos)
    kb = ks.reshape(Bz, G, n_slc, SLC_BLOCK, hd)
    vb = v_slc.transpose(0, 2, 1, 3).reshape(Bz, G, n_slc, SLC_BLOCK, hd)

    pad = ((0, 0), (0, 0), (WINDOW, 0), (0, 0))
    kw = jnp.pad(rope_partial(rms_norm(k_win, knorm).transpose(0, 2, 1, 3), pos), pad)
    vw = jnp.pad(v_win.transpose(0, 2, 1, 3), pad)

    overlap = jnp.asarray(cmp_to_slc_overlap(n_cmp, n_slc))
    b_ix = jnp.arange(Bz)[:, None, None, None]
    g_ix = jnp.arange(G)[None, :, None, None]
    blk_tok = jnp.arange(SLC_BLOCK)
    slc_ids = jnp.arange(n_slc)
    win_off = jnp.arange(WINDOW + Q_BLOCK) - WINDOW

    qb = q.reshape(Bz, G, R, nblk, Q_BLOCK, hd).transpose(3, 0, 1, 2, 4, 5)
    gb = gates.transpose(0, 2, 1, 3).reshape(Bz, G, R, nblk, Q_BLOCK, 3).transpose(3, 0, 1, 2, 4, 5)

    def block(args):
        qi, gt, c = args
        qpos = c * Q_BLOCK + jnp.arange(Q_BLOCK)
        s = jnp.einsum('bgrqd,bgnd->bgrqn', qi, kc).astype(jnp.float32) * scale
        p_c = masked_softmax(s, cmp_end[None, :] <= qpos[:, None])
        o_c = jnp.einsum('bgrqn,bgnd->bgrqd', p_c.astype(vc.dtype), vc)
        imp = jnp.einsum('bgrqn,nj->bgqj', p_c, overlap)
        cur = (qpos // SLC_BLOCK)[:, None]
        imp = jnp.where(slc_ids[None, :] > cur, -jnp.inf, imp)
        forced = (slc_ids[None, :] == 0) | (slc_ids[None, :] == cur) | (slc_ids[None, :] == cur - 1)
        imp = jnp.where(forced, jnp.inf, imp)
        _, sel = lax.top_k(imp, n_sel)
        ksel = kb[b_ix, g_ix, sel].reshape(Bz, G, Q_BLOCK, n_sel * SLC_BLOCK, hd)
        vsel = vb[b_ix, g_ix, sel].reshape(Bz, G, Q_BLOCK, n_sel * SLC_BLOCK, hd)
        kpos = (sel[..., None] * SLC_BLOCK + blk_tok).reshape(Bz, G, Q_BLOCK, n_sel * SLC_BLOCK)
        s = jnp.einsum('bgrqd,bgqkd->bgrqk', qi, ksel).astype(jnp.float32) * scale
        p_s = masked_softmax(s, (kpos <= qpos[:, None])[:, :, None])
        o_s = jnp.einsum('bgrqk,bgqkd->bgrqd', p_s.astype(vsel.dtype), vsel)
        kwi = lax.dynamic_slice_in_dim(kw, c * Q_BLOCK, WINDOW + Q_BLOCK, axis=2)
        vwi = lax.dynamic_slice_in_dim(vw, c * Q_BLOCK, WINDOW + Q_BLOCK, axis=2)
        wpos = c * Q_BLOCK + win_off
        dpos = qpos[:, None] - wpos[None, :]
        s = jnp.einsum('bgrqd,bgkd->bgrqk', qi, kwi).astype(jnp.float32) * scale
        p_w = masked_softmax(s, (wpos[None, :] >= 0) & (dpos >= 0) & (dpos < WINDOW))
        o_w = jnp.einsum('bgrqk,bgkd->bgrqd', p_w.astype(vwi.dtype), vwi)
        return gt[..., 0:1] * o_c + gt[..., 1:2] * o_s + gt[..., 2:3] * o_w

    o = lax.map(block, (qb, gb, jnp.arange(nblk)))
    return o.transpose(1, 0, 4, 2, 3, 5).reshape(Bz, S, D_WIDTH)


def even_layer(x, mem_n, pos, layer, norm_g, w_in, a_vnorm, a_ws, a_bs, b_qnorm, b_knorm,
               b_lq1, b_lk1, b_lq2, b_lk2, b_subln, m_wkv, m_qnorm, m_knorm, w_out):
    Bz, S, _ = x.shape
    h = rms_norm(x, norm_g)
    a_u, a_v, a_g, b_q, b_k, b_v, b_g, m_q, m_g = split_cols(h @ w_in, EVEN_SPLITS)
    y_a = chunked_sgu(jax.nn.gelu(a_u), jax.nn.gelu(a_v), a_ws, a_bs, a_vnorm) * jax.nn.silu(a_g)
    lam_init = 0.8 - 0.6 * math.exp(-0.3 * layer)
    lam = (jnp.exp(jnp.sum(b_lq1.astype(jnp.float32) * b_lk1.astype(jnp.float32)))
           - jnp.exp(jnp.sum(b_lq2.astype(jnp.float32) * b_lk2.astype(jnp.float32))) + lam_init)
    q = rms_norm(b_q.reshape(Bz, S, B_HEADS, 2, B_QK_DIM), b_qnorm).transpose(0, 2, 3, 1, 4)
    k = rms_norm(b_k.reshape(Bz, S, B_HEADS, 2, B_QK_DIM), b_knorm).transpose(0, 2, 3, 1, 4)
    q, k = rope_partial(q, pos), rope_partial(k, pos)
    v = b_v.reshape(Bz, S, B_HEADS, B_V_DIM).transpose(0, 2, 1, 3)
    o_b = rms_norm(diff_attention(q, k, v, lam), b_subln) * (1.0 - lam_init)
    y_b = o_b.reshape(Bz, S, B_WIDTH) * jax.nn.silu(b_g)
    y_m = memory_attention(m_q, mem_n, m_wkv, m_qnorm, m_knorm) * jax.nn.silu(m_g)
    return x + jnp.concatenate([y_a, y_b, y_m], axis=-1) @ w_out


def odd_layer(x, mem_n, pos, norm_g, w_in, c_conv_w, c_conv_b, c_norm, d_qnorm, d_knorm,
              cmp_pos_k, cmp_w1_k, cmp_w2_k, cmp_pos_v, cmp_w1_v, cmp_w2_v,
              m_wkv, m_qnorm, m_knorm, w_out):
    Bz, S, _ = x.shape
    h = rms_norm(x, norm_g)
    c_a, c_b, c_g, d_q, d_kv, d_bg, d_g, m_q, m_g = split_cols(h @ w_in, ODD_SPLITS)
    y_c = conformer_conv(c_a, c_b, c_conv_w, c_conv_b, c_norm) * jax.nn.silu(c_g)
    kv = d_kv.reshape(Bz, S, 6, D_KV_GROUPS, HEAD_DIM)
    gates = jax.nn.sigmoid(d_bg.reshape(Bz, S, D_HEADS, 3))
    y_d = nsa(d_q.reshape(Bz, S, D_HEADS, HEAD_DIM), kv, gates, pos, d_qnorm, d_knorm,
              cmp_pos_k, cmp_w1_k, cmp_w2_k, cmp_pos_v, cmp_w1_v, cmp_w2_v) * jax.nn.silu(d_g)
    y_m = memory_attention(m_q, mem_n, m_wkv, m_qnorm, m_knorm) * jax.nn.silu(m_g)
    return x + jnp.concatenate([y_c, y_d, y_m], axis=-1) @ w_out


def setup_inputs(seed: int = 0) -> dict:
    key = jax.random.key(seed)
    keys = iter(jax.random.split(key, 48))

    def nrm(shape, scale):
        return jax.random.normal(next(keys), shape, jnp.float32) * scale

    def gain(n):
        return 1.0 + nrm((n,), 0.05)

    return {
        "x": nrm((BATCH, SEQ, D_MODEL), 1.0),
        "mem": nrm((BATCH, MEM_TOKENS, D_MODEL), 1.0),
        "mem_norm": gain(D_MODEL),
        "l0_norm": gain(D_MODEL),
        "l0_w_in": nrm((D_MODEL, EVEN_IN), D_MODEL ** -0.5),
        "l0_a_vnorm": gain(A_WIDTH),
        "l0_a_ws": nrm((A_GROUPS, A_CHUNK, A_CHUNK), A_CHUNK ** -0.5),
        "l0_a_bs": 1.0 + nrm((A_GROUPS, A_CHUNK), 0.1),
        "l0_b_qnorm": gain(B_QK_DIM),
        "l0_b_knorm": gain(B_QK_DIM),
        "l0_b_lq1": nrm((B_QK_DIM,), 0.1),
        "l0_b_lk1": nrm((B_QK_DIM,), 0.1),
        "l0_b_lq2": nrm((B_QK_DIM,), 0.1),
        "l0_b_lk2": nrm((B_QK_DIM,), 0.1),
        "l0_b_subln": gain(B_V_DIM),
        "l0_m_wkv": nrm((D_MODEL, 2 * M_WIDTH), D_MODEL ** -0.5),
        "l0_m_qnorm": gain(HEAD_DIM),
        "l0_m_knorm": gain(HEAD_DIM),
        "l0_w_out": nrm((MIX_WIDTH, D_MODEL), MIX_WIDTH ** -0.5),
        "l1_norm": gain(D_MODEL),
        "l1_w_in": nrm((D_MODEL, ODD_IN), D_MODEL ** -0.5),
        "l1_c_conv_w": nrm((C_KERNEL, C_WIDTH), C_KERNEL ** -0.5),
        "l1_c_conv_b": nrm((C_WIDTH,), 0.02),
        "l1_c_norm": gain(C_WIDTH),
        "l1_d_qnorm": gain(HEAD_DIM),
        "l1_d_knorm": gain(HEAD_DIM),
        "l1_d_cmp_pos_k": nrm((CMP_BLOCK, HEAD_DIM), 0.02),
        "l1_d_cmp_w1_k": nrm((CMP_BLOCK * HEAD_DIM, CMP_HIDDEN), (CMP_BLOCK * HEAD_DIM) ** -0.5),
        "l1_d_cmp_w2_k": nrm((CMP_HIDDEN, HEAD_DIM), CMP_HIDDEN ** -0.5),
        "l1_d_cmp_pos_v": nrm((CMP_BLOCK, HEAD_DIM), 0.02),
        "l1_d_cmp_w1_v": nrm((CMP_BLOCK * HEAD_DIM, CMP_HIDDEN), (CMP_BLOCK * HEAD_DIM) ** -0.5),
        "l1_d_cmp_w2_v": nrm((CMP_HIDDEN, HEAD_DIM), CMP_HIDDEN ** -0.5),
        "l1_m_wkv": nrm((D_MODEL, 2 * M_WIDTH), D_MODEL ** -0.5),
        "l1_m_qnorm": gain(HEAD_DIM),
        "l1_m_knorm": gain(HEAD_DIM),
        "l1_w_out": nrm((MIX_WIDTH, D_MODEL), MIX_WIDTH ** -0.5),
    }


def reference(x, mem, mem_norm,
              l0_norm, l0_w_in, l0_a_vnorm, l0_a_ws, l0_a_bs, l0_b_qnorm, l0_b_knorm,
              l0_b_lq1, l0_b_lk1, l0_b_lq2, l0_b_lk2, l0_b_subln, l0_m_wkv, l0_m_qnorm,
              l0_m_knorm, l0_w_out,
              l1_norm, l1_w_in, l1_c_conv_w, l1_c_conv_b, l1_c_norm, l1_d_qnorm, l1_d_knorm,
              l1_d_cmp_pos_k, l1_d_cmp_w1_k, l1_d_cmp_w2_k, l1_d_cmp_pos_v, l1_d_cmp_w1_v,
              l1_d_cmp_w2_v, l1_m_wkv, l1_m_qnorm, l1_m_knorm, l1_w_out):
    pos = jnp.arange(x.shape[1], dtype=jnp.int32)
    mem_n = rms_norm(mem, mem_norm)
    even_params = (l0_norm, l0_w_in, l0_a_vnorm, l0_a_ws, l0_a_bs, l0_b_qnorm, l0_b_knorm,
                   l0_b_lq1, l0_b_lk1, l0_b_lq2, l0_b_lk2, l0_b_subln, l0_m_wkv, l0_m_qnorm,
                   l0_m_knorm, l0_w_out)
    odd_params = (l1_norm, l1_w_in, l1_c_conv_w, l1_c_conv_b, l1_c_norm, l1_d_qnorm, l1_d_knorm,
                  l1_d_cmp_pos_k, l1_d_cmp_w1_k, l1_d_cmp_w2_k, l1_d_cmp_pos_v, l1_d_cmp_w1_v,
                  l1_d_cmp_w2_v, l1_m_wkv, l1_m_qnorm, l1_m_knorm, l1_w_out)
    for layer in range(DEPTH):
        if layer % 2 == 0:
            x = even_layer(x, mem_n, pos, layer + 1, *even_params)
        else:
            x = odd_layer(x, mem_n, pos, *odd_params)
    return x
```

```python
import contextlib
import numpy as np
import ml_dtypes
import concourse.bass as bass
import concourse.mybir as mybir
from concourse.bass_utils import run_bass_kernel_spmd

F32 = mybir.dt.float32
BF16 = mybir.dt.bfloat16
AF = mybir.ActivationFunctionType
ALU = mybir.AluOpType
AX = mybir.AxisListType

S = 4096
D = 1024
NTB = S // 128
EPS = 1e-6
THETA = 500000.0
VW = 72


class Buf:
    def __init__(self, t, name):
        self.t = t
        self.name = name
        self.w = {}
        self.r = {}

    def __getitem__(self, k):
        return self.t[k]


class MK:
    EPOCH = 60000
    NSLOT = 16

    def __init__(self, nc):
        self.nc = nc
        self.es = contextlib.ExitStack()
        self.scopes = []
        self.eng = {"pe": nc.tensor, "act": nc.scalar, "dve": nc.vector, "pool": nc.gpsimd, "sp": nc.sync}
        self.cnt = {e: 0 for e in self.eng}
        self.dcnt = {"sp": 0, "pool": 0, "act": 0}
        self.sems = {}
        self.known = {e: {} for e in self.eng}
        self.latest = {}
        self.uid = 0

    def sem(self, key):
        if key not in self.sems:
            self.sems[key] = self.es.enter_context(self.nc.semaphore("s_%s_%d" % key))
        return self.sems[key]

    def _stack(self):
        return self.scopes[-1] if self.scopes else self.es

    def sb(self, name, shape, dtype):
        self.uid += 1
        t = self._stack().enter_context(self.nc.sbuf_tensor("%s_%d" % (name, self.uid), list(shape), dtype))
        return Buf(t, name)

    def ps(self, name, shape, dtype):
        self.uid += 1
        t = self._stack().enter_context(self.nc.psum_tensor("%s_%d" % (name, self.uid), list(shape), dtype))
        return Buf(t, name)

    @contextlib.contextmanager
    def scope(self):
        st = contextlib.ExitStack()
        self.scopes.append(st)
        try:
            yield
        finally:
            self.barrier()
            self.scopes.pop()
            st.close()

    def _wait(self, e, deps):
        for key, val in deps.items():
            if key[0] == "pe" and e == "pe":
                continue
            if self.known[e].get(key, 0) >= val:
                continue
            self.eng[e].wait_ge(self.sem(key), val)
            self.known[e][key] = val

    @staticmethod
    def _deps(rd, wr):
        d = {}
        for b in rd:
            for k, v in b.w.items():
                if d.get(k, 0) < v:
                    d[k] = v
        for b in wr:
            for k, v in b.w.items():
                if d.get(k, 0) < v:
                    d[k] = v
            for k, v in b.r.items():
                if d.get(k, 0) < v:
                    d[k] = v
        return d

    def _mark(self, rd, wr, key, val):
        self.latest[key] = val
        for b in wr:
            b.w = {key: val}
            b.r = {}
        for b in rd:
            if b in wr:
                continue
            if b.r.get(key, 0) < val:
                b.r[key] = val

    def op(self, e, fn, rd=(), wr=()):
        self._wait(e, self._deps(rd, wr))
        n = self.cnt[e]
        key = (e, n // self.EPOCH)
        val = n % self.EPOCH + 1
        ins = fn(self.eng[e])
        ins.then_inc(self.sem(key), 1)
        self.cnt[e] = n + 1
        self._mark(rd, wr, key, val)

    def dma(self, q, out, in_, rd=(), wr=(), **kw):
        self._wait(q, self._deps(rd, wr))
        n = self.dcnt[q]
        slot = n % self.NSLOT
        key = ("d" + q, slot)
        val = 16 * (n // self.NSLOT + 1)
        if n >= self.NSLOT:
            self._wait(q, {key: val - 16})
        self.eng[q].dma_start(out=out, in_=in_, **kw).then_inc(self.sem(key), 16)
        self.dcnt[q] = n + 1
        self._mark(rd, wr, key, val)

    def barrier(self):
        lt = dict(self.latest)
        for e in self.eng:
            self._wait(e, lt)

    def mm(self, out, lhsT, rhs, start, stop, rd, wr):
        self.op("pe", lambda e: e.matmul(out, lhsT=lhsT, rhs=rhs, start=start, stop=stop, skip_group_check=True), rd, wr)

    def tr(self, out, in_, ident, rd, wr):
        self.op("pe", lambda e: e.transpose(out, in_, ident), rd, wr)

    def act(self, out, in_, func, rd, wr, **kw):
        self.op("act", lambda e: e.activation(out=out, in_=in_, func=func, **kw), rd, wr)

    def tt(self, out, in0, in1, op, rd, wr, e="dve"):
        self.op(e, lambda g: g.tensor_tensor(out=out, in0=in0, in1=in1, op=op), rd, wr)

    def ts(self, out, in0, s1, s2, op0, op1, rd, wr, e="dve"):
        if op1 is None:
            self.op(e, lambda g: g.tensor_scalar(out=out, in0=in0, scalar1=s1, scalar2=None, op0=op0), rd, wr)
        else:
            self.op(e, lambda g: g.tensor_scalar(out=out, in0=in0, scalar1=s1, scalar2=s2, op0=op0, op1=op1), rd, wr)

    def stt(self, out, in0, scalar, in1, op0, op1, rd, wr, e="dve"):
        self.op(e, lambda g: g.scalar_tensor_tensor(out=out, in0=in0, scalar=scalar, in1=in1, op0=op0, op1=op1), rd, wr)

    def red(self, out, in_, rd, wr, op=ALU.add):
        self.op("dve", lambda g: g.tensor_reduce(out=out, in_=in_, axis=AX.X, op=op), rd, wr)

    def recip(self, out, in_, rd, wr):
        self.op("dve", lambda g: g.reciprocal(out=out, in_=in_), rd, wr)

    def cp(self, out, in_, rd, wr, e="dve"):
        if e == "act":
            self.op(e, lambda g: g.activation(out=out, in_=in_, func=AF.Copy), rd, wr)
        else:
            self.op(e, lambda g: g.tensor_copy(out=out, in_=in_), rd, wr)

    def memset(self, ap, val, wr, e="dve"):
        self.op(e, lambda g: g.memset(ap, val), (), wr)


class Pipe:
    def __init__(self, depth):
        self.depth = depth
        self.q = []

    def push(self, A, B):
        A()
        self.q.append(B)
        if len(self.q) > self.depth:
            self.q.pop(0)()

    def flush(self):
        while self.q:
            self.q.pop(0)()


class DS:
    def __init__(self, nc, name, shape, dtype, nblk=NTB):
        self.ap = nc.dram_tensor(name, list(shape), dtype, kind="Internal").ap()
        self.blk = [Buf(None, "%s_%d" % (name, i)) for i in range(nblk)]

    def bl(self, lo=0, hi=None):
        return self.blk[lo:(len(self.blk) if hi is None else hi)]


def bc(ap, shape):
    return ap.unsqueeze(2).to_broadcast(list(shape))


def bcm(ap, shape):
    return ap.unsqueeze(1).to_broadcast(list(shape))


def build_program(nlayers=2, stop=None):
    nc = bass.Bass("TRN2", target_bir_lowering=False)
    mk = MK(nc)

    def din(name, shape, dtype=F32):
        return nc.dram_tensor(name, list(shape), dtype, kind="ExternalInput").ap()

    x_in = din("x", [S, D])
    mem_in = din("mem", [256, D])
    P = {}
    specs = {
        "mem_norm": [D], "l0_norm": [D], "l0_w_in": [D, 4096], "l0_a_vnorm": [512], "l0_a_ws": [4, 128, 128],
        "l0_a_bs": [4, 128], "l0_b_qnorm": [32], "l0_b_knorm": [32], "l0_b_lq1": [32], "l0_b_lk1": [32],
        "l0_b_lq2": [32], "l0_b_lk2": [32], "l0_b_subln": [64], "l0_m_wkv": [D, 512], "l0_m_qnorm": [64],
        "l0_m_knorm": [64], "l0_w_out": [1280, D],
        "l1_norm": [D], "l1_w_in": [D, 3864], "l1_c_conv_w": [31, 512], "l1_c_conv_b": [512], "l1_c_norm": [512],
        "l1_d_qnorm": [64], "l1_d_knorm": [64], "l1_d_cmp_pos_k": [32, 64], "l1_d_cmp_w1_k": [2048, 128],
        "l1_d_cmp_w2_k": [128, 64], "l1_d_cmp_pos_v": [32, 64], "l1_d_cmp_w1_v": [2048, 128],
        "l1_d_cmp_w2_v": [128, 64], "l1_m_wkv": [D, 512], "l1_m_qnorm": [64], "l1_m_knorm": [64],
        "l1_w_out": [1280, D],
    }
    for k, shp in specs.items():
        P[k] = din(k, shp)
    C = {}
    cspecs = {
        "c_identb": ([128, 128], BF16), "c_identf": ([128, 128], F32), "c_tri": ([128, 128], BF16),
        "c_ntri": ([128, 128], BF16), "c_cos0": ([S, 4], F32), "c_sin0": ([S, 4], F32),
        "c_cos1": ([S, 8], F32), "c_sin1": ([S, 8], F32), "c_cosc": ([256, 8], F32), "c_sinc": ([256, 8], F32),
        "c_cmask": ([256, S], BF16), "c_ovl": ([256, 64], BF16), "c_keep": ([S, 64], F32),
        "c_addm": ([S, 64], F32), "c_E": ([64, 32 * 128], BF16),
    }
    for k, (shp, dt) in cspecs.items():
        C[k] = din(k, shp, dt)
    out_ap = nc.dram_tensor("out", [S, D], F32, kind="ExternalOutput").ap()

    X1 = DS(nc, "X1", [S, D], F32)
    Y = [DS(nc, "Y0", [S, 1280], BF16), DS(nc, "Y1", [S, 1280], BF16)]
    QT0 = DS(nc, "QT0", [512, S], BF16)
    KT0 = DS(nc, "KT0", [512, S], BF16)
    V0e = DS(nc, "V0e", [S, 8 * VW], BF16)
    GB0 = DS(nc, "GB0", [S, 512], F32)
    QT1 = DS(nc, "QT1", [64, 8, S], BF16)
    CRT = DS(nc, "CRT", [64, 4, S], BF16)
    KST = DS(nc, "KST", [64, 2, S], BF16)
    KWT = DS(nc, "KWT", [64, 2, S], BF16)
    VSe = DS(nc, "VSe", [S, 2 * VW], BF16)
    VWe = DS(nc, "VWe", [S, 2 * VW], BF16)
    GT1 = DS(nc, "GT1", [S, 24], F32)
    GD1 = DS(nc, "GD1", [S, 512], F32)
    GC1 = DS(nc, "GC1", [S, 512], F32)

    identb = mk.sb("identb", [128, 128], BF16)
    identf = mk.sb("identf", [128, 128], F32)
    tri = mk.sb("tri", [128, 128], BF16)
    ntri = mk.sb("ntri", [128, 128], BF16)
    epsT = mk.sb("epsT", [128, 1], F32)
    mk.dma("sp", identb[:], C["c_identb"], (), [identb])
    mk.dma("sp", identf[:], C["c_identf"], (), [identf])
    mk.dma("sp", tri[:], C["c_tri"], (), [tri])
    mk.dma("sp", ntri[:], C["c_ntri"], (), [ntri])
    mk.memset(epsT[:], EPS, [epsT])

    def load_bcast(name, src, n):
        b = mk.sb(name, [128, n], F32)
        mk.dma("sp", b[:], src.partition_broadcast(128), (), [b])
        return b

    KmT = [mk.sb("KmT%d" % L, [128, 4, 256], BF16) for L in range(2)]
    Vm = [mk.sb("Vm%d" % L, [128, 2, 4 * VW], BF16) for L in range(2)]

    def rstd_inplace(ss_buf, ss_ap, inv_d, ln=False):
        if ln:
            mk.act(ss_ap, ss_ap, AF.Ln, [ss_buf, epsT], [ss_buf], bias=epsT[:, 0:1], scale=inv_d)
            mk.act(ss_ap, ss_ap, AF.Exp, [ss_buf], [ss_buf], scale=-0.5)
            return
        mk.act(ss_ap, ss_ap, AF.Sqrt, [ss_buf, epsT], [ss_buf], bias=epsT[:, 0:1], scale=inv_d)
        mk.recip(ss_ap, ss_ap, [ss_buf], [ss_buf])

    def headnorm(T, Tap, H, hd, G, SQ, SS, gfull=False, ln=True):
        n = H * hd
        T3 = Tap.rearrange("p (h d) -> p h d", d=hd)
        mk.tt(SQ[:, 0:n], Tap, Tap, ALU.mult, [T], [SQ])
        mk.red(SS[:, 0:H], SQ[:, 0:n].rearrange("p (h d) -> p h d", d=hd), [SQ], [SS])
        rstd_inplace(SS, SS[:, 0:H], 1.0 / hd, ln=ln)
        mk.tt(T3, T3, bc(SS[:, 0:H], [128, H, hd]), ALU.mult, [T, SS], [T])
        if gfull:
            mk.tt(Tap, Tap, G[:, 0:n], ALU.mult, [T, G], [T])
        else:
            mk.tt(T3, T3, bcm(G[:, 0:hd], [128, H, hd]), ALU.mult, [T, G], [T])

    def rope(T, Tap, H, hd, half, CS, csap_c, csap_s, R):
        T3 = Tap.rearrange("p (h d) -> p h d", d=hd)
        x1 = T3[:, :, 0:half]
        x2 = T3[:, :, half:2 * half]
        R4 = R[:, 0:H * 4 * half].rearrange("p (h f d) -> p h f d", f=4, d=half)
        cb = bcm(csap_c, [128, H, half])
        sb_ = bcm(csap_s, [128, H, half])
        mk.tt(R4[:, :, 0, :], x1, cb, ALU.mult, [T, CS], [R])
        mk.tt(R4[:, :, 1, :], x2, sb_, ALU.mult, [T, CS], [R])
        mk.tt(R4[:, :, 2, :], x1, sb_, ALU.mult, [T, CS], [R])
        mk.tt(R4[:, :, 3, :], x2, cb, ALU.mult, [T, CS], [R])
        mk.tt(x1, R4[:, :, 0, :], R4[:, :, 1, :], ALU.subtract, [R], [T])
        mk.tt(x2, R4[:, :, 2, :], R4[:, :, 3, :], ALU.add, [R], [T])

    with mk.scope():
        Gmem = load_bcast("Gmem", P["mem_norm"], D)
        memt = mk.sb("memt", [128, 2, D], F32)
        mk.dma("sp", memt[:], mem_in.rearrange("(c p) d -> p c d", p=128), (), [memt])
        sq = mk.sb("sq0", [128, D], F32)
        ss = mk.sb("ss0", [128, 8], F32)
        mnb = mk.sb("mnb", [128, D], BF16)
        mnT = mk.sb("mnT", [128, 8, 256], BF16)
        psT = mk.ps("psT0", [128, 1024], BF16)
        psKV = mk.ps("psKV", [128, 512], F32)
        psK2 = mk.ps("psK2", [128, 1024], BF16)
        for mc in range(2):
            mk.act(sq[:], memt[:, mc, :], AF.Square, [memt], [sq])
            mk.red(ss[:, 0:1], sq[:], [sq], [ss])
            rstd_inplace(ss, ss[:, 0:1], 1.0 / D)
            mk.stt(mnb[:], memt[:, mc, :], ss[:, 0:1], Gmem[:], ALU.mult, ALU.mult, [memt, ss, Gmem], [mnb])
            for k in range(8):
                mk.tr(psT[:, k * 128:(k + 1) * 128], mnb[:, k * 128:(k + 1) * 128], identb[:], [mnb, identb], [psT])
            mk.cp(mnT[:, :, mc * 128:(mc + 1) * 128], psT[:].rearrange("p (k t) -> p k t", t=128), [psT], [mnT])
        wst = mk.sb("wkvst", [128, 8, 512], F32)
        wb = mk.sb("wkvb", [128, 8, 512], BF16)
        kv = mk.sb("kvsb", [128, 512], F32)
        knb = mk.sb("knb", [128, 256], BF16)
        for L in range(nlayers):
            pre = "l%d_" % L
            Gk = load_bcast("Gmk%d" % L, P[pre + "m_knorm"], 64)
            mk.dma("sp", wst[:], P[pre + "m_wkv"].rearrange("(k p) c -> p k c", p=128), (), [wst])
            mk.cp(wb[:], wst[:], [wst], [wb])
            mk.memset(Vm[L][:], 1.0, [Vm[L]])
            mk.memset(KmT[L][:], 0.0, [KmT[L]])
            for mc in range(2):
                for k in range(8):
                    mk.mm(psKV[:], mnT[:, k, mc * 128:(mc + 1) * 128], wb[:, k, :], k == 0, k == 7, [mnT, wb], [psKV])
                mk.act(kv[:], psKV[:], AF.Copy, [psKV], [kv])
                headnorm(kv, kv[:, 0:256], 4, 64, Gk, sq, ss)
                mk.cp(knb[:], kv[:, 0:256], [kv], [knb])
                for h in range(4):
                    mk.tr(psK2[0:64, h * 128:(h + 1) * 128], knb[:, h * 64:(h + 1) * 64], identb[:], [knb, identb], [psK2])
                mk.cp(KmT[L][0:64, :, mc * 128:(mc + 1) * 128], psK2[0:64, 0:512].rearrange("p (r t) -> p r t", t=128), [psK2], [KmT[L]])
                mk.cp(Vm[L][:, mc, :].rearrange("p (h d) -> p h d", d=VW)[:, :, 0:64],
                      kv[:, 256:512].rearrange("p (h d) -> p h d", d=64), [kv], [Vm[L]])

    if stop == 'p0':
        mk.barrier()
        return nc, mk
    def load_weights_bf16(Wb, w_ap, ncols, g_ap, gname, segs=None, st=None, dve_only=False):
        gT = mk.sb(gname, [128, 8], F32)
        mk.dma("sp", gT[:], g_ap.rearrange("(k p) -> p k", p=128), (), [gT], allow_slow_non_contiguous=True)
        if segs is None:
            segs = [(0, 0, ncols)]
        pieces = []
        for (d0, s0, n) in segs:
            o = 0
            while o < n:
                m = min(2048, n - o)
                pieces.append((d0 + o, s0 + o, m))
                o += m
        def body(st_):
            it = 0
            for k in range(8):
                for (d0, s0, m) in pieces:
                    s_ = st_[it % 2]
                    it += 1
                    mk.dma("sp", s_[:, 0:m], w_ap[k * 128:(k + 1) * 128, s0:s0 + m], (), [s_])
                    if it % 2 or dve_only:
                        mk.ts(Wb[:, k, d0:d0 + m], s_[:, 0:m], gT[:, k:k + 1], None, ALU.mult, None, [s_, gT], [Wb])
                    else:
                        mk.act(Wb[:, k, d0:d0 + m], s_[:, 0:m], AF.Copy, [s_, gT], [Wb], scale=gT[:, k:k + 1])
        if st is not None:
            body(st)
        else:
            with mk.scope():
                body([mk.sb("wst%d" % i, [128, 2048], F32) for i in range(2)])

    def load_wo(Wo, L, st, dve_only=False):
        for k in range(10):
            s_ = st[k % 2]
            mk.dma("sp", s_[:, 0:D], P["l%d_w_out" % L][k * 128:(k + 1) * 128, :], (), [s_])
            mk.cp(Wo[:, k, :], s_[:, 0:D], [s_], [Wo], e=("act" if (k % 2 and not dve_only) else "dve"))

    def mem_attn(L, MQ, mqb, mqT, psQ, psS, psO, PTm, OM, RL, GM, GMb, Ym_ap, Ym_buf):
        mk.cp(mqb[:], MQ[:, 0:256], [MQ], [mqb])
        for h in range(4):
            mk.tr(psQ[0:64, h * 128:(h + 1) * 128], mqb[:, h * 64:(h + 1) * 64], identb[:], [mqb, identb], [psQ])
        mk.cp(mqT[0:64, :, :], psQ[0:64, 0:512].rearrange("p (r t) -> p r t", t=128), [psQ], [mqT])
        for pr in range(2):
            for hh in range(2):
                h = pr * 2 + hh
                for mc in range(2):
                    blk = hh * 2 + mc
                    mk.mm(psS[:, blk * 128:(blk + 1) * 128], KmT[L][:, h, mc * 128:(mc + 1) * 128],
                          mqT[:, h, :], True, True, [KmT[L], mqT], [psS])
            mk.act(PTm[:], psS[:], AF.Exp, [psS], [PTm], scale=0.125)
            for hh in range(2):
                h = pr * 2 + hh
                for mc in range(2):
                    blk = hh * 2 + mc
                    mk.mm(psO[:, h * VW:(h + 1) * VW], PTm[:, blk * 128:(blk + 1) * 128],
                          Vm[L][:, mc, h * VW:(h + 1) * VW], mc == 0, mc == 1, [PTm, Vm[L]], [psO])
        o3 = psO[:, 0:4 * VW].rearrange("p (h d) -> p h d", d=VW)
        mk.recip(RL[:, 0:4], o3[:, :, 64], [psO], [RL])
        mk.tt(OM[:, 0:256].rearrange("p (h d) -> p h d", d=64), o3[:, :, 0:64], bc(RL[:, 0:4], [128, 4, 64]),
              ALU.mult, [psO, RL], [OM])
        mk.tt(Ym_ap, OM[:, 0:256], GM, ALU.mult, [OM, GMb], [Ym_buf])

    def mem_attn_parts(L, mqb, mqT, psQ, psS2, psO, PTm2, OM, RL, GMb, Ym_ap, Ym_buf, tail=None):
        def c0():
            for h in range(4):
                mk.tr(psQ[0:64, h * 128:(h + 1) * 128], mqb[:, h * 64:(h + 1) * 64], identb[:], [mqb, identb], [psQ])
            mk.cp(mqT[0:64, :, :], psQ[0:64, 0:512].rearrange("p (r t) -> p r t", t=128), [psQ], [mqT], e="act")
            for pr in range(2):
                for hh in range(2):
                    h = pr * 2 + hh
                    for mc in range(2):
                        blk = hh * 2 + mc
                        mk.mm(psS2[pr][:, blk * 128:(blk + 1) * 128], KmT[L][:, h, mc * 128:(mc + 1) * 128],
                              mqT[:, h, :], True, True, [KmT[L], mqT], [psS2[pr]])
                mk.act(PTm2[pr][:], psS2[pr][:], AF.Exp, [psS2[pr]], [PTm2[pr]], scale=0.125)

        def c1():
            for pr in range(2):
                for hh in range(2):
                    h = pr * 2 + hh
                    for mc in range(2):
                        blk = hh * 2 + mc
                        mk.mm(psO[:, h * VW:(h + 1) * VW], PTm2[pr][:, blk * 128:(blk + 1) * 128],
                              Vm[L][:, mc, h * VW:(h + 1) * VW], mc == 0, mc == 1, [PTm2[pr], Vm[L]], [psO])

        def c2():
            o3 = psO[:, 0:4 * VW].rearrange("p (h d) -> p h d", d=VW)
            mk.recip(RL[:, 0:4], o3[:, :, 64], [psO], [RL])
            mk.tt(OM[:, 0:256].rearrange("p (h d) -> p h d", d=64), o3[:, :, 0:64], bc(RL[:, 0:4], [128, 4, 64]),
                  ALU.mult, [psO, RL], [OM])
            mk.tt(Ym_ap, OM[:, 0:256], GMb[:], ALU.mult, [OM, GMb], [Ym_buf])
            if tail is not None:
                tail()
        return [c0, c1, c2]

    def xnorm_a(xt, sq, ss, xn):
        mk.act(sq[:], xt[:], AF.Square, [xt], [sq])
        mk.red(ss[:, 0:1], sq[:], [sq], [ss])
        rstd_inplace(ss, ss[:, 0:1], 1.0 / D, ln=True)
        mk.act(xn[:], xt[:], AF.Copy, [xt, ss], [xn], scale=ss[:, 0:1])

    def xnorm_b(xn, psT, hT):
        for k in range(8):
            mk.tr(psT[:, k * 128:(k + 1) * 128], xn[:, k * 128:(k + 1) * 128], identb[:], [xn, identb], [psT])
        mk.cp(hT[:], psT[:].rearrange("p (k t) -> p k t", t=128), [psT], [hT])

    def xnorm_to_hT(src_ap, src_deps, xt, sq, ss, xn, psT, hT):
        xnorm_a(xt, sq, ss, xn)
        xnorm_b(xn, psT, hT)

    def out_proj(L, Xres_ap, Xres_blks, dst_ap, dst_blks, Wo_pre=None):
        with mk.scope():
            if Wo_pre is not None:
                Wo = Wo_pre
            else:
                Wo = mk.sb("Wo", [128, 10, D], BF16)
                with mk.scope():
                    load_wo(Wo, L, [mk.sb("wost%d" % i, [128, D], F32) for i in range(2)])
            yt = [mk.sb("yt%d" % i, [128, 1280], BF16) for i in range(2)]
            xr = [mk.sb("xr%d" % i, [128, D], F32) for i in range(2)]
            xo = [mk.sb("xo%d" % i, [128, D], F32) for i in range(2)]
            yT = mk.sb("yT", [128, 10, 128], BF16)
            psY = [mk.ps("psY%d" % i, [128, 1024], BF16) for i in range(2)]
            psP = [mk.ps("psP%d" % i, [128, 512], F32) for i in range(2)]

            def loads(tb):
                mk.dma("sp", yt[tb % 2][:], Y[L].ap[tb * 128:(tb + 1) * 128, :], [Y[L].blk[tb]], [yt[tb % 2]])
                mk.dma("sp", xr[tb % 2][:], Xres_ap[tb * 128:(tb + 1) * 128, :],
                       [Xres_blks[tb]] if Xres_blks else [], [xr[tb % 2]])

            loads(0)
            for tb in range(NTB):
                if tb + 1 < NTB:
                    loads(tb + 1)
                y_ = yt[tb % 2]
                for k in range(10):
                    pT = psY[k // 8]
                    kk = k % 8
                    mk.tr(pT[:, kk * 128:(kk + 1) * 128], y_[:, k * 128:(k + 1) * 128], identb[:], [y_, identb], [pT])
                mk.cp(yT[:, 0:8, :], psY[0][:].rearrange("p (k t) -> p k t", t=128), [psY[0]], [yT])
                mk.cp(yT[:, 8:10, :], psY[1][:, 0:256].rearrange("p (k t) -> p k t", t=128), [psY[1]], [yT], e="act")
                for cg in range(2):
                    pp = psP[cg]
                    for k in range(10):
                        mk.mm(pp[:], yT[:, k, :], Wo[:, k, cg * 512:(cg + 1) * 512], k == 0, k == 9, [yT, Wo], [pp])
                    mk.tt(xo[tb % 2][:, cg * 512:(cg + 1) * 512], pp[:], xr[tb % 2][:, cg * 512:(cg + 1) * 512],
                          ALU.add, [pp, xr[tb % 2]], [xo[tb % 2]])
                mk.dma("sp", dst_ap[tb * 128:(tb + 1) * 128, :], xo[tb % 2][:], [xo[tb % 2]],
                       [dst_blks[tb]] if dst_blks else [])

    lam_init0 = 0.8 - 0.6 * float(np.exp(-0.3 * 1))
    LAMN = mk.sb("LAMN", [128, 1], F32)
    with mk.scope():
        Wb = mk.sb("Wb0", [128, 8, 4096], BF16)
        load_weights_bf16(Wb, P["l0_w_in"], 4096, P["l0_norm"], "gT0")
        Gav = load_bcast("Gav", P["l0_a_vnorm"], 512)
        Gq = load_bcast("Gq0", P["l0_b_qnorm"], 32)
        Gk = load_bcast("Gk0", P["l0_b_knorm"], 32)
        Gmq = load_bcast("Gmq0", P["l0_m_qnorm"], 64)
        BsT = mk.sb("BsT", [128, 4], F32)
        mk.dma("sp", BsT[:], P["l0_a_bs"].rearrange("g t -> t g"), (), [BsT], allow_slow_non_contiguous=True)
        lq = [load_bcast("lq%d" % i, P["l0_b_" + n], 32) for i, n in enumerate(["lq1", "lk1", "lq2", "lk2"])]
        ltmp = mk.sb("ltmp", [128, 32], F32)
        lsum = mk.sb("lsum", [128, 2], F32)
        for i in range(2):
            mk.tt(ltmp[:], lq[2 * i][:], lq[2 * i + 1][:], ALU.mult, [lq[2 * i], lq[2 * i + 1]], [ltmp])
            mk.red(lsum[:, i:i + 1], ltmp[:], [ltmp], [lsum])
        mk.act(lsum[:], lsum[:], AF.Exp, [lsum], [lsum])
        mk.tt(LAMN[:], lsum[:, 1:2], lsum[:, 0:1], ALU.subtract, [lsum], [LAMN])
        mk.ts(LAMN[:], LAMN[:], -lam_init0, None, ALU.add, None, [LAMN], [LAMN])
        WsT = mk.sb("WsT", [128, 4, 128], BF16)
        with mk.scope():
            wsl = mk.sb("wsl", [128, 4, 128], F32)
            mk.dma("sp", wsl[:], P["l0_a_ws"].rearrange("g t s -> t g s"), (), [wsl])
            psW = mk.ps("psW", [128, 512], F32)
            for g in range(4):
                mk.tr(psW[:, g * 128:(g + 1) * 128], wsl[:, g, :], identf[:], [wsl, identf], [psW])
            mk.tt(WsT[:], psW[:].rearrange("p (g t) -> p g t", t=128), bcm(tri[:], [128, 4, 128]), ALU.mult,
                  [psW, tri], [WsT])

        if stop == 'p1a':
            mk.barrier()
            return nc, mk
        xt = [mk.sb("xt%d" % i, [128, D], F32) for i in range(3)]
        cs = [mk.sb("cs%d" % i, [128, 8], F32) for i in range(3)]
        sq = mk.sb("sq", [128, D], F32)
        ss = mk.sb("ss", [128, 16], F32)
        sqx = mk.sb("sqx", [128, D], F32)
        ssx = mk.sb("ssx", [128, 8], F32)
        xns = [mk.sb("xn%d" % i, [128, D], BF16) for i in range(2)]
        hTs = [mk.sb("hT%d" % i, [128, 8, 128], BF16) for i in range(2)]
        Us = [mk.sb("U%d" % i, [128, 512], F32) for i in range(2)]
        Vv = mk.sb("Vv", [128, 512], F32)
        Vbs = [mk.sb("Vb%d" % i, [128, 512], BF16) for i in range(2)]
        Gas = [mk.sb("Ga%d" % i, [128, 512], F32) for i in range(2)]
        TA = mk.sb("TA", [128, 512], F32)
        T = mk.sb("T", [128, 512], F32)
        Rr = mk.sb("Rr", [128, 512], F32)
        TbQs = [mk.sb("TbQ%d" % i, [128, 512], BF16) for i in range(2)]
        TbKs = [mk.sb("TbK%d" % i, [128, 512], BF16) for i in range(2)]
        QKs = [mk.sb("QKs%d" % i, [128, 4, 128], BF16) for i in range(2)]
        VS = [mk.sb("VS%d" % i, [128, 8 * VW], BF16) for i in range(2)]
        GBs = [mk.sb("GBs%d" % i, [128, 512], F32) for i in range(2)]
        Ys = [mk.sb("Ys%d" % i, [128, 768], BF16) for i in range(3)]
        MQ = mk.sb("MQ", [128, 512], F32)
        GMs = [mk.sb("GM%d" % i, [128, 256], F32) for i in range(2)]
        mqbs = [mk.sb("mqb%d" % i, [128, 256], BF16) for i in range(2)]
        mqT = mk.sb("mqT", [128, 4, 128], BF16)
        mk.memset(mqT[:], 0.0, [mqT])
        PTm2 = [mk.sb("PTm%d" % i, [128, 512], BF16) for i in range(2)]
        OM = mk.sb("OM", [128, 256], F32)
        RL = mk.sb("RL", [128, 8], F32)
        psT = mk.ps("psT", [128, 1024], BF16)
        psA = [mk.ps("psA%d" % i, [128, 512], F32) for i in range(2)]
        psZ = mk.ps("psZ", [128, 512], F32)
        psQ = mk.ps("psQ", [128, 1024], BF16)
        psS2 = [mk.ps("psS%d" % i, [128, 512], F32) for i in range(2)]
        psO = mk.ps("psO", [128, 512], F32)
        for i in range(2):
            mk.memset(VS[i][:], 1.0, [VS[i]])

        def loads0(tb):
            mk.dma("sp", xt[tb % 3][:], x_in[tb * 128:(tb + 1) * 128, :], (), [xt[tb % 3]])
            mk.dma("sp", cs[tb % 3][:, 0:4], C["c_cos0"][tb * 128:(tb + 1) * 128, :], (), [cs[tb % 3]])
            mk.dma("sp", cs[tb % 3][:, 4:8], C["c_sin0"][tb * 128:(tb + 1) * 128, :], (), [cs[tb % 3]])

        def make_B0(tb):
            par = tb % 2
            rows = slice(tb * 128, (tb + 1) * 128)
            ys = Ys[tb % 3]
            U, Vb, Ga, TbQ, TbK = Us[par], Vbs[par], Gas[par], TbQs[par], TbKs[par]

            def bz():
                for g in range(4):
                    mk.mm(psZ[:, g * 128:(g + 1) * 128], WsT[:, g, :], Vb[:, g * 128:(g + 1) * 128], True, True,
                          [WsT, Vb], [psZ])
                for g in range(4):
                    mk.stt(TA[:, g * 128:(g + 1) * 128], psZ[:, g * 128:(g + 1) * 128], BsT[:, g:g + 1],
                           U[:, g * 128:(g + 1) * 128], ALU.add, ALU.mult, [psZ, BsT, U], [TA])
                mk.tt(ys[:, 0:512], TA[:], Ga[:], ALU.mult, [TA, Ga], [ys])
                mk.dma("sp", Y[0].ap[rows, 0:512], ys[:, 0:512], [ys], [Y[0].blk[tb]])

            def bqk(Tb_, qs, dst):
                def f():
                    for j in range(4):
                        mk.tr(psQ[:, j * 128:(j + 1) * 128], Tb_[:, j * 128:(j + 1) * 128], identb[:], [Tb_, identb], [psQ])
                    mk.cp(qs[:], psQ[:, 0:512].rearrange("p (j t) -> p j t", t=128), [psQ], [qs], e="act")
                    mk.dma("sp", dst.ap.rearrange("(j p) t -> p j t", p=128)[:, :, rows], qs[:], [qs], [dst.blk[tb]])
                return f

            def tail():
                mk.dma("sp", Y[0].ap[rows, 1024:1280], ys[:, 512:768], [ys], [Y[0].blk[tb]])

            mparts = mem_attn_parts(0, mqbs[par], mqT, psQ, psS2, psO, PTm2, OM, RL, GMs[par], ys[:, 512:768], ys, tail)
            return [bz, bqk(TbQ, QKs[0], QT0), bqk(TbK, QKs[1], KT0)] + mparts

        import os as _os
        _ntb = int(_os.environ.get('P1_NTB', NTB))
        loads0(0)
        if _ntb > 1:
            loads0(1)
        xnorm_a(xt[0], sqx, ssx, xns[0])
        xnorm_b(xns[0], psT, hTs[0])
        pi = 0
        pending = []
        for tb in range(_ntb):
            if tb + 2 < _ntb:
                loads0(tb + 2)
            if tb + 1 < _ntb:
                xnorm_a(xt[(tb + 1) % 3], sqx, ssx, xns[(tb + 1) % 2])
            par = tb % 2
            cs_ = cs[tb % 3]
            hT = hTs[par]
            rows = slice(tb * 128, (tb + 1) * 128)
            for cg in range(8):
                pp = psA[pi % 2]
                pi += 1
                for k in range(8):
                    mk.mm(pp[:], hT[:, k, :], Wb[:, k, cg * 512:(cg + 1) * 512], k == 0, k == 7, [hT, Wb], [pp])
                if cg == 0:
                    mk.act(Us[par][:], pp[:], AF.Gelu_apprx_tanh, [pp], [Us[par]])
                elif cg == 1:
                    mk.act(Vv[:], pp[:], AF.Gelu_apprx_tanh, [pp], [Vv])
                    headnorm(Vv, Vv[:], 4, 128, Gav, sq, ss, gfull=True)
                    mk.cp(Vbs[par][:], Vv[:], [Vv], [Vbs[par]])
                elif cg == 2:
                    mk.act(Gas[par][:], pp[:], AF.Silu, [pp], [Gas[par]])
                elif cg in (3, 4):
                    mk.act(T[:], pp[:], AF.Copy, [pp], [T])
                    headnorm(T, T[:], 16, 32, Gq if cg == 3 else Gk, sq, ss)
                    rope(T, T[:], 16, 32, 4, cs_, cs_[:, 0:4], cs_[:, 4:8], Rr)
                    Tb_ = TbQs[par] if cg == 3 else TbKs[par]
                    mk.cp(Tb_[:], T[:], [T], [Tb_])
                elif cg == 5:
                    vs = VS[par]
                    mk.act(vs[:].rearrange("p (h d) -> p h d", d=VW)[:, :, 0:64],
                           pp[:].rearrange("p (h d) -> p h d", d=64), AF.Copy, [pp], [vs])
                    mk.dma("sp", V0e.ap[rows, :], vs[:], [vs], [V0e.blk[tb]])
                elif cg == 6:
                    gb = GBs[par]
                    mk.act(gb[:], pp[:], AF.Silu, [pp], [gb])
                    mk.dma("sp", GB0.ap[rows, :], gb[:], [gb], [GB0.blk[tb]])
                else:
                    mk.act(MQ[:, 0:256], pp[:, 0:256], AF.Copy, [pp], [MQ])
                    mk.act(GMs[par][:], pp[:, 256:512], AF.Silu, [pp], [GMs[par]])
                    headnorm(MQ, MQ[:, 0:256], 4, 64, Gmq, sq, ss)
                    mk.cp(mqbs[par][:], MQ[:, 0:256], [MQ], [mqbs[par]])
                if pending:
                    pending.pop(0)()
            while pending:
                pending.pop(0)()
            if tb + 1 < _ntb:
                xnorm_b(xns[(tb + 1) % 2], psT, hTs[(tb + 1) % 2])
            pending = make_B0(tb)
        while pending:
            pending.pop(0)()

    if stop == 'p1':
        mk.barrier()
        return nc, mk
    Wb1 = mk.sb("Wb1", [128, 8, 3864], BF16) if nlayers == 2 else None
    sc23 = mk.scope()
    sc23.__enter__()
    Wo0 = mk.sb("Wo0", [128, 10, D], BF16)
    with mk.scope():
        Gsub = load_bcast("Gsub", P["l0_b_subln"], 64)
        mk.ts(Gsub[:], Gsub[:], 1.0 - lam_init0, None, ALU.mult, None, [Gsub], [Gsub])
        Ve = mk.sb("Ve", [128, NTB, 8 * VW], BF16)
        Vsrc = V0e.ap.rearrange("(c p) f -> p c f", p=128)
        for c8 in range(8):
            mk.dma("sp", Ve[:, c8 * 4:(c8 + 1) * 4, :], Vsrc[:, c8 * 4:(c8 + 1) * 4, :], V0e.bl(c8 * 4, c8 * 4 + 4), [Ve])
        QTh = [mk.sb("QTh%d" % i, [128, S], BF16) for i in range(2)]
        KTh = [[mk.sb("KTh%d_%d" % (i, m), [128, S], BF16) for m in range(2)] for i in range(2)]
        for i in range(2):
            mk.memset(QTh[i][:], 0.0, [QTh[i]])
            for m in range(2):
                mk.memset(KTh[i][m][:], 0.0, [KTh[i][m]])
        PT = [mk.sb("PT%d" % i, [128, 1024], BF16) for i in range(4)]
        GBt = [mk.sb("GBt%d" % i, [128, 4, 64], F32) for i in range(2)]
        YB = [mk.sb("YB%d" % i, [128, 4, 64], BF16) for i in range(2)]
        R0 = mk.sb("R0", [128, 8], F32)
        Aa = mk.sb("Aa", [128, 256], F32)
        Bb = mk.sb("Bb", [128, 256], F32)
        sq = mk.sb("sqd", [128, 256], F32)
        ss = mk.sb("ssd", [128, 8], F32)
        psS2 = [mk.ps("psS2_%d" % i, [128, 1024], F32) for i in range(3)]
        psO2 = [mk.ps("psO2_%d" % m, [128, 512], F32) for m in range(2)]
        OT = [mk.sb("OT%d" % m, [VW, 512], F32) for m in range(2)]
        sc = 32 ** -0.5

        def loadh(h):
            mk.dma("sp", QTh[h % 2][0:64, :], QT0.ap[h * 64:(h + 1) * 64, :], QT0.bl(), [QTh[h % 2]])
            for m in range(2):
                mk.dma("sp", KTh[h % 2][m][m * 32:(m + 1) * 32, :], KT0.ap[h * 64 + m * 32:h * 64 + (m + 1) * 32, :], KT0.bl(),
                       [KTh[h % 2][m]])

        loadh(0)
        wst_pf = [mk.sb("wstpf%d" % i, [128, 2048], F32) for i in range(2)]
        load_wo(Wo0, 0, wst_pf, dve_only=True)
        if nlayers == 2:
            load_weights_bf16(Wb1, P["l1_w_in"], 3864, P["l1_norm"], "gT1",
                              segs=[(0, 0, 2432), (2432, 2560, 128), (2560, 2432, 128), (2688, 2688, 3864 - 2688)],
                              st=wst_pf, dve_only=True)
        it = 0
        ep = 0
        pipe = Pipe(2)

        cnt2 = {"it": 0}

        def nxt2():
            i = cnt2["it"]
            cnt2["it"] += 1
            return psS2[i % 3], PT[i % 4]

        def p2_A(pS, k_, q_, j, G, c0):
            for m in range(2):
                mk.mm(pS[:, m * 512 + c0:(m + 1) * 512], k_[m][:, j * 128:(j + 1) * 128],
                      q_[:, G * 512 + c0:(G + 1) * 512], True, True, [k_[m], q_], [pS])

        def p2_B(pS, pt, h, j, G, c0, epi):
            pS3 = pS[:].rearrange("p (m c) -> p m c", m=2)
            pt3 = pt[:].rearrange("p (m c) -> p m c", m=2)
            mk.act(pt3[:, :, c0:512], pS3[:, :, c0:512], AF.Exp, [pS], [pt], scale=sc)
            if j >= 4 * G:
                mk.tt(pt3[:, :, c0:c0 + 128], pt3[:, :, c0:c0 + 128], bcm(tri[:], [128, 2, 128]), ALU.mult, [pt, tri], [pt])
            for m in range(2):
                mk.mm(psO2[m][0:VW, c0:512], Ve[:, j, h * VW:(h + 1) * VW], pt[:, m * 512 + c0:(m + 1) * 512],
                      j == 0, j == 4 * G + 3, [pt, Ve], [psO2[m]])
            if epi is not None:
                epi(pS)

        def p2_epi(par, gbt, h, G, pE):
            for m in range(2):
                mk.act(OT[m][:], psO2[m][0:VW, :], AF.Copy, [psO2[m]], [OT[m]])
                for i in range(4):
                    mk.tr(pE[:, m * 512 + i * VW:m * 512 + (i + 1) * VW], OT[m][:, i * 128:(i + 1) * 128], identf[0:VW, 0:VW],
                          [OT[m], identf], [pE])
            o0 = pE[:, 0:4 * VW].rearrange("p (i d) -> p i d", d=VW)
            o1 = pE[:, 512:512 + 4 * VW].rearrange("p (i d) -> p i d", d=VW)
            mk.recip(R0[:, 0:4], o0[:, :, 64], [pE], [R0])
            mk.recip(R0[:, 4:8], o1[:, :, 64], [pE], [R0])
            mk.ts(R0[:, 4:8], R0[:, 4:8], LAMN[:, 0:1], None, ALU.mult, None, [R0, LAMN], [R0])
            A3 = Aa[:].rearrange("p (i d) -> p i d", d=64)
            B3 = Bb[:].rearrange("p (i d) -> p i d", d=64)
            mk.tt(A3, o0[:, :, 0:64], bc(R0[:, 0:4], [128, 4, 64]), ALU.mult, [pE, R0], [Aa])
            mk.tt(B3, o1[:, :, 0:64], bc(R0[:, 4:8], [128, 4, 64]), ALU.mult, [pE, R0], [Bb])
            mk.tt(Aa[:], Aa[:], Bb[:], ALU.add, [Aa, Bb], [Aa])
            headnorm(Aa, Aa[:], 4, 64, Gsub, sq, ss, ln=True)
            yb = YB[par]
            mk.tt(yb[:], A3, gbt[:], ALU.mult, [Aa, gbt], [yb])
            mk.dma("sp", Y[0].ap[G * 512:(G + 1) * 512, 512 + h * 64:512 + (h + 1) * 64].rearrange("(i p) d -> p i d", p=128),
                   yb[:], [yb], Y[0].bl(G * 4, G * 4 + 4))

        import functools as _ft
        for h in range(8):
            if h + 1 < 8:
                loadh(h + 1)
            q_ = QTh[h % 2]
            k_ = KTh[h % 2]
            for G in range(8):
                par = ep % 2
                ep += 1
                gbt = GBt[par]
                mk.dma("sp", gbt[:], GB0.ap[G * 512:(G + 1) * 512, h * 64:(h + 1) * 64].rearrange("(i p) d -> p i d", p=128),
                       GB0.bl(G * 4, G * 4 + 4), [gbt])
                nj = 4 * G + 4
                for j in range(nj):
                    imin = max(0, j - 4 * G)
                    c0 = imin * 128
                    pS, pt = nxt2()
                    epi = _ft.partial(p2_epi, par, gbt, h, G) if j == nj - 1 else None
                    pipe.push(_ft.partial(p2_A, pS, k_, q_, j, G, c0),
                              _ft.partial(p2_B, pS, pt, h, j, G, c0, epi))
        pipe.flush()

    if stop == 'p2':
        mk.barrier()
        return nc, mk
    if nlayers == 1:
        out_proj(0, x_in, None, out_ap, None, Wo_pre=Wo0)
        sc23.__exit__(None, None, None)
    else:
        out_proj(0, x_in, None, X1.ap, X1.blk, Wo_pre=Wo0)
        sc23.__exit__(None, None, None)
        build_layer1(mk, nc, P, C, locals())
    mk.barrier()
    return nc, mk


def build_layer1(mk, nc, P, C, env):
    g_ = env
    (identb, identf, tri, ntri, epsT, KmT, Vm, X1, Y, QT1, CRT, KST, KWT, VSe, VWe, GT1, GD1, out_ap) = [g_[k] for k in (
        "identb", "identf", "tri", "ntri", "epsT", "KmT", "Vm", "X1", "Y", "QT1", "CRT", "KST", "KWT", "VSe", "VWe",
        "GT1", "GD1", "out_ap")]
    GC1 = g_["GC1"]
    (headnorm, rope, load_bcast, load_weights_bf16, mem_attn, xnorm_to_hT, out_proj, rstd_inplace, mem_attn_parts,
     xnorm_a, xnorm_b) = [g_[k] for k in (
        "headnorm", "rope", "load_bcast", "load_weights_bf16", "mem_attn", "xnorm_to_hT", "out_proj", "rstd_inplace",
        "mem_attn_parts", "xnorm_a", "xnorm_b")]
    stop = g_["stop"]
    NC1 = 3864
    KcT = [mk.sb("KcT%d" % g, [128, 256], BF16) for g in range(2)]
    for g in range(2):
        mk.memset(KcT[g][:], 0.0, [KcT[g]])
    Vc = [mk.sb("Vc%d" % g, [128, 2, VW], BF16) for g in range(2)]
    Gdk = load_bcast("Gdk", P["l1_d_knorm"], 64)

    with mk.scope():
        Wb = g_["Wb1"]
        Gcn = load_bcast("Gcn", P["l1_c_norm"], 512)
        Gdq = load_bcast("Gdq", P["l1_d_qnorm"], 64)
        Gmq = load_bcast("Gmq1", P["l1_m_qnorm"], 64)
        cwT = mk.sb("cwT", [128, 4, 32], F32)
        with mk.scope():
            psX = mk.ps("psX", [128, 512], F32)
            cw = mk.sb("cw", [32, 512], F32)
            mk.dma("sp", cw[0:31, :], P["l1_c_conv_w"], (), [cw])
            for ch in range(4):
                mk.tr(psX[:, ch * 32:ch * 32 + 31], cw[0:31, ch * 128:(ch + 1) * 128], identf[0:31, 0:31], [cw, identf], [psX])
            mk.cp(cwT[:, :, 0:31], psX[:, 0:128].rearrange("p (c j) -> p c j", j=32)[:, :, 0:31], [psX], [cwT])
        Dg = mk.sb("Dg", [128, 4, 31, 128], BF16)
        for ch in range(4):
            for j in range(31):
                mk.ts(Dg[:, ch, j, :], identf[:], cwT[:, ch, j:j + 1], None, ALU.mult, None, [identf, cwT], [Dg],
                      e="dve")
        cbT = mk.sb("cbT", [128, 4], F32)
        mk.dma("sp", cbT[:], P["l1_c_conv_b"].rearrange("(c p) -> p c", p=128), (), [cbT], allow_slow_non_contiguous=True)
        HTr = [mk.sb("HTr%d" % i, [128, 4, 30 + 512], BF16) for i in range(2)]
        mk.memset(HTr[0][:, :, 0:30], 0.0, [HTr[0]])

        xt = [mk.sb("xt%d" % i, [128, D], F32) for i in range(3)]
        cs = [mk.sb("cs%d" % i, [128, 16], F32) for i in range(3)]
        sq = mk.sb("sq", [128, 512], F32)
        ss = mk.sb("ss", [128, 16], F32)
        sqx = mk.sb("sqx", [128, D], F32)
        ssx = mk.sb("ssx", [128, 8], F32)
        xns = [mk.sb("xn%d" % i, [128, D], BF16) for i in range(2)]
        hTs = [mk.sb("hT%d" % i, [128, 8, 128], BF16) for i in range(2)]
        SG = mk.sb("SG", [128, 512], F32)
        Hbs = [mk.sb("Hb%d" % i, [128, 512], BF16) for i in range(2)]
        YC4 = mk.sb("YC4", [128, 4, 512], F32)
        Gcl = [mk.sb("Gcl%d" % i, [128, 512], F32) for i in range(2)]
        ycs = [mk.sb("ycs%d" % i, [128, 512], BF16) for i in range(2)]
        YN = mk.sb("YN", [128, 512], F32)
        Gcs = [mk.sb("Gc%d" % i, [128, 512], F32) for i in range(2)]
        T = mk.sb("T", [128, 512], F32)
        Rr = mk.sb("Rr", [128, 512], F32)
        TbQs = [mk.sb("TbQ%d" % i, [128, 512], BF16) for i in range(2)]
        TbCs = [mk.sb("TbC%d" % i, [128, 512], BF16) for i in range(2)]
        QTs = [mk.sb("QTs%d" % i, [64, 8, 128], BF16) for i in range(2)]
        CRs = [mk.sb("CRs%d" % i, [64, 4, 128], BF16) for i in range(2)]
        KSs = [mk.sb("KSs%d" % i, [64, 2, 128], BF16) for i in range(2)]
        KWs = [mk.sb("KWs%d" % i, [64, 2, 128], BF16) for i in range(2)]
        VSs = [mk.sb("VSs%d" % i, [128, 2 * VW], BF16) for i in range(2)]
        VWs = [mk.sb("VWs%d" % i, [128, 2 * VW], BF16) for i in range(2)]
        GTs = [mk.sb("GTs%d" % i, [128, 24], F32) for i in range(2)]
        GDs = [mk.sb("GDs%d" % i, [128, 512], F32) for i in range(2)]
        Ys = [mk.sb("Ys%d" % i, [128, 768], BF16) for i in range(2)]
        MQ = mk.sb("MQ", [128, 512], F32)
        GMs = [mk.sb("GM%d" % i, [128, 256], F32) for i in range(2)]
        mqbs = [mk.sb("mqb%d" % i, [128, 256], BF16) for i in range(2)]
        mqT = mk.sb("mqT", [128, 4, 128], BF16)
        mk.memset(mqT[:], 0.0, [mqT])
        PTm2 = [mk.sb("PTm%d" % i, [128, 512], BF16) for i in range(2)]
        OM = mk.sb("OM", [128, 256], F32)
        RL = mk.sb("RL", [128, 8], F32)
        psT = mk.ps("psT", [128, 1024], BF16)
        psA = [mk.ps("psA%d" % i, [128, 512], F32) for i in range(2)]
        psQ = mk.ps("psQ", [128, 1024], BF16)
        psS = mk.ps("psS", [128, 512], F32)
        psO = mk.ps("psO", [128, 512], F32)
        psC = mk.ps("psC", [128, 512], F32)
        psCt = mk.ps("psCt", [128, 512], F32)
        for i in range(2):
            mk.memset(VSs[i][:], 1.0, [VSs[i]])
            mk.memset(VWs[i][:], 1.0, [VWs[i]])

        def loads1(tb):
            mk.dma("sp", xt[tb % 3][:], X1.ap[tb * 128:(tb + 1) * 128, :], [X1.blk[tb]], [xt[tb % 3]])
            mk.dma("sp", cs[tb % 3][:, 0:8], C["c_cos1"][tb * 128:(tb + 1) * 128, :], (), [cs[tb % 3]])
            mk.dma("sp", cs[tb % 3][:, 8:16], C["c_sin1"][tb * 128:(tb + 1) * 128, :], (), [cs[tb % 3]])

        def tr_heads(src, ncols_heads, dstbuf, col0=0):
            for h in range(ncols_heads):
                mk.tr(psQ[0:64, h * 128:(h + 1) * 128], src[:, col0 + h * 64:col0 + (h + 1) * 64], identb[:], [src, identb], [psQ])
            mk.cp(dstbuf[:], psQ[0:64, 0:ncols_heads * 128].rearrange("p (r t) -> p r t", t=128), [psQ], [dstbuf], e="act")

        def make_B1(tb):
            par = tb % 2
            rows = slice(tb * 128, (tb + 1) * 128)
            ys = Ys[tb % 2]
            TbQ, TbC, Gc = TbQs[par], TbCs[par], Gcs[par]

            sb_i = tb // 4
            sub = tb % 4
            HTc = HTr[sb_i % 2]

            def b_conv():
                Hb = Hbs[par]
                if sub == 0 and sb_i >= 1:
                    mk.cp(HTc[:, :, 0:30], HTr[(sb_i - 1) % 2][:, :, 512:542], [HTr[(sb_i - 1) % 2]], [HTc])
                for ch in range(4):
                    mk.tr(psQ[:, ch * 128:(ch + 1) * 128], Hb[:, ch * 128:(ch + 1) * 128], identb[:], [Hb, identb], [psQ])
                mk.cp(HTc[:, :, 30 + sub * 128:30 + (sub + 1) * 128], psQ[:, 0:512].rearrange("p (c t) -> p c t", t=128),
                      [psQ], [HTc], e="act")

            def conv_ch(ch):
                def f():
                    for j in range(31):
                        mk.mm(psC[:, 0:512], Dg[:, ch, j, :], HTc[:, ch, j:j + 512], j == 0, j == 30, [Dg, HTc], [psC])
                    mk.ts(YC4[:, ch, :], psC[:, 0:512], cbT[:, ch:ch + 1], None, ALU.add, None, [psC, cbT], [YC4])
                return f

            def post_i(i):
                def f():
                    tbi = sb_i * 4 + i
                    rws = slice(tbi * 128, (tbi + 1) * 128)
                    gl = Gcl[i % 2]
                    yo = ycs[i % 2]
                    mk.dma("sp", gl[:], GC1.ap[rws, :], [GC1.blk[tbi]], [gl])
                    for ch in range(4):
                        mk.tr(psCt[:, ch * 128:(ch + 1) * 128], YC4[:, ch, i * 128:(i + 1) * 128], identf[:], [YC4, identf], [psCt])
                    mk.act(YN[:], psCt[:], AF.Copy, [psCt], [YN])
                    headnorm(YN, YN[:], 1, 512, Gcn, sq2, ss2, gfull=True)
                    mk.act(YN[:], YN[:], AF.Silu, [YN], [YN])
                    mk.tt(yo[:], YN[:], gl[:], ALU.mult, [YN, gl], [yo])
                    mk.dma("sp", Y[1].ap[rws, 0:512], yo[:], [yo], [Y[1].blk[tbi]])
                return f

            def b_q():
                qs = QTs[par]
                tr_heads(TbQ, 8, qs)
                mk.dma("sp", QT1.ap[:, :, rows], qs[:], [qs], [QT1.blk[tb]])

            def b_kv():
                cr = CRs[par]
                tr_heads(TbC, 4, cr)
                mk.dma("sp", CRT.ap[:, :, rows], cr[:], [cr], [CRT.blk[tb]])
                ks = KSs[par]
                tr_heads(TbC, 2, ks, col0=256)
                mk.dma("sp", KST.ap[:, :, rows], ks[:], [ks], [KST.blk[tb]])
                kw = KWs[par]
                tr_heads(TbC, 2, kw, col0=384)
                mk.dma("sp", KWT.ap[:, :, rows], kw[:], [kw], [KWT.blk[tb]])

            def tail():
                mk.dma("sp", Y[1].ap[rows, 1024:1280], ys[:, 512:768], [ys], [Y[1].blk[tb]])

            mparts = mem_attn_parts(1, mqbs[par], mqT, psQ, [psS, psS], psO, PTm2, OM, RL, GMs[par], ys[:, 512:768], ys, tail)
            slow_new = ([conv_ch(c) for c in range(4)] + [post_i(i) for i in range(4)]) if sub == 3 else []
            return [b_conv, b_q, b_kv] + mparts, slow_new

        sq2 = sq
        ss2 = mk.sb("ss2", [128, 8], F32)
        loads1(0)
        loads1(1)
        xnorm_a(xt[0], sqx, ssx, xns[0])
        xnorm_b(xns[0], psT, hTs[0])
        pending = []
        slow = []

        def pop():
            if pending:
                pending.pop(0)()
            elif slow:
                slow.pop(0)()

        for tb in range(NTB):
            if tb + 2 < NTB:
                loads1(tb + 2)
            if tb + 1 < NTB:
                xnorm_a(xt[(tb + 1) % 3], sqx, ssx, xns[(tb + 1) % 2])
            par = tb % 2
            cs_ = cs[tb % 3]
            hT = hTs[par]
            rows = slice(tb * 128, (tb + 1) * 128)

            def proj(pp, c0, n):
                for k in range(8):
                    mk.mm(pp[:, 0:n], hT[:, k, :], Wb[:, k, c0:c0 + n], k == 0, k == 7, [hT, Wb], [pp])

            pa = psA[0]
            pb = psA[1]
            proj(pa, 0, 512)
            proj(pb, 512, 512)
            mk.act(SG[:], pb[:], AF.Sigmoid, [pb], [SG])
            mk.tt(Hbs[par][:], pa[:], SG[:], ALU.mult, [pa, SG], [Hbs[par]])
            pop()
            pp = psA[0]
            proj(pp, 1024, 512)
            mk.act(Gcs[par][:], pp[:], AF.Silu, [pp], [Gcs[par]])
            mk.dma("sp", GC1.ap[rows, :], Gcs[par][:], [Gcs[par]], [GC1.blk[tb]])
            pop()
            pop()
            pp = psA[1]
            proj(pp, 1536, 512)
            mk.act(T[:], pp[:], AF.Copy, [pp], [T])
            headnorm(T, T[:], 8, 64, Gdq, sq, ss)
            rope(T, T[:], 8, 64, 8, cs_, cs_[:, 0:8], cs_[:, 8:16], Rr)
            mk.cp(TbQs[par][:], T[:], [T], [TbQs[par]])
            pop()
            pp = psA[0]
            proj(pp, 2048, 512)
            mk.act(T[:], pp[:], AF.Copy, [pp], [T])
            TbC = TbCs[par]
            mk.cp(TbC[:, 0:256], T[:, 0:256], [T], [TbC])
            headnorm(T, T[:, 256:512], 4, 64, Gdk, sq, ss)
            rope(T, T[:, 256:512], 4, 64, 8, cs_, cs_[:, 0:8], cs_[:, 8:16], Rr)
            mk.cp(TbC[:, 256:512], T[:, 256:512], [T], [TbC])
            pop()
            pp = psA[1]
            proj(pp, 2560, 280)
            vs = VSs[par]
            mk.act(vs[:].rearrange("p (g d) -> p g d", d=VW)[:, :, 0:64], pp[:, 0:128].rearrange("p (g d) -> p g d", d=64),
                   AF.Copy, [pp], [vs])
            mk.dma("sp", VSe.ap[rows, :], vs[:], [vs], [VSe.blk[tb]])
            vw = VWs[par]
            mk.act(vw[:].rearrange("p (g d) -> p g d", d=VW)[:, :, 0:64], pp[:, 128:256].rearrange("p (g d) -> p g d", d=64),
                   AF.Copy, [pp], [vw])
            mk.dma("sp", VWe.ap[rows, :], vw[:], [vw], [VWe.blk[tb]])
            gt = GTs[par]
            mk.act(gt[:], pp[:, 256:280], AF.Sigmoid, [pp], [gt])
            mk.dma("sp", GT1.ap[rows, :], gt[:], [gt], [GT1.blk[tb]])
            pop()
            pp = psA[0]
            proj(pp, 2840, 512)
            gd = GDs[par]
            mk.act(gd[:], pp[:], AF.Silu, [pp], [gd])
            mk.dma("sp", GD1.ap[rows, :], gd[:], [gd], [GD1.blk[tb]])
            pop()
            pop()
            pp = psA[1]
            proj(pp, 3352, 512)
            mk.act(MQ[:, 0:256], pp[:, 0:256], AF.Copy, [pp], [MQ])
            mk.act(GMs[par][:], pp[:, 256:512], AF.Silu, [pp], [GMs[par]])
            headnorm(MQ, MQ[:, 0:256], 4, 64, Gmq, sq, ss)
            mk.cp(mqbs[par][:], MQ[:, 0:256], [MQ], [mqbs[par]])
            pop()
            while pending:
                pending.pop(0)()
            if tb + 1 < NTB:
                xnorm_b(xns[(tb + 1) % 2], psT, hTs[(tb + 1) % 2])
            fast_new, slow_new = make_B1(tb)
            if slow_new:
                while slow:
                    slow.pop(0)()
                slow.extend(slow_new)
            pending.extend(fast_new)
        while pending:
            pending.pop(0)()
        while slow:
            slow.pop(0)()
    if stop == 'p4':
        return

    with mk.scope():
        Gd1 = Gdk
        cc = mk.sb("cc", [128, 2, 16], F32)
        mk.dma("sp", cc[:, :, 0:8], C["c_cosc"].rearrange("(c p) d -> p c d", p=128), (), [cc])
        mk.dma("sp", cc[:, :, 8:16], C["c_sinc"].rearrange("(c p) d -> p c d", p=128), (), [cc])
        crt = [mk.sb("crt%d" % i, [64, S], BF16) for i in range(2)]
        w1s = mk.sb("w1s", [64, 32, 128], F32)
        w1b = mk.sb("w1b", [128, 32, 128], BF16)
        mk.memset(w1b[:], 0.0, [w1b])
        w2s = mk.sb("w2s", [128, 64], F32)
        w2b = mk.sb("w2b", [128, 64], BF16)
        pl = mk.sb("pl", [32, 64], F32)
        posT = mk.sb("posT", [64, 32], F32)
        A = mk.sb("A", [128, 32, 256], BF16)
        hs = mk.sb("hs", [128, 256], BF16)
        T = mk.sb("Tc", [128, 64], F32)
        Tb = mk.sb("Tcb", [128, 64], BF16)
        Rr = mk.sb("Rrc", [128, 64], F32)
        sq = mk.sb("sqc", [128, 64], F32)
        ss = mk.sb("ssc", [128, 8], F32)
        psH = mk.ps("psH", [128, 512], F32)
        psC2 = mk.ps("psC2", [128, 512], F32)
        psP = mk.ps("psP", [128, 512], F32)
        psK = mk.ps("psK", [128, 1024], BF16)
        mk.memset(A[:], 0.0, [A])
        for g in range(2):
            mk.memset(Vc[g][:], 1.0, [Vc[g]])
        it = 0
        for kv in range(2):
            nm = "k" if kv == 0 else "v"
            mk.dma("sp", w1s[:], P["l1_d_cmp_w1_" + nm].rearrange("(l d) h -> d l h", d=64), (), [w1s])
            mk.cp(w1b[0:64, :, :], w1s[:], [w1s], [w1b], e="act")
            mk.dma("sp", w2s[:], P["l1_d_cmp_w2_" + nm], (), [w2s])
            mk.cp(w2b[:], w2s[:], [w2s], [w2b])
            mk.dma("sp", pl[:], P["l1_d_cmp_pos_" + nm], (), [pl])
            mk.tr(psP[0:64, 0:32], pl[:], identf[0:32, 0:32], [pl, identf], [psP])
            mk.cp(posT[:], psP[0:64, 0:32], [psP], [posT])
            for g in range(2):
                c_ = crt[it % 2]
                it += 1
                mk.dma("sp", c_[:], CRT.ap[:, kv * 2 + g, :], CRT.bl(), [c_])
                for l in range(32):
                    mk.ts(A[0:64, l, 0:255], c_[:, l:l + 16 * 254 + 1:16], posT[:, l:l + 1], None, ALU.add, None,
                          [c_, posT], [A])
                for l in range(32):
                    mk.mm(psH[:, 0:256], w1b[:, l, :], A[:, l, :], l == 0, l == 31, [w1b, A], [psH])
                mk.act(hs[:], psH[:, 0:256], AF.Silu, [psH], [hs])
                for c in range(2):
                    mk.mm(psC2[:, c * 64:(c + 1) * 64], hs[:, c * 128:(c + 1) * 128], w2b[:], True, True, [hs, w2b], [psC2])
                for c in range(2):
                    if kv == 0:
                        mk.act(T[:], psC2[:, c * 64:(c + 1) * 64], AF.Copy, [psC2], [T])
                        headnorm(T, T[:], 1, 64, Gd1, sq, ss)
                        rope(T, T[:], 1, 64, 8, cc, cc[:, c, 0:8], cc[:, c, 8:16], Rr)
                        mk.cp(Tb[:], T[:], [T], [Tb])
                        mk.tr(psK[0:64, 0:128], Tb[:], identb[:], [Tb, identb], [psK])
                        mk.cp(KcT[g][0:64, c * 128:(c + 1) * 128], psK[0:64, 0:128], [psK], [KcT[g]])
                    else:
                        mk.cp(Vc[g][:, c, 0:64], psC2[:, c * 64:(c + 1) * 64], [psC2], [Vc[g]])
    if stop == 'p5':
        return

    with mk.scope():
        import functools as _ft
        cmask = mk.sb("cmask", [128, 2, S], BF16)
        mk.dma("sp", cmask[:], C["c_cmask"].rearrange("(c p) q -> p c q", p=128), (), [cmask])
        OVL = mk.sb("OVL", [128, 2, 64], BF16)
        mk.dma("sp", OVL[:], C["c_ovl"].rearrange("(c p) j -> p c j", p=128), (), [OVL])
        QTg = mk.sb("QTg", [128, 4, S], BF16)
        KsT = mk.sb("KsT", [128, S], BF16)
        KwT = mk.sb("KwT", [128, S], BF16)
        mk.memset(QTg[:], 0.0, [QTg])
        mk.dma("sp", KsT[64:128, :], C["c_E"], (), [KsT])
        mk.memset(KwT[:], 0.0, [KwT])
        Vs = mk.sb("Vs", [128, NTB, VW], BF16)
        Vw = mk.sb("Vw", [128, NTB, VW], BF16)
        PT = [mk.sb("PT%d" % i, [128, 512], BF16) for i in range(4)]
        GTt = [mk.sb("GTt%d" % i, [128, 4, 24], F32) for i in range(2)]
        GDt = [mk.sb("GDt%d" % i, [128, 4, 256], F32) for i in range(2)]
        KA = [mk.sb("KA%d" % i, [128, 4, 128], F32) for i in range(2)]
        OC = mk.sb("OC", [128, 4, 4, 64], F32)
        IMP = mk.sb("IMP", [128, 4, 64], F32)
        IF = mk.sb("IF", [128, 64], F32)
        IW = mk.sb("IW", [128, 64], F32)
        M8 = mk.sb("M8", [128, 16], F32)
        NEG = mk.sb("NEG", [128, 128], BF16)
        mk.memset(NEG[:], 0.0, [NEG])
        RL = mk.sb("RLn", [128, 16], F32)
        YD = mk.sb("YD", [128, 4, 64], F32)
        YT = mk.sb("YT", [128, 4, 64], F32)
        YDb = [mk.sb("YDb%d" % i, [128, 4, 64], BF16) for i in range(2)]
        OTa = mk.sb("OTa", [VW, 512], F32)
        OTb = mk.sb("OTb", [VW, 512], F32)
        tiny = mk.sb("tiny", [128, 1], F32)
        mk.memset(tiny[:], 1e-30, [tiny])
        psS = [mk.ps("psSn%d" % i, [128, 512], F32) for i in range(3)]
        psOa = mk.ps("psOa", [128, 512], F32)
        psOb = mk.ps("psOb", [128, 512], F32)
        psN = mk.ps("psN", [128, 1024], BF16)
        psEa = mk.ps("psEa", [128, 512], F32)
        psEb = mk.ps("psEb", [128, 512], F32)
        cnt = {"it": 0, "ep": 0}
        pipe = Pipe(2)

        def nxt():
            i = cnt["it"]
            cnt["it"] += 1
            return psS[i % 3], PT[i % 4]

        def tr_back(src_ps, OTx, psE, nrow, wout):
            mk.act(OTx[0:nrow, :], src_ps[0:nrow, :], AF.Copy, [src_ps], [OTx])
            for i in range(4):
                mk.tr(psE[:, i * wout:i * wout + nrow], OTx[0:nrow, i * 128:(i + 1) * 128], identf[0:nrow, 0:nrow],
                      [OTx, identf], [psE])

        def cmp_A(pS, g, c, r, qrows):
            mk.mm(pS[:, 0:512], KcT[g][:, c * 128:(c + 1) * 128], QTg[:, r, qrows], True, True, [KcT[g], QTg], [pS])

        def cmp_B(pS, pt, g, c, qrows, first, last, epi):
            mk.act(pt[:], pS[:], AF.Exp, [pS], [pt], scale=0.125)
            mk.tt(pt[:], pt[:], cmask[:, c, qrows], ALU.mult, [pt, cmask], [pt])
            mk.mm(psOa[0:VW, :], Vc[g][:, c, :], pt[:], first, last, [pt, Vc[g]], [psOa])
            mk.mm(psOb[0:64, :], OVL[:, c, :], pt[:], first, last, [pt, OVL], [psOb])
            if epi is not None:
                epi()

        def cmp_epi(r, h, gtt):
            tr_back(psOa, OTa, psEa, VW, VW)
            tr_back(psOb, OTb, psEb, 64, 64)
            oc3 = psEa[:, 0:4 * VW].rearrange("p (i d) -> p i d", d=VW)
            mk.ts(RL[:, 0:4], oc3[:, :, 64], tiny[:, 0:1], None, ALU.max, None, [psEa, tiny], [RL])
            mk.recip(RL[:, 0:4], RL[:, 0:4], [RL], [RL])
            mk.tt(RL[:, 4:8], RL[:, 0:4], gtt[:, :, h * 3], ALU.mult, [RL, gtt], [RL])
            mk.tt(OC[:, r, :, :], oc3[:, :, 0:64], bc(RL[:, 4:8], [128, 4, 64]), ALU.mult, [psEa, RL], [OC])
            im3 = psEb[:, 0:256].rearrange("p (i d) -> p i d", d=64)
            if r == 0:
                mk.tt(IMP[:], im3, bc(RL[:, 0:4], [128, 4, 64]), ALU.mult, [psEb, RL], [IMP])
            else:
                mk.tt(YT[:], im3, bc(RL[:, 0:4], [128, 4, 64]), ALU.mult, [psEb, RL], [YT])
                mk.tt(IMP[:], IMP[:], YT[:], ALU.add, [IMP, YT], [IMP])

        def sel_A(pS, j, r, G, c0):
            mk.mm(pS[:, c0:512], KsT[:, j * 128:(j + 1) * 128], QTg[:, r, G * 512 + c0:(G + 1) * 512], True, True,
                  [KsT, QTg], [pS])

        def sel_B(pS, pt, j, G, c0):
            mk.act(pt[:, c0:512], pS[:, c0:512], AF.Exp, [pS], [pt], scale=0.125)
            if j >= 4 * G:
                mk.tt(pt[:, c0:c0 + 128], pt[:, c0:c0 + 128], tri[:], ALU.mult, [pt, tri], [pt])
            mk.mm(psOa[0:VW, c0:512], Vs[:, j, :], pt[:, c0:512], j == 0, j == 4 * G + 3, [pt, Vs], [psOa])

        def win_A(pS, j, r, G, c0, c1):
            mk.mm(pS[:, c0:c1], KwT[:, j * 128:(j + 1) * 128], QTg[:, r, G * 512 + c0:G * 512 + c1], True, True,
                  [KwT, QTg], [pS])

        def win_B(pS, pt, j, G, c0, c1, rel, first, last, epi):
            mk.act(pt[:, c0:c1], pS[:, c0:c1], AF.Exp, [pS], [pt], scale=0.125)
            if rel >= 0:
                mk.tt(pt[:, rel * 128:(rel + 1) * 128], pt[:, rel * 128:(rel + 1) * 128], tri[:], ALU.mult,
                      [pt, tri], [pt])
            if rel + 4 <= 3:
                a = rel + 4
                mk.tt(pt[:, a * 128:(a + 1) * 128], pt[:, a * 128:(a + 1) * 128], ntri[:], ALU.mult,
                      [pt, ntri], [pt])
            mk.mm(psOb[0:VW, c0:c1], Vw[:, j, :], pt[:, c0:c1], first, last, [pt, Vw], [psOb])
            if epi is not None:
                epi()

        def fin_epi(r, h, g, G, gtt, gdt, yb, qrows):
            tr_back(psOa, OTa, psEa, VW, VW)
            tr_back(psOb, OTb, psEb, VW, VW)
            os3 = psEa[:, 0:4 * VW].rearrange("p (i d) -> p i d", d=VW)
            ow3 = psEb[:, 0:4 * VW].rearrange("p (i d) -> p i d", d=VW)
            mk.recip(RL[:, 8:12], os3[:, :, 64], [psEa], [RL])
            mk.recip(RL[:, 12:16], ow3[:, :, 64], [psEb], [RL])
            mk.tt(RL[:, 8:12], RL[:, 8:12], gtt[:, :, h * 3 + 1], ALU.mult, [RL, gtt], [RL])
            mk.tt(RL[:, 12:16], RL[:, 12:16], gtt[:, :, h * 3 + 2], ALU.mult, [RL, gtt], [RL])
            mk.tt(YD[:], os3[:, :, 0:64], bc(RL[:, 8:12], [128, 4, 64]), ALU.mult, [psEa, RL], [YD])
            mk.tt(YT[:], ow3[:, :, 0:64], bc(RL[:, 12:16], [128, 4, 64]), ALU.mult, [psEb, RL], [YT])
            mk.tt(YD[:], YD[:], YT[:], ALU.add, [YD, YT], [YD])
            mk.tt(YD[:], YD[:], OC[:, r, :, :], ALU.add, [YD, OC], [YD])
            mk.tt(yb[:], YD[:], gdt[:, :, r * 64:(r + 1) * 64], ALU.mult, [YD, gdt], [yb])
            mk.dma("sp", Y[1].ap[qrows, 512 + h * 64:512 + (h + 1) * 64].rearrange("(i p) d -> p i d", p=128),
                   yb[:], [yb], Y[1].bl(G * 4, G * 4 + 4))

        for g in range(2):
            mk.dma("sp", QTg[0:64, :, :], QT1.ap[:, g * 4:(g + 1) * 4, :], QT1.bl(), [QTg])
            mk.dma("sp", KsT[0:64, :], KST.ap[:, g, :], KST.bl(), [KsT])
            mk.dma("sp", KwT[0:64, :], KWT.ap[:, g, :], KWT.bl(), [KwT])
            mk.dma("sp", Vs[:], VSe.ap[:, g * VW:(g + 1) * VW].rearrange("(c p) f -> p c f", p=128), VSe.bl(), [Vs])
            mk.dma("sp", Vw[:], VWe.ap[:, g * VW:(g + 1) * VW].rearrange("(c p) f -> p c f", p=128), VWe.bl(), [Vw])
            for G in range(8):
                par = cnt["ep"] % 2
                cnt["ep"] += 1
                gtt = GTt[par]
                gdt = GDt[par]
                ka = KA[par]
                qrows = slice(G * 512, (G + 1) * 512)
                mk.dma("sp", gtt[:], GT1.ap[qrows, :].rearrange("(i p) d -> p i d", p=128), GT1.bl(G * 4, G * 4 + 4), [gtt])
                mk.dma("sp", gdt[:], GD1.ap[qrows, g * 256:(g + 1) * 256].rearrange("(i p) d -> p i d", p=128),
                       GD1.bl(G * 4, G * 4 + 4), [gdt])
                mk.dma("sp", ka[:, :, 0:64], C["c_keep"][qrows, :].rearrange("(i p) d -> p i d", p=128), (), [ka])
                mk.dma("sp", ka[:, :, 64:128], C["c_addm"][qrows, :].rearrange("(i p) d -> p i d", p=128), (), [ka])
                chunks = [0] if G < 4 else [0, 1]
                for r in range(4):
                    h = g * 4 + r
                    for ci, c in enumerate(chunks):
                        pS, pt = nxt()
                        last = ci == len(chunks) - 1
                        epi = _ft.partial(cmp_epi, r, h, gtt) if last else None
                        pipe.push(_ft.partial(cmp_A, pS, g, c, r, qrows),
                                  _ft.partial(cmp_B, pS, pt, g, c, qrows, ci == 0, last, epi))
                pipe.flush()
                for i in range(4):
                    mk.tt(IF[:], IMP[:, i, :], ka[:, i, 0:64], ALU.mult, [IMP, ka], [IF])
                    mk.tt(IF[:], IF[:], ka[:, i, 64:128], ALU.add, [IF, ka], [IF])
                    mk.op("dve", lambda e: e.max(out=M8[:, 0:8], in_=IF[:]), [IF], [M8])
                    mk.op("dve", lambda e: e.match_replace(out=IW[:], in_to_replace=M8[:, 0:8], in_values=IF[:], imm_value=-9.0),
                          [M8, IF], [IW])
                    mk.op("dve", lambda e: e.max(out=M8[:, 8:16], in_=IW[:]), [IW], [M8])
                    mk.ts(IW[:], IF[:], M8[:, 15:16], None, ALU.is_ge, None, [IF, M8], [IW])
                    mk.ts(NEG[:, 64:128], IW[:], -1.0, 30000.0, ALU.add, ALU.mult, [IW], [NEG])
                    mk.tr(psN[:, i * 128:(i + 1) * 128], NEG[:], identb[:], [NEG, identb], [psN])
                for r in range(4):
                    mk.cp(QTg[64:128, r, qrows], psN[64:128, 0:512], [psN], [QTg], e=("act" if r % 2 else "dve"))
                for r in range(4):
                    h = g * 4 + r
                    for j in range(4 * G + 4):
                        imin = max(0, j - 4 * G)
                        c0 = imin * 128
                        pS, pt = nxt()
                        pipe.push(_ft.partial(sel_A, pS, j, r, G, c0), _ft.partial(sel_B, pS, pt, j, G, c0))
                    j0 = max(0, 4 * G - 4)
                    for j in range(j0, 4 * G + 4):
                        rel = j - 4 * G
                        ilo = max(0, rel)
                        ihi = min(3, rel + 4)
                        c0, c1 = ilo * 128, (ihi + 1) * 128
                        pS, pt = nxt()
                        last = j == 4 * G + 3
                        yb = YDb[r % 2]
                        epi = _ft.partial(fin_epi, r, h, g, G, gtt, gdt, yb, qrows) if last else None
                        pipe.push(_ft.partial(win_A, pS, j, r, G, c0, c1),
                                  _ft.partial(win_B, pS, pt, j, G, c0, c1, rel, j == j0, last, epi))
                pipe.flush()
    if stop == 'p6':
        return
    out_proj(1, X1.ap, X1.blk, out_ap, None)


def host_consts():
    c = {}
    c["c_identb"] = np.eye(128, dtype=np.float32).astype(ml_dtypes.bfloat16)
    c["c_identf"] = np.eye(128, dtype=np.float32)
    k = np.arange(128)[:, None]
    q = np.arange(128)[None, :]
    c["c_tri"] = (k <= q).astype(np.float32).astype(ml_dtypes.bfloat16)
    c["c_ntri"] = (k > q).astype(np.float32).astype(ml_dtypes.bfloat16)

    def cs(pos, half):
        inv = (np.float32(THETA) ** (-np.arange(half, dtype=np.float32) / np.float32(half))).astype(np.float32)
        ang = pos.astype(np.float32)[:, None] * inv[None, :]
        return np.cos(ang).astype(np.float32), np.sin(ang).astype(np.float32)

    pos = np.arange(S)
    c["c_cos0"], c["c_sin0"] = cs(pos, 4)
    c["c_cos1"], c["c_sin1"] = cs(pos, 8)
    n_cmp = (S - 32) // 16 + 1
    cend = np.arange(256) * 16 + 31
    c["c_cosc"], c["c_sinc"] = cs(cend, 8)
    cm = (cend[:, None] <= pos[None, :]).astype(np.float32)
    cm[n_cmp:, :] = 0.0
    c["c_cmask"] = cm.astype(ml_dtypes.bfloat16)
    start = np.arange(256) * 16
    s0 = np.arange(64) * 64
    lo = np.maximum(start[:, None], s0[None, :])
    hi = np.minimum(start[:, None] + 32, s0[None, :] + 64)
    ov = (np.clip(hi - lo, 0, None) / 32.0).astype(np.float32)
    ov[n_cmp:, :] = 0.0
    c["c_ovl"] = ov.astype(ml_dtypes.bfloat16)
    cur = (pos // 64)[:, None]
    j = np.arange(64)[None, :]
    forced = (j == 0) | (j == cur) | (j == cur - 1)
    future = j > cur
    keep = (~forced & ~future).astype(np.float32)
    addm = np.where(forced, 10.0 + j / 64.0, np.where(future, -1.0 - j / 64.0, 0.0)).astype(np.float32)
    c["c_keep"] = keep
    c["c_addm"] = addm
    E = np.zeros((64, 32, 128), np.float32)
    for ch in range(32):
        for kk in range(128):
            E[2 * ch + kk // 64, ch, kk] = 1.0
    c["c_E"] = E.reshape(64, 32 * 128).astype(ml_dtypes.bfloat16)
    return c


_CACHE = {}


def kernel(**inputs):
    nl = 2
    if "nc" not in _CACHE:
        _CACHE["nc"] = build_program(nl)[0]
    nc = _CACHE["nc"]
    consts = host_consts()
    in_maps = []
    for c in range(8):
        b = c % 4
        m = {"x": np.ascontiguousarray(inputs["x"][b]), "mem": np.ascontiguousarray(inputs["mem"][b])}
        for k, v in inputs.items():
            if k not in ("x", "mem"):
                m[k] = np.ascontiguousarray(v)
        m.update(consts)
        in_maps.append(m)
    res = run_bass_kernel_spmd(nc, in_maps, core_ids=list(range(8)))
    out = np.stack([np.asarray(res.results[b]["out"]) for b in range(4)], axis=0)
    return out.astype(np.float32)
```

```python
import contextlib
import numpy as np
import ml_dtypes
import concourse.bass as bass
import concourse.mybir as mybir
from concourse.bass_utils import run_bass_kernel_spmd

F32 = mybir.dt.float32
BF16 = mybir.dt.bfloat16
AF = mybir.ActivationFunctionType
ALU = mybir.AluOpType
AX = mybir.AxisListType

S = 4096
D = 1024
NTB = S // 128
EPS = 1e-6
THETA = 500000.0
VW = 72


class Buf:
    def __init__(self, t, name):
        self.t = t
        self.name = name
        self.w = {}
        self.r = {}

    def __getitem__(self, k):
        return self.t[k]


class MK:
    EPOCH = 60000
    NSLOT = 16

    def __init__(self, nc):
        self.nc = nc
        self.es = contextlib.ExitStack()
        self.scopes = []
        self.eng = {"pe": nc.tensor, "act": nc.scalar, "dve": nc.vector, "pool": nc.gpsimd, "sp": nc.sync}
        self.cnt = {e: 0 for e in self.eng}
        self.dcnt = {"sp": 0, "pool": 0, "act": 0}
        self.sems = {}
        self.known = {e: {} for e in self.eng}
        self.latest = {}
        self.uid = 0

    def sem(self, key):
        if key not in self.sems:
            self.sems[key] = self.es.enter_context(self.nc.semaphore("s_%s_%d" % key))
        return self.sems[key]

    def _stack(self):
        return self.scopes[-1] if self.scopes else self.es

    def sb(self, name, shape, dtype):
        self.uid += 1
        t = self._stack().enter_context(self.nc.sbuf_tensor("%s_%d" % (name, self.uid), list(shape), dtype))
        return Buf(t, name)

    def ps(self, name, shape, dtype):
        self.uid += 1
        t = self._stack().enter_context(self.nc.psum_tensor("%s_%d" % (name, self.uid), list(shape), dtype))
        return Buf(t, name)

    @contextlib.contextmanager
    def scope(self):
        st = contextlib.ExitStack()
        self.scopes.append(st)
        try:
            yield
        finally:
            self.barrier()
            self.scopes.pop()
            st.close()

    def _wait(self, e, deps):
        for key, val in deps.items():
            if key[0] == "pe" and e == "pe":
                continue
            if self.known[e].get(key, 0) >= val:
                continue
            self.eng[e].wait_ge(self.sem(key), val)
            self.known[e][key] = val

    @staticmethod
    def _deps(rd, wr):
        d = {}
        for b in rd:
            for k, v in b.w.items():
                if d.get(k, 0) < v:
                    d[k] = v
        for b in wr:
            for k, v in b.w.items():
                if d.get(k, 0) < v:
                    d[k] = v
            for k, v in b.r.items():
                if d.get(k, 0) < v:
                    d[k] = v
        return d

    def _mark(self, rd, wr, key, val):
        self.latest[key] = val
        for b in wr:
            b.w = {key: val}
            b.r = {}
        for b in rd:
            if b in wr:
                continue
            if b.r.get(key, 0) < val:
                b.r[key] = val

    def op(self, e, fn, rd=(), wr=()):
        self._wait(e, self._deps(rd, wr))
        n = self.cnt[e]
        key = (e, n // self.EPOCH)
        val = n % self.EPOCH + 1
        ins = fn(self.eng[e])
        ins.then_inc(self.sem(key), 1)
        self.cnt[e] = n + 1
        self._mark(rd, wr, key, val)

    def dma(self, q, out, in_, rd=(), wr=(), **kw):
        self._wait(q, self._deps(rd, wr))
        n = self.dcnt[q]
        slot = n % self.NSLOT
        key = ("d" + q, slot)
        val = 16 * (n // self.NSLOT + 1)
        if n >= self.NSLOT:
            self._wait(q, {key: val - 16})
        self.eng[q].dma_start(out=out, in_=in_, **kw).then_inc(self.sem(key), 16)
        self.dcnt[q] = n + 1
        self._mark(rd, wr, key, val)

    def barrier(self):
        lt = dict(self.latest)
        for e in self.eng:
            self._wait(e, lt)

    def mm(self, out, lhsT, rhs, start, stop, rd, wr):
        self.op("pe", lambda e: e.matmul(out, lhsT=lhsT, rhs=rhs, start=start, stop=stop, skip_group_check=True), rd, wr)

    def tr(self, out, in_, ident, rd, wr):
        self.op("pe", lambda e: e.transpose(out, in_, ident), rd, wr)

    def act(self, out, in_, func, rd, wr, **kw):
        self.op("act", lambda e: e.activation(out=out, in_=in_, func=func, **kw), rd, wr)

    def tt(self, out, in0, in1, op, rd, wr, e="dve"):
        self.op(e, lambda g: g.tensor_tensor(out=out, in0=in0, in1=in1, op=op), rd, wr)

    def ts(self, out, in0, s1, s2, op0, op1, rd, wr, e="dve"):
        if op1 is None:
            self.op(e, lambda g: g.tensor_scalar(out=out, in0=in0, scalar1=s1, scalar2=None, op0=op0), rd, wr)
        else:
            self.op(e, lambda g: g.tensor_scalar(out=out, in0=in0, scalar1=s1, scalar2=s2, op0=op0, op1=op1), rd, wr)

    def stt(self, out, in0, scalar, in1, op0, op1, rd, wr, e="dve"):
        self.op(e, lambda g: g.scalar_tensor_tensor(out=out, in0=in0, scalar=scalar, in1=in1, op0=op0, op1=op1), rd, wr)

    def red(self, out, in_, rd, wr, op=ALU.add):
        self.op("dve", lambda g: g.tensor_reduce(out=out, in_=in_, axis=AX.X, op=op), rd, wr)

    def recip(self, out, in_, rd, wr):
        self.op("dve", lambda g: g.reciprocal(out=out, in_=in_), rd, wr)

    def cp(self, out, in_, rd, wr, e="dve"):
        if e == "act":
            self.op(e, lambda g: g.activation(out=out, in_=in_, func=AF.Copy), rd, wr)
        else:
            self.op(e, lambda g: g.tensor_copy(out=out, in_=in_), rd, wr)

    def memset(self, ap, val, wr, e="dve"):
        self.op(e, lambda g: g.memset(ap, val), (), wr)


class Pipe:
    def __init__(self, depth):
        self.depth = depth
        self.q = []

    def push(self, A, B):
        A()
        self.q.append(B)
        if len(self.q) > self.depth:
            self.q.pop(0)()

    def flush(self):
        while self.q:
            self.q.pop(0)()


class DS:
    def __init__(self, nc, name, shape, dtype, nblk=NTB):
        self.ap = nc.dram_tensor(name, list(shape), dtype, kind="Internal").ap()
        self.blk = [Buf(None, "%s_%d" % (name, i)) for i in range(nblk)]

    def bl(self, lo=0, hi=None):
        return self.blk[lo:(len(self.blk) if hi is None else hi)]


def bc(ap, shape):
    return ap.unsqueeze(2).to_broadcast(list(shape))


def bcm(ap, shape):
    return ap.unsqueeze(1).to_broadcast(list(shape))


def build_program(nlayers=2, stop=None):
    nc = bass.Bass("TRN2", target_bir_lowering=False)
    mk = MK(nc)

    def din(name, shape, dtype=F32):
        return nc.dram_tensor(name, list(shape), dtype, kind="ExternalInput").ap()

    x_in = din("x", [S, D])
    mem_in = din("mem", [256, D])
    P = {}
    specs = {
        "mem_norm": [D], "l0_norm": [D], "l0_w_in": [D, 4096], "l0_a_vnorm": [512], "l0_a_ws": [4, 128, 128],
        "l0_a_bs": [4, 128], "l0_b_qnorm": [32], "l0_b_knorm": [32], "l0_b_lq1": [32], "l0_b_lk1": [32],
        "l0_b_lq2": [32], "l0_b_lk2": [32], "l0_b_subln": [64], "l0_m_wkv": [D, 512], "l0_m_qnorm": [64],
        "l0_m_knorm": [64], "l0_w_out": [1280, D],
        "l1_norm": [D], "l1_w_in": [D, 3864], "l1_c_conv_w": [31, 512], "l1_c_conv_b": [512], "l1_c_norm": [512],
        "l1_d_qnorm": [64], "l1_d_knorm": [64], "l1_d_cmp_pos_k": [32, 64], "l1_d_cmp_w1_k": [2048, 128],
        "l1_d_cmp_w2_k": [128, 64], "l1_d_cmp_pos_v": [32, 64], "l1_d_cmp_w1_v": [2048, 128],
        "l1_d_cmp_w2_v": [128, 64], "l1_m_wkv": [D, 512], "l1_m_qnorm": [64], "l1_m_knorm": [64],
        "l1_w_out": [1280, D],
    }
    for k, shp in specs.items():
        P[k] = din(k, shp)
    C = {}
    cspecs = {
        "c_identb": ([128, 128], BF16), "c_identf": ([128, 128], F32), "c_tri": ([128, 128], BF16),
        "c_ntri": ([128, 128], BF16), "c_cos0": ([S, 4], F32), "c_sin0": ([S, 4], F32),
        "c_cos1": ([S, 8], F32), "c_sin1": ([S, 8], F32), "c_cosc": ([256, 8], F32), "c_sinc": ([256, 8], F32),
        "c_cmask": ([256, S], BF16), "c_ovl": ([256, 64], BF16), "c_keep": ([S, 64], F32),
        "c_addm": ([S, 64], F32), "c_E": ([64, 32 * 128], BF16),
    }
    for k, (shp, dt) in cspecs.items():
        C[k] = din(k, shp, dt)
    out_ap = nc.dram_tensor("out", [S, D], F32, kind="ExternalOutput").ap()

    X1 = DS(nc, "X1", [S, D], F32)
    Y = [DS(nc, "Y0", [S, 1280], BF16), DS(nc, "Y1", [S, 1280], BF16)]
    QT0 = DS(nc, "QT0", [512, S], BF16)
    KT0 = DS(nc, "KT0", [512, S], BF16)
    V0e = DS(nc, "V0e", [S, 8 * VW], BF16)
    GB0 = DS(nc, "GB0", [S, 512], F32)
    QT1 = DS(nc, "QT1", [64, 8, S], BF16)
    CRT = DS(nc, "CRT", [64, 4, S], BF16)
    KST = DS(nc, "KST", [64, 2, S], BF16)
    KWT = DS(nc, "KWT", [64, 2, S], BF16)
    VSe = DS(nc, "VSe", [S, 2 * VW], BF16)
    VWe = DS(nc, "VWe", [S, 2 * VW], BF16)
    GT1 = DS(nc, "GT1", [S, 24], F32)
    GD1 = DS(nc, "GD1", [S, 512], F32)
    GC1 = DS(nc, "GC1", [S, 512], F32)

    identb = mk.sb("identb", [128, 128], BF16)
    identf = mk.sb("identf", [128, 128], F32)
    tri = mk.sb("tri", [128, 128], BF16)
    ntri = mk.sb("ntri", [128, 128], BF16)
    epsT = mk.sb("epsT", [128, 1], F32)
    mk.dma("sp", identb[:], C["c_identb"], (), [identb])
    mk.dma("sp", identf[:], C["c_identf"], (), [identf])
    mk.dma("sp", tri[:], C["c_tri"], (), [tri])
    mk.dma("sp", ntri[:], C["c_ntri"], (), [ntri])
    mk.memset(epsT[:], EPS, [epsT])

    def load_bcast(name, src, n):
        b = mk.sb(name, [128, n], F32)
        mk.dma("sp", b[:], src.partition_broadcast(128), (), [b])
        return b

    KmT = [mk.sb("KmT%d" % L, [128, 4, 256], BF16) for L in range(2)]
    Vm = [mk.sb("Vm%d" % L, [128, 2, 4 * VW], BF16) for L in range(2)]

    def rstd_inplace(ss_buf, ss_ap, inv_d, ln=False):
        if ln:
            mk.act(ss_ap, ss_ap, AF.Ln, [ss_buf, epsT], [ss_buf], bias=epsT[:, 0:1], scale=inv_d)
            mk.act(ss_ap, ss_ap, AF.Exp, [ss_buf], [ss_buf], scale=-0.5)
            return
        mk.act(ss_ap, ss_ap, AF.Sqrt, [ss_buf, epsT], [ss_buf], bias=epsT[:, 0:1], scale=inv_d)
        mk.recip(ss_ap, ss_ap, [ss_buf], [ss_buf])

    def headnorm(T, Tap, H, hd, G, SQ, SS, gfull=False, ln=True):
        n = H * hd
        T3 = Tap.rearrange("p (h d) -> p h d", d=hd)
        mk.tt(SQ[:, 0:n], Tap, Tap, ALU.mult, [T], [SQ])
        mk.red(SS[:, 0:H], SQ[:, 0:n].rearrange("p (h d) -> p h d", d=hd), [SQ], [SS])
        rstd_inplace(SS, SS[:, 0:H], 1.0 / hd, ln=ln)
        mk.tt(T3, T3, bc(SS[:, 0:H], [128, H, hd]), ALU.mult, [T, SS], [T])
        if gfull:
            mk.tt(Tap, Tap, G[:, 0:n], ALU.mult, [T, G], [T])
        else:
            mk.tt(T3, T3, bcm(G[:, 0:hd], [128, H, hd]), ALU.mult, [T, G], [T])

    def rope(T, Tap, H, hd, half, CS, csap_c, csap_s, R):
        T3 = Tap.rearrange("p (h d) -> p h d", d=hd)
        x1 = T3[:, :, 0:half]
        x2 = T3[:, :, half:2 * half]
        R4 = R[:, 0:H * 4 * half].rearrange("p (h f d) -> p h f d", f=4, d=half)
        cb = bcm(csap_c, [128, H, half])
        sb_ = bcm(csap_s, [128, H, half])
        mk.tt(R4[:, :, 0, :], x1, cb, ALU.mult, [T, CS], [R])
        mk.tt(R4[:, :, 1, :], x2, sb_, ALU.mult, [T, CS], [R])
        mk.tt(R4[:, :, 2, :], x1, sb_, ALU.mult, [T, CS], [R])
        mk.tt(R4[:, :, 3, :], x2, cb, ALU.mult, [T, CS], [R])
        mk.tt(x1, R4[:, :, 0, :], R4[:, :, 1, :], ALU.subtract, [R], [T])
        mk.tt(x2, R4[:, :, 2, :], R4[:, :, 3, :], ALU.add, [R], [T])

    with mk.scope():
        Gmem = load_bcast("Gmem", P["mem_norm"], D)
        memt = mk.sb("memt", [128, 2, D], F32)
        mk.dma("sp", memt[:], mem_in.rearrange("(c p) d -> p c d", p=128), (), [memt])
        sq = mk.sb("sq0", [128, D], F32)
        ss = mk.sb("ss0", [128, 8], F32)
        mnb = mk.sb("mnb", [128, D], BF16)
        mnT = mk.sb("mnT", [128, 8, 256], BF16)
        psT = mk.ps("psT0", [128, 1024], BF16)
        psKV = mk.ps("psKV", [128, 512], F32)
        psK2 = mk.ps("psK2", [128, 1024], BF16)
        for mc in range(2):
            mk.act(sq[:], memt[:, mc, :], AF.Square, [memt], [sq])
            mk.red(ss[:, 0:1], sq[:], [sq], [ss])
            rstd_inplace(ss, ss[:, 0:1], 1.0 / D)
            mk.stt(mnb[:], memt[:, mc, :], ss[:, 0:1], Gmem[:], ALU.mult, ALU.mult, [memt, ss, Gmem], [mnb])
            for k in range(8):
                mk.tr(psT[:, k * 128:(k + 1) * 128], mnb[:, k * 128:(k + 1) * 128], identb[:], [mnb, identb], [psT])
            mk.cp(mnT[:, :, mc * 128:(mc + 1) * 128], psT[:].rearrange("p (k t) -> p k t", t=128), [psT], [mnT])
        wst = mk.sb("wkvst", [128, 8, 512], F32)
        wb = mk.sb("wkvb", [128, 8, 512], BF16)
        kv = mk.sb("kvsb", [128, 512], F32)
        knb = mk.sb("knb", [128, 256], BF16)
        for L in range(nlayers):
            pre = "l%d_" % L
            Gk = load_bcast("Gmk%d" % L, P[pre + "m_knorm"], 64)
            mk.dma("sp", wst[:], P[pre + "m_wkv"].rearrange("(k p) c -> p k c", p=128), (), [wst])
            mk.cp(wb[:], wst[:], [wst], [wb])
            mk.memset(Vm[L][:], 1.0, [Vm[L]])
            mk.memset(KmT[L][:], 0.0, [KmT[L]])
            for mc in range(2):
                for k in range(8):
                    mk.mm(psKV[:], mnT[:, k, mc * 128:(mc + 1) * 128], wb[:, k, :], k == 0, k == 7, [mnT, wb], [psKV])
                mk.act(kv[:], psKV[:], AF.Copy, [psKV], [kv])
                headnorm(kv, kv[:, 0:256], 4, 64, Gk, sq, ss)
                mk.cp(knb[:], kv[:, 0:256], [kv], [knb])
                for h in range(4):
                    mk.tr(psK2[0:64, h * 128:(h + 1) * 128], knb[:, h * 64:(h + 1) * 64], identb[:], [knb, identb], [psK2])
                mk.cp(KmT[L][0:64, :, mc * 128:(mc + 1) * 128], psK2[0:64, 0:512].rearrange("p (r t) -> p r t", t=128), [psK2], [KmT[L]])
                mk.cp(Vm[L][:, mc, :].rearrange("p (h d) -> p h d", d=VW)[:, :, 0:64],
                      kv[:, 256:512].rearrange("p (h d) -> p h d", d=64), [kv], [Vm[L]])

    if stop == 'p0':
        mk.barrier()
        return nc, mk
    def load_weights_bf16(Wb, w_ap, ncols, g_ap, gname, segs=None, st=None, dve_only=False):
        gT = mk.sb(gname, [128, 8], F32)
        mk.dma("sp", gT[:], g_ap.rearrange("(k p) -> p k", p=128), (), [gT], allow_slow_non_contiguous=True)
        if segs is None:
            segs = [(0, 0, ncols)]
        pieces = []
        for (d0, s0, n) in segs:
            o = 0
            while o < n:
                m = min(2048, n - o)
                pieces.append((d0 + o, s0 + o, m))
                o += m
        def body(st_):
            it = 0
            for k in range(8):
                for (d0, s0, m) in pieces:
                    s_ = st_[it % 2]
                    it += 1
                    mk.dma("sp", s_[:, 0:m], w_ap[k * 128:(k + 1) * 128, s0:s0 + m], (), [s_])
                    if it % 2 or dve_only:
                        mk.ts(Wb[:, k, d0:d0 + m], s_[:, 0:m], gT[:, k:k + 1], None, ALU.mult, None, [s_, gT], [Wb])
                    else:
                        mk.act(Wb[:, k, d0:d0 + m], s_[:, 0:m], AF.Copy, [s_, gT], [Wb], scale=gT[:, k:k + 1])
        if st is not None:
            body(st)
        else:
            with mk.scope():
                body([mk.sb("wst%d" % i, [128, 2048], F32) for i in range(2)])

    def load_wo(Wo, L, st, dve_only=False):
        for k in range(10):
            s_ = st[k % 2]
            mk.dma("sp", s_[:, 0:D], P["l%d_w_out" % L][k * 128:(k + 1) * 128, :], (), [s_])
            mk.cp(Wo[:, k, :], s_[:, 0:D], [s_], [Wo], e=("act" if (k % 2 and not dve_only) else "dve"))

    def mem_attn(L, MQ, mqb, mqT, psQ, psS, psO, PTm, OM, RL, GM, GMb, Ym_ap, Ym_buf):
        mk.cp(mqb[:], MQ[:, 0:256], [MQ], [mqb])
        for h in range(4):
            mk.tr(psQ[0:64, h * 128:(h + 1) * 128], mqb[:, h * 64:(h + 1) * 64], identb[:], [mqb, identb], [psQ])
        mk.cp(mqT[0:64, :, :], psQ[0:64, 0:512].rearrange("p (r t) -> p r t", t=128), [psQ], [mqT])
        for pr in range(2):
            for hh in range(2):
                h = pr * 2 + hh
                for mc in range(2):
                    blk = hh * 2 + mc
                    mk.mm(psS[:, blk * 128:(blk + 1) * 128], KmT[L][:, h, mc * 128:(mc + 1) * 128],
                          mqT[:, h, :], True, True, [KmT[L], mqT], [psS])
            mk.act(PTm[:], psS[:], AF.Exp, [psS], [PTm], scale=0.125)
            for hh in range(2):
                h = pr * 2 + hh
                for mc in range(2):
                    blk = hh * 2 + mc
                    mk.mm(psO[:, h * VW:(h + 1) * VW], PTm[:, blk * 128:(blk + 1) * 128],
                          Vm[L][:, mc, h * VW:(h + 1) * VW], mc == 0, mc == 1, [PTm, Vm[L]], [psO])
        o3 = psO[:, 0:4 * VW].rearrange("p (h d) -> p h d", d=VW)
        mk.recip(RL[:, 0:4], o3[:, :, 64], [psO], [RL])
        mk.tt(OM[:, 0:256].rearrange("p (h d) -> p h d", d=64), o3[:, :, 0:64], bc(RL[:, 0:4], [128, 4, 64]),
              ALU.mult, [psO, RL], [OM])
        mk.tt(Ym_ap, OM[:, 0:256], GM, ALU.mult, [OM, GMb], [Ym_buf])

    def mem_attn_parts(L, mqb, mqT, psQ, psS2, psO, PTm2, OM, RL, GMb, Ym_ap, Ym_buf, tail=None):
        def c0():
            for h in range(4):
                mk.tr(psQ[0:64, h * 128:(h + 1) * 128], mqb[:, h * 64:(h + 1) * 64], identb[:], [mqb, identb], [psQ])
            mk.cp(mqT[0:64, :, :], psQ[0:64, 0:512].rearrange("p (r t) -> p r t", t=128), [psQ], [mqT], e="act")
            for pr in range(2):
                for hh in range(2):
                    h = pr * 2 + hh
                    for mc in range(2):
                        blk = hh * 2 + mc
                        mk.mm(psS2[pr][:, blk * 128:(blk + 1) * 128], KmT[L][:, h, mc * 128:(mc + 1) * 128],
                              mqT[:, h, :], True, True, [KmT[L], mqT], [psS2[pr]])
                mk.act(PTm2[pr][:], psS2[pr][:], AF.Exp, [psS2[pr]], [PTm2[pr]], scale=0.125)

        def c1():
            for pr in range(2):
                for hh in range(2):
                    h = pr * 2 + hh
                    for mc in range(2):
                        blk = hh * 2 + mc
                        mk.mm(psO[:, h * VW:(h + 1) * VW], PTm2[pr][:, blk * 128:(blk + 1) * 128],
                              Vm[L][:, mc, h * VW:(h + 1) * VW], mc == 0, mc == 1, [PTm2[pr], Vm[L]], [psO])

        def c2():
            o3 = psO[:, 0:4 * VW].rearrange("p (h d) -> p h d", d=VW)
            mk.recip(RL[:, 0:4], o3[:, :, 64], [psO], [RL])
            mk.tt(OM[:, 0:256].rearrange("p (h d) -> p h d", d=64), o3[:, :, 0:64], bc(RL[:, 0:4], [128, 4, 64]),
                  ALU.mult, [psO, RL], [OM])
            mk.tt(Ym_ap, OM[:, 0:256], GMb[:], ALU.mult, [OM, GMb], [Ym_buf])
            if tail is not None:
                tail()
        return [c0, c1, c2]

    def xnorm_a(xt, sq, ss, xn):
        mk.act(sq[:], xt[:], AF.Square, [xt], [sq])
        mk.red(ss[:, 0:1], sq[:], [sq], [ss])
        rstd_inplace(ss, ss[:, 0:1], 1.0 / D, ln=True)
        mk.act(xn[:], xt[:], AF.Copy, [xt, ss], [xn], scale=ss[:, 0:1])

    def xnorm_b(xn, psT, hT):
        for k in range(8):
            mk.tr(psT[:, k * 128:(k + 1) * 128], xn[:, k * 128:(k + 1) * 128], identb[:], [xn, identb], [psT])
        mk.cp(hT[:], psT[:].rearrange("p (k t) -> p k t", t=128), [psT], [hT])

    def xnorm_to_hT(src_ap, src_deps, xt, sq, ss, xn, psT, hT):
        xnorm_a(xt, sq, ss, xn)
        xnorm_b(xn, psT, hT)

    def out_proj(L, Xres_ap, Xres_blks, dst_ap, dst_blks, Wo_pre=None):
        with mk.scope():
            if Wo_pre is not None:
                Wo = Wo_pre
            else:
                Wo = mk.sb("Wo", [128, 10, D], BF16)
                with mk.scope():
                    load_wo(Wo, L, [mk.sb("wost%d" % i, [128, D], F32) for i in range(2)])
            yt = [mk.sb("yt%d" % i, [128, 1280], BF16) for i in range(2)]
            xr = [mk.sb("xr%d" % i, [128, D], F32) for i in range(2)]
            xo = [mk.sb("xo%d" % i, [128, D], F32) for i in range(2)]
            yT = mk.sb("yT", [128, 10, 128], BF16)
            psY = [mk.ps("psY%d" % i, [128, 1024], BF16) for i in range(2)]
            psP = [mk.ps("psP%d" % i, [128, 512], F32) for i in range(2)]

            def loads(tb):
                mk.dma("sp", yt[tb % 2][:], Y[L].ap[tb * 128:(tb + 1) * 128, :], [Y[L].blk[tb]], [yt[tb % 2]])
                mk.dma("sp", xr[tb % 2][:], Xres_ap[tb * 128:(tb + 1) * 128, :],
                       [Xres_blks[tb]] if Xres_blks else [], [xr[tb % 2]])

            loads(0)
            for tb in range(NTB):
                if tb + 1 < NTB:
                    loads(tb + 1)
                y_ = yt[tb % 2]
                for k in range(10):
                    pT = psY[k // 8]
                    kk = k % 8
                    mk.tr(pT[:, kk * 128:(kk + 1) * 128], y_[:, k * 128:(k + 1) * 128], identb[:], [y_, identb], [pT])
                mk.cp(yT[:, 0:8, :], psY[0][:].rearrange("p (k t) -> p k t", t=128), [psY[0]], [yT])
                mk.cp(yT[:, 8:10, :], psY[1][:, 0:256].rearrange("p (k t) -> p k t", t=128), [psY[1]], [yT], e="act")
                for cg in range(2):
                    pp = psP[cg]
                    for k in range(10):
                        mk.mm(pp[:], yT[:, k, :], Wo[:, k, cg * 512:(cg + 1) * 512], k == 0, k == 9, [yT, Wo], [pp])
                    mk.tt(xo[tb % 2][:, cg * 512:(cg + 1) * 512], pp[:], xr[tb % 2][:, cg * 512:(cg + 1) * 512],
                          ALU.add, [pp, xr[tb % 2]], [xo[tb % 2]])
                mk.dma("sp", dst_ap[tb * 128:(tb + 1) * 128, :], xo[tb % 2][:], [xo[tb % 2]],
                       [dst_blks[tb]] if dst_blks else [])

    lam_init0 = 0.8 - 0.6 * float(np.exp(-0.3 * 1))
    LAMN = mk.sb("LAMN", [128, 1], F32)
    with mk.scope():
        Wb = mk.sb("Wb0", [128, 8, 4096], BF16)
        load_weights_bf16(Wb, P["l0_w_in"], 4096, P["l0_norm"], "gT0")
        Gav = load_bcast("Gav", P["l0_a_vnorm"], 512)
        Gq = load_bcast("Gq0", P["l0_b_qnorm"], 32)
        Gk = load_bcast("Gk0", P["l0_b_knorm"], 32)
        Gmq = load_bcast("Gmq0", P["l0_m_qnorm"], 64)
        BsT = mk.sb("BsT", [128, 4], F32)
        mk.dma("sp", BsT[:], P["l0_a_bs"].rearrange("g t -> t g"), (), [BsT], allow_slow_non_contiguous=True)
        lq = [load_bcast("lq%d" % i, P["l0_b_" + n], 32) for i, n in enumerate(["lq1", "lk1", "lq2", "lk2"])]
        ltmp = mk.sb("ltmp", [128, 32], F32)
        lsum = mk.sb("lsum", [128, 2], F32)
        for i in range(2):
            mk.tt(ltmp[:], lq[2 * i][:], lq[2 * i + 1][:], ALU.mult, [lq[2 * i], lq[2 * i + 1]], [ltmp])
            mk.red(lsum[:, i:i + 1], ltmp[:], [ltmp], [lsum])
        mk.act(lsum[:], lsum[:], AF.Exp, [lsum], [lsum])
        mk.tt(LAMN[:], lsum[:, 1:2], lsum[:, 0:1], ALU.subtract, [lsum], [LAMN])
        mk.ts(LAMN[:], LAMN[:], -lam_init0, None, ALU.add, None, [LAMN], [LAMN])
        WsT = mk.sb("WsT", [128, 4, 128], BF16)
        with mk.scope():
            wsl = mk.sb("wsl", [128, 4, 128], F32)
            mk.dma("sp", wsl[:], P["l0_a_ws"].rearrange("g t s -> t g s"), (), [wsl])
            psW = mk.ps("psW", [128, 512], F32)
            for g in range(4):
                mk.tr(psW[:, g * 128:(g + 1) * 128], wsl[:, g, :], identf[:], [wsl, identf], [psW])
            mk.tt(WsT[:], psW[:].rearrange("p (g t) -> p g t", t=128), bcm(tri[:], [128, 4, 128]), ALU.mult,
                  [psW, tri], [WsT])

        if stop == 'p1a':
            mk.barrier()
            return nc, mk
        xt = [mk.sb("xt%d" % i, [128, D], F32) for i in range(3)]
        cs = [mk.sb("cs%d" % i, [128, 8], F32) for i in range(3)]
        sq = mk.sb("sq", [128, D], F32)
        ss = mk.sb("ss", [128, 16], F32)
        sqx = mk.sb("sqx", [128, D], F32)
        ssx = mk.sb("ssx", [128, 8], F32)
        xns = [mk.sb("xn%d" % i, [128, D], BF16) for i in range(2)]
        hTs = [mk.sb("hT%d" % i, [128, 8, 128], BF16) for i in range(2)]
        Us = [mk.sb("U%d" % i, [128, 512], F32) for i in range(2)]
        Vv = mk.sb("Vv", [128, 512], F32)
        Vbs = [mk.sb("Vb%d" % i, [128, 512], BF16) for i in range(2)]
        Gas = [mk.sb("Ga%d" % i, [128, 512], F32) for i in range(2)]
        TA = mk.sb("TA", [128, 512], F32)
        T = mk.sb("T", [128, 512], F32)
        Rr = mk.sb("Rr", [128, 512], F32)
        TbQs = [mk.sb("TbQ%d" % i, [128, 512], BF16) for i in range(2)]
        TbKs = [mk.sb("TbK%d" % i, [128, 512], BF16) for i in range(2)]
        QKs = [mk.sb("QKs%d" % i, [128, 4, 128], BF16) for i in range(2)]
        VS = [mk.sb("VS%d" % i, [128, 8 * VW], BF16) for i in range(2)]
        GBs = [mk.sb("GBs%d" % i, [128, 512], F32) for i in range(2)]
        Ys = [mk.sb("Ys%d" % i, [128, 768], BF16) for i in range(3)]
        MQ = mk.sb("MQ", [128, 512], F32)
        GMs = [mk.sb("GM%d" % i, [128, 256], F32) for i in range(2)]
        mqbs = [mk.sb("mqb%d" % i, [128, 256], BF16) for i in range(2)]
        mqT = mk.sb("mqT", [128, 4, 128], BF16)
        mk.memset(mqT[:], 0.0, [mqT])
        PTm2 = [mk.sb("PTm%d" % i, [128, 512], BF16) for i in range(2)]
        OM = mk.sb("OM", [128, 256], F32)
        RL = mk.sb("RL", [128, 8], F32)
        psT = mk.ps("psT", [128, 1024], BF16)
        psA = [mk.ps("psA%d" % i, [128, 512], F32) for i in range(2)]
        psZ = mk.ps("psZ", [128, 512], F32)
        psQ = mk.ps("psQ", [128, 1024], BF16)
        psS2 = [mk.ps("psS%d" % i, [128, 512], F32) for i in range(2)]
        psO = mk.ps("psO", [128, 512], F32)
        for i in range(2):
            mk.memset(VS[i][:], 1.0, [VS[i]])

        def loads0(tb):
            mk.dma("sp", xt[tb % 3][:], x_in[tb * 128:(tb + 1) * 128, :], (), [xt[tb % 3]])
            mk.dma("sp", cs[tb % 3][:, 0:4], C["c_cos0"][tb * 128:(tb + 1) * 128, :], (), [cs[tb % 3]])
            mk.dma("sp", cs[tb % 3][:, 4:8], C["c_sin0"][tb * 128:(tb + 1) * 128, :], (), [cs[tb % 3]])

        def make_B0(tb):
            par = tb % 2
            rows = slice(tb * 128, (tb + 1) * 128)
            ys = Ys[tb % 3]
            U, Vb, Ga, TbQ, TbK = Us[par], Vbs[par], Gas[par], TbQs[par], TbKs[par]

            def bz():
                for g in range(4):
                    mk.mm(psZ[:, g * 128:(g + 1) * 128], WsT[:, g, :], Vb[:, g * 128:(g + 1) * 128], True, True,
                          [WsT, Vb], [psZ])
                for g in range(4):
                    mk.stt(TA[:, g * 128:(g + 1) * 128], psZ[:, g * 128:(g + 1) * 128], BsT[:, g:g + 1],
                           U[:, g * 128:(g + 1) * 128], ALU.add, ALU.mult, [psZ, BsT, U], [TA])
                mk.tt(ys[:, 0:512], TA[:], Ga[:], ALU.mult, [TA, Ga], [ys])
                mk.dma("sp", Y[0].ap[rows, 0:512], ys[:, 0:512], [ys], [Y[0].blk[tb]])

            def bqk(Tb_, qs, dst):
                def f():
                    for j in range(4):
                        mk.tr(psQ[:, j * 128:(j + 1) * 128], Tb_[:, j * 128:(j + 1) * 128], identb[:], [Tb_, identb], [psQ])
                    mk.cp(qs[:], psQ[:, 0:512].rearrange("p (j t) -> p j t", t=128), [psQ], [qs], e="act")
                    mk.dma("sp", dst.ap.rearrange("(j p) t -> p j t", p=128)[:, :, rows], qs[:], [qs], [dst.blk[tb]])
                return f

            def tail():
                mk.dma("sp", Y[0].ap[rows, 1024:1280], ys[:, 512:768], [ys], [Y[0].blk[tb]])

            mparts = mem_attn_parts(0, mqbs[par], mqT, psQ, psS2, psO, PTm2, OM, RL, GMs[par], ys[:, 512:768], ys, tail)
            return [bz, bqk(TbQ, QKs[0], QT0), bqk(TbK, QKs[1], KT0)] + mparts

        import os as _os
        _ntb = int(_os.environ.get('P1_NTB', NTB))
        loads0(0)
        if _ntb > 1:
            loads0(1)
        xnorm_a(xt[0], sqx, ssx, xns[0])
        xnorm_b(xns[0], psT, hTs[0])
        pi = 0
        pending = []
        for tb in range(_ntb):
            if tb + 2 < _ntb:
                loads0(tb + 2)
            if tb + 1 < _ntb:
                xnorm_a(xt[(tb + 1) % 3], sqx, ssx, xns[(tb + 1) % 2])
            par = tb % 2
            cs_ = cs[tb % 3]
            hT = hTs[par]
            rows = slice(tb * 128, (tb + 1) * 128)
            for cg in range(8):
                pp = psA[pi % 2]
                pi += 1
                for k in range(8):
                    mk.mm(pp[:], hT[:, k, :], Wb[:, k, cg * 512:(cg + 1) * 512], k == 0, k == 7, [hT, Wb], [pp])
                if cg == 0:
                    mk.act(Us[par][:], pp[:], AF.Gelu_apprx_tanh, [pp], [Us[par]])
                elif cg == 1:
                    mk.act(Vv[:], pp[:], AF.Gelu_apprx_tanh, [pp], [Vv])
                    headnorm(Vv, Vv[:], 4, 128, Gav, sq, ss, gfull=True)
                    mk.cp(Vbs[par][:], Vv[:], [Vv], [Vbs[par]])
                elif cg == 2:
                    mk.act(Gas[par][:], pp[:], AF.Silu, [pp], [Gas[par]])
                elif cg in (3, 4):
                    mk.act(T[:], pp[:], AF.Copy, [pp], [T])
                    headnorm(T, T[:], 16, 32, Gq if cg == 3 else Gk, sq, ss)
                    rope(T, T[:], 16, 32, 4, cs_, cs_[:, 0:4], cs_[:, 4:8], Rr)
                    Tb_ = TbQs[par] if cg == 3 else TbKs[par]
                    mk.cp(Tb_[:], T[:], [T], [Tb_])
                elif cg == 5:
                    vs = VS[par]
                    mk.act(vs[:].rearrange("p (h d) -> p h d", d=VW)[:, :, 0:64],
                           pp[:].rearrange("p (h d) -> p h d", d=64), AF.Copy, [pp], [vs])
                    mk.dma("sp", V0e.ap[rows, :], vs[:], [vs], [V0e.blk[tb]])
                elif cg == 6:
                    gb = GBs[par]
                    mk.act(gb[:], pp[:], AF.Silu, [pp], [gb])
                    mk.dma("sp", GB0.ap[rows, :], gb[:], [gb], [GB0.blk[tb]])
                else:
                    mk.act(MQ[:, 0:256], pp[:, 0:256], AF.Copy, [pp], [MQ])
                    mk.act(GMs[par][:], pp[:, 256:512], AF.Silu, [pp], [GMs[par]])
                    headnorm(MQ, MQ[:, 0:256], 4, 64, Gmq, sq, ss)
                    mk.cp(mqbs[par][:], MQ[:, 0:256], [MQ], [mqbs[par]])
                if pending:
                    pending.pop(0)()
            while pending:
                pending.pop(0)()
            if tb + 1 < _ntb:
                xnorm_b(xns[(tb + 1) % 2], psT, hTs[(tb + 1) % 2])
            pending = make_B0(tb)
        while pending:
            pending.pop(0)()

    if stop == 'p1':
        mk.barrier()
        return nc, mk
    Wb1 = mk.sb("Wb1", [128, 8, 3864], BF16) if nlayers == 2 else None
    sc23 = mk.scope()
    sc23.__enter__()
    Wo0 = mk.sb("Wo0", [128, 10, D], BF16)
    with mk.scope():
        Gsub = load_bcast("Gsub", P["l0_b_subln"], 64)
        mk.ts(Gsub[:], Gsub[:], 1.0 - lam_init0, None, ALU.mult, None, [Gsub], [Gsub])
        Ve = mk.sb("Ve", [128, NTB, 8 * VW], BF16)
        Vsrc = V0e.ap.rearrange("(c p) f -> p c f", p=128)
        for c8 in range(8):
            mk.dma("sp", Ve[:, c8 * 4:(c8 + 1) * 4, :], Vsrc[:, c8 * 4:(c8 + 1) * 4, :], V0e.bl(c8 * 4, c8 * 4 + 4), [Ve])
        QTh = [mk.sb("QTh%d" % i, [128, S], BF16) for i in range(2)]
        KTh = [[mk.sb("KTh%d_%d" % (i, m), [128, S], BF16) for m in range(2)] for i in range(2)]
        for i in range(2):
            mk.memset(QTh[i][:], 0.0, [QTh[i]])
            for m in range(2):
                mk.memset(KTh[i][m][:], 0.0, [KTh[i][m]])
        PT = [mk.sb("PT%d" % i, [128, 1024], BF16) for i in range(4)]
        GBt = [mk.sb("GBt%d" % i, [128, 4, 64], F32) for i in range(2)]
        YB = [mk.sb("YB%d" % i, [128, 4, 64], BF16) for i in range(2)]
        R0 = mk.sb("R0", [128, 8], F32)
        Aa = mk.sb("Aa", [128, 256], F32)
        Bb = mk.sb("Bb", [128, 256], F32)
        sq = mk.sb("sqd", [128, 256], F32)
        ss = mk.sb("ssd", [128, 8], F32)
        psS2 = [mk.ps("psS2_%d" % i, [128, 1024], F32) for i in range(3)]
        psO2 = [mk.ps("psO2_%d" % m, [128, 512], F32) for m in range(2)]
        OT = [mk.sb("OT%d" % m, [VW, 512], F32) for m in range(2)]
        sc = 32 ** -0.5

        def loadh(h):
            mk.dma("sp", QTh[h % 2][0:64, :], QT0.ap[h * 64:(h + 1) * 64, :], QT0.bl(), [QTh[h % 2]])
            for m in range(2):
                mk.dma("sp", KTh[h % 2][m][m * 32:(m + 1) * 32, :], KT0.ap[h * 64 + m * 32:h * 64 + (m + 1) * 32, :], KT0.bl(),
                       [KTh[h % 2][m]])

        loadh(0)
        wst_pf = [mk.sb("wstpf%d" % i, [128, 2048], F32) for i in range(2)]
        load_wo(Wo0, 0, wst_pf, dve_only=True)
        if nlayers == 2:
            load_weights_bf16(Wb1, P["l1_w_in"], 3864, P["l1_norm"], "gT1",
                              segs=[(0, 0, 2432), (2432, 2560, 128), (2560, 2432, 128), (2688, 2688, 3864 - 2688)],
                              st=wst_pf, dve_only=True)
        it = 0
        ep = 0
        pipe = Pipe(2)

        cnt2 = {"it": 0}

        def nxt2():
            i = cnt2["it"]
            cnt2["it"] += 1
            return psS2[i % 3], PT[i % 4]

        def p2_A(pS, k_, q_, j, G, c0):
            for m in range(2):
                mk.mm(pS[:, m * 512 + c0:(m + 1) * 512], k_[m][:, j * 128:(j + 1) * 128],
                      q_[:, G * 512 + c0:(G + 1) * 512], True, True, [k_[m], q_], [pS])

        def p2_B(pS, pt, h, j, G, c0, epi):
            pS3 = pS[:].rearrange("p (m c) -> p m c", m=2)
            pt3 = pt[:].rearrange("p (m c) -> p m c", m=2)
            mk.act(pt3[:, :, c0:512], pS3[:, :, c0:512], AF.Exp, [pS], [pt], scale=sc)
            if j >= 4 * G:
                mk.tt(pt3[:, :, c0:c0 + 128], pt3[:, :, c0:c0 + 128], bcm(tri[:], [128, 2, 128]), ALU.mult, [pt, tri], [pt])
            for m in range(2):
                mk.mm(psO2[m][0:VW, c0:512], Ve[:, j, h * VW:(h + 1) * VW], pt[:, m * 512 + c0:(m + 1) * 512],
                      j == 0, j == 4 * G + 3, [pt, Ve], [psO2[m]])
            if epi is not None:
                epi(pS)

        def p2_epi(par, gbt, h, G, pE):
            for m in range(2):
                mk.act(OT[m][:], psO2[m][0:VW, :], AF.Copy, [psO2[m]], [OT[m]])
                for i in range(4):
                    mk.tr(pE[:, m * 512 + i * VW:m * 512 + (i + 1) * VW], OT[m][:, i * 128:(i + 1) * 128], identf[0:VW, 0:VW],
                          [OT[m], identf], [pE])
            o0 = pE[:, 0:4 * VW].rearrange("p (i d) -> p i d", d=VW)
            o1 = pE[:, 512:512 + 4 * VW].rearrange("p (i d) -> p i d", d=VW)
            mk.recip(R0[:, 0:4], o0[:, :, 64], [pE], [R0])
            mk.recip(R0[:, 4:8], o1[:, :, 64], [pE], [R0])
            mk.ts(R0[:, 4:8], R0[:, 4:8], LAMN[:, 0:1], None, ALU.mult, None, [R0, LAMN], [R0])
            A3 = Aa[:].rearrange("p (i d) -> p i d", d=64)
            B3 = Bb[:].rearrange("p (i d) -> p i d", d=64)
            mk.tt(A3, o0[:, :, 0:64], bc(R0[:, 0:4], [128, 4, 64]), ALU.mult, [pE, R0], [Aa])
            mk.tt(B3, o1[:, :, 0:64], bc(R0[:, 4:8], [128, 4, 64]), ALU.mult, [pE, R0], [Bb])
            mk.tt(Aa[:], Aa[:], Bb[:], ALU.add, [Aa, Bb], [Aa])
            headnorm(Aa, Aa[:], 4, 64, Gsub, sq, ss, ln=True)
            yb = YB[par]
            mk.tt(yb[:], A3, gbt[:], ALU.mult, [Aa, gbt], [yb])
            mk.dma("sp", Y[0].ap[G * 512:(G + 1) * 512, 512 + h * 64:512 + (h + 1) * 64].rearrange("(i p) d -> p i d", p=128),
                   yb[:], [yb], Y[0].bl(G * 4, G * 4 + 4))

        import functools as _ft
        for h in range(8):
            if h + 1 < 8:
                loadh(h + 1)
            q_ = QTh[h % 2]
            k_ = KTh[h % 2]
            for G in range(8):
                par = ep % 2
                ep += 1
                gbt = GBt[par]
                mk.dma("sp", gbt[:], GB0.ap[G * 512:(G + 1) * 512, h * 64:(h + 1) * 64].rearrange("(i p) d -> p i d", p=128),
                       GB0.bl(G * 4, G * 4 + 4), [gbt])
                nj = 4 * G + 4
                for j in range(nj):
                    imin = max(0, j - 4 * G)
                    c0 = imin * 128
                    pS, pt = nxt2()
                    epi = _ft.partial(p2_epi, par, gbt, h, G) if j == nj - 1 else None
                    pipe.push(_ft.partial(p2_A, pS, k_, q_, j, G, c0),
                              _ft.partial(p2_B, pS, pt, h, j, G, c0, epi))
        pipe.flush()

    if stop == 'p2':
        mk.barrier()
        return nc, mk
    if nlayers == 1:
        out_proj(0, x_in, None, out_ap, None, Wo_pre=Wo0)
        sc23.__exit__(None, None, None)
    else:
        out_proj(0, x_in, None, X1.ap, X1.blk, Wo_pre=Wo0)
        sc23.__exit__(None, None, None)
        build_layer1(mk, nc, P, C, locals())
    mk.barrier()
    return nc, mk


def build_layer1(mk, nc, P, C, env):
    g_ = env
    (identb, identf, tri, ntri, epsT, KmT, Vm, X1, Y, QT1, CRT, KST, KWT, VSe, VWe, GT1, GD1, out_ap) = [g_[k] for k in (
        "identb", "identf", "tri", "ntri", "epsT", "KmT", "Vm", "X1", "Y", "QT1", "CRT", "KST", "KWT", "VSe", "VWe",
        "GT1", "GD1", "out_ap")]
    GC1 = g_["GC1"]
    (headnorm, rope, load_bcast, load_weights_bf16, mem_attn, xnorm_to_hT, out_proj, rstd_inplace, mem_attn_parts,
     xnorm_a, xnorm_b) = [g_[k] for k in (
        "headnorm", "rope", "load_bcast", "load_weights_bf16", "mem_attn", "xnorm_to_hT", "out_proj", "rstd_inplace",
        "mem_attn_parts", "xnorm_a", "xnorm_b")]
    stop = g_["stop"]
    NC1 = 3864
    KcT = [mk.sb("KcT%d" % g, [128, 256], BF16) for g in range(2)]
    for g in range(2):
        mk.memset(KcT[g][:], 0.0, [KcT[g]])
    Vc = [mk.sb("Vc%d" % g, [128, 2, VW], BF16) for g in range(2)]
    Gdk = load_bcast("Gdk", P["l1_d_knorm"], 64)

    with mk.scope():
        Wb = g_["Wb1"]
        Gcn = load_bcast("Gcn", P["l1_c_norm"], 512)
        Gdq = load_bcast("Gdq", P["l1_d_qnorm"], 64)
        Gmq = load_bcast("Gmq1", P["l1_m_qnorm"], 64)
        cwT = mk.sb("cwT", [128, 4, 32], F32)
        with mk.scope():
            psX = mk.ps("psX", [128, 512], F32)
            cw = mk.sb("cw", [32, 512], F32)
            mk.dma("sp", cw[0:31, :], P["l1_c_conv_w"], (), [cw])
            for ch in range(4):
                mk.tr(psX[:, ch * 32:ch * 32 + 31], cw[0:31, ch * 128:(ch + 1) * 128], identf[0:31, 0:31], [cw, identf], [psX])
            mk.cp(cwT[:, :, 0:31], psX[:, 0:128].rearrange("p (c j) -> p c j", j=32)[:, :, 0:31], [psX], [cwT])
        Dg = mk.sb("Dg", [128, 4, 31, 128], BF16)
        for ch in range(4):
            for j in range(31):
                mk.ts(Dg[:, ch, j, :], identf[:], cwT[:, ch, j:j + 1], None, ALU.mult, None, [identf, cwT], [Dg],
                      e="dve")
        cbT = mk.sb("cbT", [128, 4], F32)
        mk.dma("sp", cbT[:], P["l1_c_conv_b"].rearrange("(c p) -> p c", p=128), (), [cbT], allow_slow_non_contiguous=True)
        HTr = [mk.sb("HTr%d" % i, [128, 4, 30 + 512], BF16) for i in range(2)]
        mk.memset(HTr[0][:, :, 0:30], 0.0, [HTr[0]])

        xt = [mk.sb("xt%d" % i, [128, D], F32) for i in range(3)]
        cs = [mk.sb("cs%d" % i, [128, 16], F32) for i in range(3)]
        sq = mk.sb("sq", [128, 512], F32)
        ss = mk.sb("ss", [128, 16], F32)
        sqx = mk.sb("sqx", [128, D], F32)
        ssx = mk.sb("ssx", [128, 8], F32)
        xns = [mk.sb("xn%d" % i, [128, D], BF16) for i in range(2)]
        hTs = [mk.sb("hT%d" % i, [128, 8, 128], BF16) for i in range(2)]
        SG = mk.sb("SG", [128, 512], F32)
        Hbs = [mk.sb("Hb%d" % i, [128, 512], BF16) for i in range(2)]
        YC4 = mk.sb("YC4", [128, 4, 512], F32)
        Gcl = [mk.sb("Gcl%d" % i, [128, 512], F32) for i in range(2)]
        ycs = [mk.sb("ycs%d" % i, [128, 512], BF16) for i in range(2)]
        YN = mk.sb("YN", [128, 512], F32)
        Gcs = [mk.sb("Gc%d" % i, [128, 512], F32) for i in range(2)]
        T = mk.sb("T", [128, 512], F32)
        Rr = mk.sb("Rr", [128, 512], F32)
        TbQs = [mk.sb("TbQ%d" % i, [128, 512], BF16) for i in range(2)]
        TbCs = [mk.sb("TbC%d" % i, [128, 512], BF16) for i in range(2)]
        QTs = [mk.sb("QTs%d" % i, [64, 8, 128], BF16) for i in range(2)]
        CRs = [mk.sb("CRs%d" % i, [64, 4, 128], BF16) for i in range(2)]
        KSs = [mk.sb("KSs%d" % i, [64, 2, 128], BF16) for i in range(2)]
        KWs = [mk.sb("KWs%d" % i, [64, 2, 128], BF16) for i in range(2)]
        VSs = [mk.sb("VSs%d" % i, [128, 2 * VW], BF16) for i in range(2)]
        VWs = [mk.sb("VWs%d" % i, [128, 2 * VW], BF16) for i in range(2)]
        GTs = [mk.sb("GTs%d" % i, [128, 24], F32) for i in range(2)]
        GDs = [mk.sb("GDs%d" % i, [128, 512], F32) for i in range(2)]
        Ys = [mk.sb("Ys%d" % i, [128, 768], BF16) for i in range(2)]
        MQ = mk.sb("MQ", [128, 512], F32)
        GMs = [mk.sb("GM%d" % i, [128, 256], F32) for i in range(2)]
        mqbs = [mk.sb("mqb%d" % i, [128, 256], BF16) for i in range(2)]
        mqT = mk.sb("mqT", [128, 4, 128], BF16)
        mk.memset(mqT[:], 0.0, [mqT])
        PTm2 = [mk.sb("PTm%d" % i, [128, 512], BF16) for i in range(2)]
        OM = mk.sb("OM", [128, 256], F32)
        RL = mk.sb("RL", [128, 8], F32)
        psT = mk.ps("psT", [128, 1024], BF16)
        psA = [mk.ps("psA%d" % i, [128, 512], F32) for i in range(4)]
        pctr = {"i": 0}

        def nbank():
            b_ = psA[pctr["i"] % 4]
            pctr["i"] += 1
            return b_

        psQ = mk.ps("psQ", [128, 1024], BF16)
        psS = mk.ps("psS", [128, 512], F32)
        psO = mk.ps("psO", [128, 512], F32)
        psC = psS
        psCt = psS
        for i in range(2):
            mk.memset(VSs[i][:], 1.0, [VSs[i]])
            mk.memset(VWs[i][:], 1.0, [VWs[i]])

        def loads1(tb):
            mk.dma("sp", xt[tb % 3][:], X1.ap[tb * 128:(tb + 1) * 128, :], [X1.blk[tb]], [xt[tb % 3]])
            mk.dma("sp", cs[tb % 3][:, 0:8], C["c_cos1"][tb * 128:(tb + 1) * 128, :], (), [cs[tb % 3]])
            mk.dma("sp", cs[tb % 3][:, 8:16], C["c_sin1"][tb * 128:(tb + 1) * 128, :], (), [cs[tb % 3]])

        def tr_heads(src, ncols_heads, dstbuf, col0=0):
            for h in range(ncols_heads):
                mk.tr(psQ[0:64, h * 128:(h + 1) * 128], src[:, col0 + h * 64:col0 + (h + 1) * 64], identb[:], [src, identb], [psQ])
            mk.cp(dstbuf[:], psQ[0:64, 0:ncols_heads * 128].rearrange("p (r t) -> p r t", t=128), [psQ], [dstbuf], e="act")

        def make_B1(tb):
            par = tb % 2
            rows = slice(tb * 128, (tb + 1) * 128)
            ys = Ys[tb % 2]
            TbQ, TbC, Gc = TbQs[par], TbCs[par], Gcs[par]

            sb_i = tb // 4
            sub = tb % 4
            HTc = HTr[sb_i % 2]

            def b_conv():
                Hb = Hbs[par]
                if sub == 0 and sb_i >= 1:
                    mk.cp(HTc[:, :, 0:30], HTr[(sb_i - 1) % 2][:, :, 512:542], [HTr[(sb_i - 1) % 2]], [HTc])
                for ch in range(4):
                    mk.tr(psQ[:, ch * 128:(ch + 1) * 128], Hb[:, ch * 128:(ch + 1) * 128], identb[:], [Hb, identb], [psQ])
                mk.cp(HTc[:, :, 30 + sub * 128:30 + (sub + 1) * 128], psQ[:, 0:512].rearrange("p (c t) -> p c t", t=128),
                      [psQ], [HTc], e="act")

            def conv_ch(ch):
                def f():
                    for j in range(31):
                        mk.mm(psC[:, 0:512], Dg[:, ch, j, :], HTc[:, ch, j:j + 512], j == 0, j == 30, [Dg, HTc], [psC])
                    mk.ts(YC4[:, ch, :], psC[:, 0:512], cbT[:, ch:ch + 1], None, ALU.add, None, [psC, cbT], [YC4])
                return f

            def post_i(i):
                def f():
                    tbi = sb_i * 4 + i
                    rws = slice(tbi * 128, (tbi + 1) * 128)
                    gl = Gcl[i % 2]
                    yo = ycs[i % 2]
                    mk.dma("sp", gl[:], GC1.ap[rws, :], [GC1.blk[tbi]], [gl])
                    for ch in range(4):
                        mk.tr(psCt[:, ch * 128:(ch + 1) * 128], YC4[:, ch, i * 128:(i + 1) * 128], identf[:], [YC4, identf], [psCt])
                    mk.act(YN[:], psCt[:], AF.Copy, [psCt], [YN])
                    headnorm(YN, YN[:], 1, 512, Gcn, sq2, ss2, gfull=True)
                    mk.act(YN[:], YN[:], AF.Silu, [YN], [YN])
                    mk.tt(yo[:], YN[:], gl[:], ALU.mult, [YN, gl], [yo])
                    mk.dma("sp", Y[1].ap[rws, 0:512], yo[:], [yo], [Y[1].blk[tbi]])
                return f

            def b_q():
                qs = QTs[par]
                tr_heads(TbQ, 8, qs)
                mk.dma("sp", QT1.ap[:, :, rows], qs[:], [qs], [QT1.blk[tb]])

            def b_kv():
                cr = CRs[par]
                tr_heads(TbC, 4, cr)
                mk.dma("sp", CRT.ap[:, :, rows], cr[:], [cr], [CRT.blk[tb]])
                ks = KSs[par]
                tr_heads(TbC, 2, ks, col0=256)
                mk.dma("sp", KST.ap[:, :, rows], ks[:], [ks], [KST.blk[tb]])
                kw = KWs[par]
                tr_heads(TbC, 2, kw, col0=384)
                mk.dma("sp", KWT.ap[:, :, rows], kw[:], [kw], [KWT.blk[tb]])

            def tail():
                mk.dma("sp", Y[1].ap[rows, 1024:1280], ys[:, 512:768], [ys], [Y[1].blk[tb]])

            mparts = mem_attn_parts(1, mqbs[par], mqT, psQ, [psS, psS], psO, PTm2, OM, RL, GMs[par], ys[:, 512:768], ys, tail)
            slow_new = ([conv_ch(c) for c in range(4)] + [post_i(i) for i in range(4)]) if sub == 3 else []
            return [b_conv, b_q, b_kv] + mparts, slow_new

        sq2 = sq
        ss2 = mk.sb("ss2", [128, 8], F32)
        loads1(0)
        loads1(1)
        xnorm_a(xt[0], sqx, ssx, xns[0])
        xnorm_b(xns[0], psT, hTs[0])
        pending = []
        slow = []

        def pop():
            if pending:
                pending.pop(0)()
            elif slow:
                slow.pop(0)()

        for tb in range(NTB):
            if tb + 2 < NTB:
                loads1(tb + 2)
            if tb + 1 < NTB:
                xnorm_a(xt[(tb + 1) % 3], sqx, ssx, xns[(tb + 1) % 2])
            par = tb % 2
            cs_ = cs[tb % 3]
            hT = hTs[par]
            rows = slice(tb * 128, (tb + 1) * 128)

            def proj(pp, c0, n):
                for k in range(8):
                    mk.mm(pp[:, 0:n], hT[:, k, :], Wb[:, k, c0:c0 + n], k == 0, k == 7, [hT, Wb], [pp])

            pa = nbank()
            pb = nbank()
            proj(pa, 0, 512)
            proj(pb, 512, 512)
            mk.act(SG[:], pb[:], AF.Sigmoid, [pb], [SG])
            mk.tt(Hbs[par][:], pa[:], SG[:], ALU.mult, [pa, SG], [Hbs[par]])
            pop()
            pp = nbank()
            proj(pp, 1024, 512)
            mk.act(Gcs[par][:], pp[:], AF.Silu, [pp], [Gcs[par]])
            mk.dma("sp", GC1.ap[rows, :], Gcs[par][:], [Gcs[par]], [GC1.blk[tb]])
            pop()
            pop()
            pp = nbank()
            proj(pp, 1536, 512)
            mk.act(T[:], pp[:], AF.Copy, [pp], [T])
            headnorm(T, T[:], 8, 64, Gdq, sq, ss)
            rope(T, T[:], 8, 64, 8, cs_, cs_[:, 0:8], cs_[:, 8:16], Rr)
            mk.cp(TbQs[par][:], T[:], [T], [TbQs[par]])
            pop()
            pp = nbank()
            proj(pp, 2048, 512)
            mk.act(T[:], pp[:], AF.Copy, [pp], [T])
            TbC = TbCs[par]
            mk.cp(TbC[:, 0:256], T[:, 0:256], [T], [TbC])
            headnorm(T, T[:, 256:512], 4, 64, Gdk, sq, ss)
            rope(T, T[:, 256:512], 4, 64, 8, cs_, cs_[:, 0:8], cs_[:, 8:16], Rr)
            mk.cp(TbC[:, 256:512], T[:, 256:512], [T], [TbC])
            pop()
            pp = nbank()
            proj(pp, 2560, 280)
            vs = VSs[par]
            mk.act(vs[:].rearrange("p (g d) -> p g d", d=VW)[:, :, 0:64], pp[:, 0:128].rearrange("p (g d) -> p g d", d=64),
                   AF.Copy, [pp], [vs])
            mk.dma("sp", VSe.ap[rows, :], vs[:], [vs], [VSe.blk[tb]])
            vw = VWs[par]
            mk.act(vw[:].rearrange("p (g d) -> p g d", d=VW)[:, :, 0:64], pp[:, 128:256].rearrange("p (g d) -> p g d", d=64),
                   AF.Copy, [pp], [vw])
            mk.dma("sp", VWe.ap[rows, :], vw[:], [vw], [VWe.blk[tb]])
            gt = GTs[par]
            mk.act(gt[:], pp[:, 256:280], AF.Sigmoid, [pp], [gt])
            mk.dma("sp", GT1.ap[rows, :], gt[:], [gt], [GT1.blk[tb]])
            pop()
            pp = nbank()
            proj(pp, 2840, 512)
            gd = GDs[par]
            mk.act(gd[:], pp[:], AF.Silu, [pp], [gd])
            mk.dma("sp", GD1.ap[rows, :], gd[:], [gd], [GD1.blk[tb]])
            pop()
            pop()
            pp = nbank()
            proj(pp, 3352, 512)
            mk.act(MQ[:, 0:256], pp[:, 0:256], AF.Copy, [pp], [MQ])
            mk.act(GMs[par][:], pp[:, 256:512], AF.Silu, [pp], [GMs[par]])
            headnorm(MQ, MQ[:, 0:256], 4, 64, Gmq, sq, ss)
            mk.cp(mqbs[par][:], MQ[:, 0:256], [MQ], [mqbs[par]])
            pop()
            while pending:
                pending.pop(0)()
            if tb + 1 < NTB:
                xnorm_b(xns[(tb + 1) % 2], psT, hTs[(tb + 1) % 2])
            fast_new, slow_new = make_B1(tb)
            if slow_new:
                while slow:
                    slow.pop(0)()
                slow.extend(slow_new)
            pending.extend(fast_new)
        while pending:
            pending.pop(0)()
        while slow:
            slow.pop(0)()
    if stop == 'p4':
        return

    with mk.scope():
        Gd1 = Gdk
        cc = mk.sb("cc", [128, 2, 16], F32)
        mk.dma("sp", cc[:, :, 0:8], C["c_cosc"].rearrange("(c p) d -> p c d", p=128), (), [cc])
        mk.dma("sp", cc[:, :, 8:16], C["c_sinc"].rearrange("(c p) d -> p c d", p=128), (), [cc])
        crt = [mk.sb("crt%d" % i, [64, S], BF16) for i in range(2)]
        w1s = mk.sb("w1s", [64, 32, 128], F32)
        w1b = mk.sb("w1b", [128, 32, 128], BF16)
        mk.memset(w1b[:], 0.0, [w1b])
        w2s = mk.sb("w2s", [128, 64], F32)
        w2b = mk.sb("w2b", [128, 64], BF16)
        pl = mk.sb("pl", [32, 64], F32)
        posT = mk.sb("posT", [64, 32], F32)
        A = mk.sb("A", [128, 32, 256], BF16)
        hs = mk.sb("hs", [128, 256], BF16)
        T = mk.sb("Tc", [128, 64], F32)
        Tb = mk.sb("Tcb", [128, 64], BF16)
        Rr = mk.sb("Rrc", [128, 64], F32)
        sq = mk.sb("sqc", [128, 64], F32)
        ss = mk.sb("ssc", [128, 8], F32)
        psH = mk.ps("psH", [128, 512], F32)
        psC2 = mk.ps("psC2", [128, 512], F32)
        psP = mk.ps("psP", [128, 512], F32)
        psK = mk.ps("psK", [128, 1024], BF16)
        mk.memset(A[:], 0.0, [A])
        for g in range(2):
            mk.memset(Vc[g][:], 1.0, [Vc[g]])
        it = 0
        for kv in range(2):
            nm = "k" if kv == 0 else "v"
            mk.dma("sp", w1s[:], P["l1_d_cmp_w1_" + nm].rearrange("(l d) h -> d l h", d=64), (), [w1s])
            mk.cp(w1b[0:64, :, :], w1s[:], [w1s], [w1b], e="act")
            mk.dma("sp", w2s[:], P["l1_d_cmp_w2_" + nm], (), [w2s])
            mk.cp(w2b[:], w2s[:], [w2s], [w2b])
            mk.dma("sp", pl[:], P["l1_d_cmp_pos_" + nm], (), [pl])
            mk.tr(psP[0:64, 0:32], pl[:], identf[0:32, 0:32], [pl, identf], [psP])
            mk.cp(posT[:], psP[0:64, 0:32], [psP], [posT])
            for g in range(2):
                c_ = crt[it % 2]
                it += 1
                mk.dma("sp", c_[:], CRT.ap[:, kv * 2 + g, :], CRT.bl(), [c_])
                for l in range(32):
                    mk.ts(A[0:64, l, 0:255], c_[:, l:l + 16 * 254 + 1:16], posT[:, l:l + 1], None, ALU.add, None,
                          [c_, posT], [A])
                for l in range(32):
                    mk.mm(psH[:, 0:256], w1b[:, l, :], A[:, l, :], l == 0, l == 31, [w1b, A], [psH])
                mk.act(hs[:], psH[:, 0:256], AF.Silu, [psH], [hs])
                for c in range(2):
                    mk.mm(psC2[:, c * 64:(c + 1) * 64], hs[:, c * 128:(c + 1) * 128], w2b[:], True, True, [hs, w2b], [psC2])
                for c in range(2):
                    if kv == 0:
                        mk.act(T[:], psC2[:, c * 64:(c + 1) * 64], AF.Copy, [psC2], [T])
                        headnorm(T, T[:], 1, 64, Gd1, sq, ss)
                        rope(T, T[:], 1, 64, 8, cc, cc[:, c, 0:8], cc[:, c, 8:16], Rr)
                        mk.cp(Tb[:], T[:], [T], [Tb])
                        mk.tr(psK[0:64, 0:128], Tb[:], identb[:], [Tb, identb], [psK])
                        mk.cp(KcT[g][0:64, c * 128:(c + 1) * 128], psK[0:64, 0:128], [psK], [KcT[g]])
                    else:
                        mk.cp(Vc[g][:, c, 0:64], psC2[:, c * 64:(c + 1) * 64], [psC2], [Vc[g]])
    if stop == 'p5':
        return

    with mk.scope():
        import functools as _ft
        cmask = mk.sb("cmask", [128, 2, S], BF16)
        mk.dma("sp", cmask[:], C["c_cmask"].rearrange("(c p) q -> p c q", p=128), (), [cmask])
        OVL = mk.sb("OVL", [128, 2, 64], BF16)
        mk.dma("sp", OVL[:], C["c_ovl"].rearrange("(c p) j -> p c j", p=128), (), [OVL])
        QTg = mk.sb("QTg", [128, 4, S], BF16)
        KsT = mk.sb("KsT", [128, S], BF16)
        KwT = mk.sb("KwT", [128, S], BF16)
        mk.memset(QTg[:], 0.0, [QTg])
        mk.dma("sp", KsT[64:128, :], C["c_E"], (), [KsT])
        mk.memset(KwT[:], 0.0, [KwT])
        Vs = mk.sb("Vs", [128, NTB, VW], BF16)
        Vw = mk.sb("Vw", [128, NTB, VW], BF16)
        PT = [mk.sb("PT%d" % i, [128, 512], BF16) for i in range(4)]
        GTt = [mk.sb("GTt%d" % i, [128, 4, 24], F32) for i in range(2)]
        GDt = [mk.sb("GDt%d" % i, [128, 4, 256], F32) for i in range(2)]
        KA = [mk.sb("KA%d" % i, [128, 4, 128], F32) for i in range(2)]
        OC = mk.sb("OC", [128, 4, 4, 64], F32)
        IMP = mk.sb("IMP", [128, 4, 64], F32)
        IF = mk.sb("IF", [128, 64], F32)
        IW = mk.sb("IW", [128, 64], F32)
        M8 = mk.sb("M8", [128, 16], F32)
        NEG = mk.sb("NEG", [128, 128], BF16)
        mk.memset(NEG[:], 0.0, [NEG])
        RL = mk.sb("RLn", [128, 16], F32)
        YD = mk.sb("YD", [128, 4, 64], F32)
        YT = mk.sb("YT", [128, 4, 64], F32)
        YDb = [mk.sb("YDb%d" % i, [128, 4, 64], BF16) for i in range(2)]
        OTa = mk.sb("OTa", [VW, 512], F32)
        OTb = mk.sb("OTb", [VW, 512], F32)
        tiny = mk.sb("tiny", [128, 1], F32)
        mk.memset(tiny[:], 1e-30, [tiny])
        psS = [mk.ps("psSn%d" % i, [128, 512], F32) for i in range(3)]
        psOa = mk.ps("psOa", [128, 512], F32)
        psOb = mk.ps("psOb", [128, 512], F32)
        psN = mk.ps("psN", [128, 1024], BF16)
        psEa = mk.ps("psEa", [128, 512], F32)
        psEb = mk.ps("psEb", [128, 512], F32)
        cnt = {"it": 0, "ep": 0}
        pipe = Pipe(2)

        def nxt():
            i = cnt["it"]
            cnt["it"] += 1
            return psS[i % 3], PT[i % 4]

        def tr_back(src_ps, OTx, psE, nrow, wout):
            mk.act(OTx[0:nrow, :], src_ps[0:nrow, :], AF.Copy, [src_ps], [OTx])
            for i in range(4):
                mk.tr(psE[:, i * wout:i * wout + nrow], OTx[0:nrow, i * 128:(i + 1) * 128], identf[0:nrow, 0:nrow],
                      [OTx, identf], [psE])

        def cmp_A(pS, g, c, r, qrows):
            mk.mm(pS[:, 0:512], KcT[g][:, c * 128:(c + 1) * 128], QTg[:, r, qrows], True, True, [KcT[g], QTg], [pS])

        def cmp_B(pS, pt, g, c, qrows, first, last, epi):
            mk.act(pt[:], pS[:], AF.Exp, [pS], [pt], scale=0.125)
            mk.tt(pt[:], pt[:], cmask[:, c, qrows], ALU.mult, [pt, cmask], [pt])
            mk.mm(psOa[0:VW, :], Vc[g][:, c, :], pt[:], first, last, [pt, Vc[g]], [psOa])
            mk.mm(psOb[0:64, :], OVL[:, c, :], pt[:], first, last, [pt, OVL], [psOb])
            if epi is not None:
                epi()

        def cmp_epi(r, h, gtt):
            tr_back(psOa, OTa, psEa, VW, VW)
            tr_back(psOb, OTb, psEb, 64, 64)
            oc3 = psEa[:, 0:4 * VW].rearrange("p (i d) -> p i d", d=VW)
            mk.ts(RL[:, 0:4], oc3[:, :, 64], tiny[:, 0:1], None, ALU.max, None, [psEa, tiny], [RL])
            mk.recip(RL[:, 0:4], RL[:, 0:4], [RL], [RL])
            mk.tt(RL[:, 4:8], RL[:, 0:4], gtt[:, :, h * 3], ALU.mult, [RL, gtt], [RL])
            mk.tt(OC[:, r, :, :], oc3[:, :, 0:64], bc(RL[:, 4:8], [128, 4, 64]), ALU.mult, [psEa, RL], [OC])
            im3 = psEb[:, 0:256].rearrange("p (i d) -> p i d", d=64)
            if r == 0:
                mk.tt(IMP[:], im3, bc(RL[:, 0:4], [128, 4, 64]), ALU.mult, [psEb, RL], [IMP])
            else:
                mk.tt(YT[:], im3, bc(RL[:, 0:4], [128, 4, 64]), ALU.mult, [psEb, RL], [YT])
                mk.tt(IMP[:], IMP[:], YT[:], ALU.add, [IMP, YT], [IMP])

        def sel_A(pS, j, r, G, c0):
            mk.mm(pS[:, c0:512], KsT[:, j * 128:(j + 1) * 128], QTg[:, r, G * 512 + c0:(G + 1) * 512], True, True,
                  [KsT, QTg], [pS])

        def sel_B(pS, pt, j, G, c0):
            mk.act(pt[:, c0:512], pS[:, c0:512], AF.Exp, [pS], [pt], scale=0.125)
            if j >= 4 * G:
                mk.tt(pt[:, c0:c0 + 128], pt[:, c0:c0 + 128], tri[:], ALU.mult, [pt, tri], [pt])
            mk.mm(psOa[0:VW, c0:512], Vs[:, j, :], pt[:, c0:512], j == 0, j == 4 * G + 3, [pt, Vs], [psOa])

        def win_A(pS, j, r, G, c0, c1):
            mk.mm(pS[:, c0:c1], KwT[:, j * 128:(j + 1) * 128], QTg[:, r, G * 512 + c0:G * 512 + c1], True, True,
                  [KwT, QTg], [pS])

        def win_B(pS, pt, j, G, c0, c1, rel, first, last, epi):
            mk.act(pt[:, c0:c1], pS[:, c0:c1], AF.Exp, [pS], [pt], scale=0.125)
            if rel >= 0:
                mk.tt(pt[:, rel * 128:(rel + 1) * 128], pt[:, rel * 128:(rel + 1) * 128], tri[:], ALU.mult,
                      [pt, tri], [pt])
            if rel + 4 <= 3:
                a = rel + 4
                mk.tt(pt[:, a * 128:(a + 1) * 128], pt[:, a * 128:(a + 1) * 128], ntri[:], ALU.mult,
                      [pt, ntri], [pt])
            mk.mm(psOb[0:VW, c0:c1], Vw[:, j, :], pt[:, c0:c1], first, last, [pt, Vw], [psOb])
            if epi is not None:
                epi()

        def fin_epi(r, h, g, G, gtt, gdt, yb, qrows):
            tr_back(psOa, OTa, psEa, VW, VW)
            tr_back(psOb, OTb, psEb, VW, VW)
            os3 = psEa[:, 0:4 * VW].rearrange("p (i d) -> p i d", d=VW)
            ow3 = psEb[:, 0:4 * VW].rearrange("p (i d) -> p i d", d=VW)
            mk.recip(RL[:, 8:12], os3[:, :, 64], [psEa], [RL])
            mk.recip(RL[:, 12:16], ow3[:, :, 64], [psEb], [RL])
            mk.tt(RL[:, 8:12], RL[:, 8:12], gtt[:, :, h * 3 + 1], ALU.mult, [RL, gtt], [RL])
            mk.tt(RL[:, 12:16], RL[:, 12:16], gtt[:, :, h * 3 + 2], ALU.mult, [RL, gtt], [RL])
            mk.tt(YD[:], os3[:, :, 0:64], bc(RL[:, 8:12], [128, 4, 64]), ALU.mult, [psEa, RL], [YD])
            mk.tt(YT[:], ow3[:, :, 0:64], bc(RL[:, 12:16], [128, 4, 64]), ALU.mult, [psEb, RL], [YT])
            mk.tt(YD[:], YD[:], YT[:], ALU.add, [YD, YT], [YD])
            mk.tt(YD[:], YD[:], OC[:, r, :, :], ALU.add, [YD, OC], [YD])
            mk.tt(yb[:], YD[:], gdt[:, :, r * 64:(r + 1) * 64], ALU.mult, [YD, gdt], [yb])
            mk.dma("sp", Y[1].ap[qrows, 512 + h * 64:512 + (h + 1) * 64].rearrange("(i p) d -> p i d", p=128),
                   yb[:], [yb], Y[1].bl(G * 4, G * 4 + 4))

        for g in range(2):
            mk.dma("sp", QTg[0:64, :, :], QT1.ap[:, g * 4:(g + 1) * 4, :], QT1.bl(), [QTg])
            mk.dma("sp", KsT[0:64, :], KST.ap[:, g, :], KST.bl(), [KsT])
            mk.dma("sp", KwT[0:64, :], KWT.ap[:, g, :], KWT.bl(), [KwT])
            mk.dma("sp", Vs[:], VSe.ap[:, g * VW:(g + 1) * VW].rearrange("(c p) f -> p c f", p=128), VSe.bl(), [Vs])
            mk.dma("sp", Vw[:], VWe.ap[:, g * VW:(g + 1) * VW].rearrange("(c p) f -> p c f", p=128), VWe.bl(), [Vw])
            for G in range(8):
                par = cnt["ep"] % 2
                cnt["ep"] += 1
                gtt = GTt[par]
                gdt = GDt[par]
                ka = KA[par]
                qrows = slice(G * 512, (G + 1) * 512)
                mk.dma("sp", gtt[:], GT1.ap[qrows, :].rearrange("(i p) d -> p i d", p=128), GT1.bl(G * 4, G * 4 + 4), [gtt])
                mk.dma("sp", gdt[:], GD1.ap[qrows, g * 256:(g + 1) * 256].rearrange("(i p) d -> p i d", p=128),
                       GD1.bl(G * 4, G * 4 + 4), [gdt])
                mk.dma("sp", ka[:, :, 0:64], C["c_keep"][qrows, :].rearrange("(i p) d -> p i d", p=128), (), [ka])
                mk.dma("sp", ka[:, :, 64:128], C["c_addm"][qrows, :].rearrange("(i p) d -> p i d", p=128), (), [ka])
                chunks = [0] if G < 4 else [0, 1]
                for r in range(4):
                    h = g * 4 + r
                    for ci, c in enumerate(chunks):
                        pS, pt = nxt()
                        last = ci == len(chunks) - 1
                        epi = _ft.partial(cmp_epi, r, h, gtt) if last else None
                        pipe.push(_ft.partial(cmp_A, pS, g, c, r, qrows),
                                  _ft.partial(cmp_B, pS, pt, g, c, qrows, ci == 0, last, epi))
                pipe.flush()
                for i in range(4):
                    mk.tt(IF[:], IMP[:, i, :], ka[:, i, 0:64], ALU.mult, [IMP, ka], [IF])
                    mk.tt(IF[:], IF[:], ka[:, i, 64:128], ALU.add, [IF, ka], [IF])
                    mk.op("dve", lambda e: e.max(out=M8[:, 0:8], in_=IF[:]), [IF], [M8])
                    mk.op("dve", lambda e: e.match_replace(out=IW[:], in_to_replace=M8[:, 0:8], in_values=IF[:], imm_value=-9.0),
                          [M8, IF], [IW])
                    mk.op("dve", lambda e: e.max(out=M8[:, 8:16], in_=IW[:]), [IW], [M8])
                    mk.ts(IW[:], IF[:], M8[:, 15:16], None, ALU.is_ge, None, [IF, M8], [IW])
                    mk.ts(NEG[:, 64:128], IW[:], -1.0, 30000.0, ALU.add, ALU.mult, [IW], [NEG])
                    mk.tr(psN[:, i * 128:(i + 1) * 128], NEG[:], identb[:], [NEG, identb], [psN])
                for r in range(4):
                    mk.cp(QTg[64:128, r, qrows], psN[64:128, 0:512], [psN], [QTg], e=("act" if r % 2 else "dve"))
                for r in range(4):
                    h = g * 4 + r
                    for j in range(4 * G + 4):
                        imin = max(0, j - 4 * G)
                        c0 = imin * 128
                        pS, pt = nxt()
                        pipe.push(_ft.partial(sel_A, pS, j, r, G, c0), _ft.partial(sel_B, pS, pt, j, G, c0))
                    j0 = max(0, 4 * G - 4)
                    for j in range(j0, 4 * G + 4):
                        rel = j - 4 * G
                        ilo = max(0, rel)
                        ihi = min(3, rel + 4)
                        c0, c1 = ilo * 128, (ihi + 1) * 128
                        pS, pt = nxt()
                        last = j == 4 * G + 3
                        yb = YDb[r % 2]
                        epi = _ft.partial(fin_epi, r, h, g, G, gtt, gdt, yb, qrows) if last else None
                        pipe.push(_ft.partial(win_A, pS, j, r, G, c0, c1),
                                  _ft.partial(win_B, pS, pt, j, G, c0, c1, rel, j == j0, last, epi))
                pipe.flush()
    if stop == 'p6':
        return
    out_proj(1, X1.ap, X1.blk, out_ap, None)


def host_consts():
    c = {}
    c["c_identb"] = np.eye(128, dtype=np.float32).astype(ml_dtypes.bfloat16)
    c["c_identf"] = np.eye(128, dtype=np.float32)
    k = np.arange(128)[:, None]
    q = np.arange(128)[None, :]
    c["c_tri"] = (k <= q).astype(np.float32).astype(ml_dtypes.bfloat16)
    c["c_ntri"] = (k > q).astype(np.float32).astype(ml_dtypes.bfloat16)

    def cs(pos, half):
        inv = (np.float32(THETA) ** (-np.arange(half, dtype=np.float32) / np.float32(half))).astype(np.float32)
        ang = pos.astype(np.float32)[:, None] * inv[None, :]
        return np.cos(ang).astype(np.float32), np.sin(ang).astype(np.float32)

    pos = np.arange(S)
    c["c_cos0"], c["c_sin0"] = cs(pos, 4)
    c["c_cos1"], c["c_sin1"] = cs(pos, 8)
    n_cmp = (S - 32) // 16 + 1
    cend = np.arange(256) * 16 + 31
    c["c_cosc"], c["c_sinc"] = cs(cend, 8)
    cm = (cend[:, None] <= pos[None, :]).astype(np.float32)
    cm[n_cmp:, :] = 0.0
    c["c_cmask"] = cm.astype(ml_dtypes.bfloat16)
    start = np.arange(256) * 16
    s0 = np.arange(64) * 64
    lo = np.maximum(start[:, None], s0[None, :])
    hi = np.minimum(start[:, None] + 32, s0[None, :] + 64)
    ov = (np.clip(hi - lo, 0, None) / 32.0).astype(np.float32)
    ov[n_cmp:, :] = 0.0
    c["c_ovl"] = ov.astype(ml_dtypes.bfloat16)
    cur = (pos // 64)[:, None]
    j = np.arange(64)[None, :]
    forced = (j == 0) | (j == cur) | (j == cur - 1)
    future = j > cur
    keep = (~forced & ~future).astype(np.float32)
    addm = np.where(forced, 10.0 + j / 64.0, np.where(future, -1.0 - j / 64.0, 0.0)).astype(np.float32)
    c["c_keep"] = keep
    c["c_addm"] = addm
    E = np.zeros((64, 32, 128), np.float32)
    for ch in range(32):
        for kk in range(128):
            E[2 * ch + kk // 64, ch, kk] = 1.0
    c["c_E"] = E.reshape(64, 32 * 128).astype(ml_dtypes.bfloat16)
    return c


_CACHE = {}


def kernel(**inputs):
    nl = 2
    if "nc" not in _CACHE:
        _CACHE["nc"] = build_program(nl)[0]
    nc = _CACHE["nc"]
    consts = host_consts()
    in_maps = []
    for c in range(8):
        b = c % 4
        m = {"x": np.ascontiguousarray(inputs["x"][b]), "mem": np.ascontiguousarray(inputs["mem"][b])}
        for k, v in inputs.items():
            if k not in ("x", "mem"):
                m[k] = np.ascontiguousarray(v)
        m.update(consts)
        in_maps.append(m)
    res = run_bass_kernel_spmd(nc, in_maps, core_ids=list(range(8)))
    out = np.stack([np.asarray(res.results[b]["out"]) for b in range(4)], axis=0)
    return out.astype(np.float32)
```

```python
import contextlib
import numpy as np
import ml_dtypes
import concourse.bass as bass
import concourse.mybir as mybir
from concourse.bass_utils import run_bass_kernel_spmd

F32 = mybir.dt.float32
BF16 = mybir.dt.bfloat16
AF = mybir.ActivationFunctionType
ALU = mybir.AluOpType
AX = mybir.AxisListType

S = 4096
D = 1024
NTB = S // 128
EPS = 1e-6
THETA = 500000.0
VW = 72


class Buf:
    def __init__(self, t, name):
        self.t = t
        self.name = name
        self.w = {}
        self.r = {}

    def __getitem__(self, k):
        return self.t[k]


class MK:
    EPOCH = 60000
    NSLOT = 16

    def __init__(self, nc):
        self.nc = nc
        self.es = contextlib.ExitStack()
        self.scopes = []
        self.eng = {"pe": nc.tensor, "act": nc.scalar, "dve": nc.vector, "pool": nc.gpsimd, "sp": nc.sync}
        self.cnt = {e: 0 for e in self.eng}
        self.dcnt = {"sp": 0, "pool": 0, "act": 0}
        self.sems = {}
        self.known = {e: {} for e in self.eng}
        self.latest = {}
        self.uid = 0

    def sem(self, key):
        if key not in self.sems:
            self.sems[key] = self.es.enter_context(self.nc.semaphore("s_%s_%d" % key))
        return self.sems[key]

    def _stack(self):
        return self.scopes[-1] if self.scopes else self.es

    def sb(self, name, shape, dtype):
        self.uid += 1
        t = self._stack().enter_context(self.nc.sbuf_tensor("%s_%d" % (name, self.uid), list(shape), dtype))
        return Buf(t, name)

    def ps(self, name, shape, dtype):
        self.uid += 1
        t = self._stack().enter_context(self.nc.psum_tensor("%s_%d" % (name, self.uid), list(shape), dtype))
        return Buf(t, name)

    @contextlib.contextmanager
    def scope(self):
        st = contextlib.ExitStack()
        self.scopes.append(st)
        try:
            yield
        finally:
            self.barrier()
            self.scopes.pop()
            st.close()

    def _wait(self, e, deps):
        for key, val in deps.items():
            if key[0] == "pe" and e == "pe":
                continue
            if self.known[e].get(key, 0) >= val:
                continue
            self.eng[e].wait_ge(self.sem(key), val)
            self.known[e][key] = val

    @staticmethod
    def _deps(rd, wr):
        d = {}
        for b in rd:
            for k, v in b.w.items():
                if d.get(k, 0) < v:
                    d[k] = v
        for b in wr:
            for k, v in b.w.items():
                if d.get(k, 0) < v:
                    d[k] = v
            for k, v in b.r.items():
                if d.get(k, 0) < v:
                    d[k] = v
        return d

    def _mark(self, rd, wr, key, val):
        self.latest[key] = val
        for b in wr:
            b.w = {key: val}
            b.r = {}
        for b in rd:
            if b in wr:
                continue
            if b.r.get(key, 0) < val:
                b.r[key] = val

    def op(self, e, fn, rd=(), wr=()):
        self._wait(e, self._deps(rd, wr))
        n = self.cnt[e]
        key = (e, n // self.EPOCH)
        val = n % self.EPOCH + 1
        ins = fn(self.eng[e])
        ins.then_inc(self.sem(key), 1)
        self.cnt[e] = n + 1
        self._mark(rd, wr, key, val)

    def dma(self, q, out, in_, rd=(), wr=(), **kw):
        self._wait(q, self._deps(rd, wr))
        n = self.dcnt[q]
        slot = n % self.NSLOT
        key = ("d" + q, slot)
        val = 16 * (n // self.NSLOT + 1)
        if n >= self.NSLOT:
            self._wait(q, {key: val - 16})
        self.eng[q].dma_start(out=out, in_=in_, **kw).then_inc(self.sem(key), 16)
        self.dcnt[q] = n + 1
        self._mark(rd, wr, key, val)

    def barrier(self):
        lt = dict(self.latest)
        for e in self.eng:
            self._wait(e, lt)

    def mm(self, out, lhsT, rhs, start, stop, rd, wr):
        self.op("pe", lambda e: e.matmul(out, lhsT=lhsT, rhs=rhs, start=start, stop=stop, skip_group_check=True), rd, wr)

    def tr(self, out, in_, ident, rd, wr):
        self.op("pe", lambda e: e.transpose(out, in_, ident), rd, wr)

    def act(self, out, in_, func, rd, wr, **kw):
        self.op("act", lambda e: e.activation(out=out, in_=in_, func=func, **kw), rd, wr)

    def tt(self, out, in0, in1, op, rd, wr, e="dve"):
        self.op(e, lambda g: g.tensor_tensor(out=out, in0=in0, in1=in1, op=op), rd, wr)

    def ts(self, out, in0, s1, s2, op0, op1, rd, wr, e="dve"):
        if op1 is None:
            self.op(e, lambda g: g.tensor_scalar(out=out, in0=in0, scalar1=s1, scalar2=None, op0=op0), rd, wr)
        else:
            self.op(e, lambda g: g.tensor_scalar(out=out, in0=in0, scalar1=s1, scalar2=s2, op0=op0, op1=op1), rd, wr)

    def stt(self, out, in0, scalar, in1, op0, op1, rd, wr, e="dve"):
        self.op(e, lambda g: g.scalar_tensor_tensor(out=out, in0=in0, scalar=scalar, in1=in1, op0=op0, op1=op1), rd, wr)

    def red(self, out, in_, rd, wr, op=ALU.add):
        self.op("dve", lambda g: g.tensor_reduce(out=out, in_=in_, axis=AX.X, op=op), rd, wr)

    def recip(self, out, in_, rd, wr):
        self.op("dve", lambda g: g.reciprocal(out=out, in_=in_), rd, wr)

    def cp(self, out, in_, rd, wr, e="dve"):
        if e == "act":
            self.op(e, lambda g: g.activation(out=out, in_=in_, func=AF.Copy), rd, wr)
        else:
            self.op(e, lambda g: g.tensor_copy(out=out, in_=in_), rd, wr)

    def memset(self, ap, val, wr, e="dve"):
        self.op(e, lambda g: g.memset(ap, val), (), wr)


class Pipe:
    def __init__(self, depth):
        self.depth = depth
        self.q = []

    def push(self, A, B):
        A()
        self.q.append(B)
        if len(self.q) > self.depth:
            self.q.pop(0)()

    def flush(self):
        while self.q:
            self.q.pop(0)()


class DS:
    def __init__(self, nc, name, shape, dtype, nblk=NTB):
        self.ap = nc.dram_tensor(name, list(shape), dtype, kind="Internal").ap()
        self.blk = [Buf(None, "%s_%d" % (name, i)) for i in range(nblk)]

    def bl(self, lo=0, hi=None):
        return self.blk[lo:(len(self.blk) if hi is None else hi)]


def bc(ap, shape):
    return ap.unsqueeze(2).to_broadcast(list(shape))


def bcm(ap, shape):
    return ap.unsqueeze(1).to_broadcast(list(shape))


def build_program(nlayers=2, stop=None):
    nc = bass.Bass("TRN2", target_bir_lowering=False)
    mk = MK(nc)

    def din(name, shape, dtype=F32):
        return nc.dram_tensor(name, list(shape), dtype, kind="ExternalInput").ap()

    x_in = din("x", [S, D])
    mem_in = din("mem", [256, D])
    P = {}
    specs = {
        "mem_norm": [D], "l0_norm": [D], "l0_w_in": [D, 4096], "l0_a_vnorm": [512], "l0_a_ws": [4, 128, 128],
        "l0_a_bs": [4, 128], "l0_b_qnorm": [32], "l0_b_knorm": [32], "l0_b_lq1": [32], "l0_b_lk1": [32],
        "l0_b_lq2": [32], "l0_b_lk2": [32], "l0_b_subln": [64], "l0_m_wkv": [D, 512], "l0_m_qnorm": [64],
        "l0_m_knorm": [64], "l0_w_out": [1280, D],
        "l1_norm": [D], "l1_w_in": [D, 3864], "l1_c_conv_w": [31, 512], "l1_c_conv_b": [512], "l1_c_norm": [512],
        "l1_d_qnorm": [64], "l1_d_knorm": [64], "l1_d_cmp_pos_k": [32, 64], "l1_d_cmp_w1_k": [2048, 128],
        "l1_d_cmp_w2_k": [128, 64], "l1_d_cmp_pos_v": [32, 64], "l1_d_cmp_w1_v": [2048, 128],
        "l1_d_cmp_w2_v": [128, 64], "l1_m_wkv": [D, 512], "l1_m_qnorm": [64], "l1_m_knorm": [64],
        "l1_w_out": [1280, D],
    }
    for k, shp in specs.items():
        P[k] = din(k, shp)
    C = {}
    cspecs = {
        "c_identb": ([128, 128], BF16), "c_identf": ([128, 128], F32), "c_tri": ([128, 128], BF16),
        "c_ntri": ([128, 128], BF16), "c_cos0": ([S, 4], F32), "c_sin0": ([S, 4], F32),
        "c_cos1": ([S, 8], F32), "c_sin1": ([S, 8], F32), "c_cosc": ([256, 8], F32), "c_sinc": ([256, 8], F32),
        "c_cmask": ([256, S], BF16), "c_ovl": ([256, 64], BF16), "c_keep": ([S, 64], F32),
        "c_addm": ([S, 64], F32), "c_E": ([64, 32 * 128], BF16),
    }
    for k, (shp, dt) in cspecs.items():
        C[k] = din(k, shp, dt)
    out_ap = nc.dram_tensor("out", [S, D], F32, kind="ExternalOutput").ap()

    X1 = DS(nc, "X1", [S, D], F32)
    Y = [DS(nc, "Y0", [S, 1280], BF16), DS(nc, "Y1", [S, 1280], BF16)]
    QT0 = DS(nc, "QT0", [512, S], BF16)
    KT0 = DS(nc, "KT0", [512, S], BF16)
    V0e = DS(nc, "V0e", [S, 8 * VW], BF16)
    GB0 = DS(nc, "GB0", [S, 512], F32)
    QT1 = DS(nc, "QT1", [64, 8, S], BF16)
    CRT = DS(nc, "CRT", [64, 4, S], BF16)
    KST = DS(nc, "KST", [64, 2, S], BF16)
    KWT = DS(nc, "KWT", [64, 2, S], BF16)
    VSe = DS(nc, "VSe", [S, 2 * VW], BF16)
    VWe = DS(nc, "VWe", [S, 2 * VW], BF16)
    GT1 = DS(nc, "GT1", [S, 24], F32)
    GD1 = DS(nc, "GD1", [S, 512], F32)
    GC1 = DS(nc, "GC1", [S, 512], F32)

    identb = mk.sb("identb", [128, 128], BF16)
    identf = mk.sb("identf", [128, 128], F32)
    tri = mk.sb("tri", [128, 128], BF16)
    ntri = mk.sb("ntri", [128, 128], BF16)
    epsT = mk.sb("epsT", [128, 1], F32)
    mk.dma("sp", identb[:], C["c_identb"], (), [identb])
    mk.dma("sp", identf[:], C["c_identf"], (), [identf])
    mk.dma("sp", tri[:], C["c_tri"], (), [tri])
    mk.dma("sp", ntri[:], C["c_ntri"], (), [ntri])
    mk.memset(epsT[:], EPS, [epsT])

    def load_bcast(name, src, n):
        b = mk.sb(name, [128, n], F32)
        mk.dma("sp", b[:], src.partition_broadcast(128), (), [b])
        return b

    KmT = [mk.sb("KmT%d" % L, [128, 4, 256], BF16) for L in range(2)]
    Vm = [mk.sb("Vm%d" % L, [128, 2, 4 * VW], BF16) for L in range(2)]

    def rstd_inplace(ss_buf, ss_ap, inv_d, ln=False):
        if ln:
            mk.act(ss_ap, ss_ap, AF.Ln, [ss_buf, epsT], [ss_buf], bias=epsT[:, 0:1], scale=inv_d)
            mk.act(ss_ap, ss_ap, AF.Exp, [ss_buf], [ss_buf], scale=-0.5)
            return
        mk.act(ss_ap, ss_ap, AF.Sqrt, [ss_buf, epsT], [ss_buf], bias=epsT[:, 0:1], scale=inv_d)
        mk.recip(ss_ap, ss_ap, [ss_buf], [ss_buf])

    def headnorm(T, Tap, H, hd, G, SQ, SS, gfull=False, ln=True):
        n = H * hd
        T3 = Tap.rearrange("p (h d) -> p h d", d=hd)
        mk.tt(SQ[:, 0:n], Tap, Tap, ALU.mult, [T], [SQ])
        mk.red(SS[:, 0:H], SQ[:, 0:n].rearrange("p (h d) -> p h d", d=hd), [SQ], [SS])
        rstd_inplace(SS, SS[:, 0:H], 1.0 / hd, ln=ln)
        mk.tt(T3, T3, bc(SS[:, 0:H], [128, H, hd]), ALU.mult, [T, SS], [T])
        if gfull:
            mk.tt(Tap, Tap, G[:, 0:n], ALU.mult, [T, G], [T])
        else:
            mk.tt(T3, T3, bcm(G[:, 0:hd], [128, H, hd]), ALU.mult, [T, G], [T])

    def rope(T, Tap, H, hd, half, CS, csap_c, csap_s, R):
        T3 = Tap.rearrange("p (h d) -> p h d", d=hd)
        x1 = T3[:, :, 0:half]
        x2 = T3[:, :, half:2 * half]
        R4 = R[:, 0:H * 4 * half].rearrange("p (h f d) -> p h f d", f=4, d=half)
        cb = bcm(csap_c, [128, H, half])
        sb_ = bcm(csap_s, [128, H, half])
        mk.tt(R4[:, :, 0, :], x1, cb, ALU.mult, [T, CS], [R])
        mk.tt(R4[:, :, 1, :], x2, sb_, ALU.mult, [T, CS], [R])
        mk.tt(R4[:, :, 2, :], x1, sb_, ALU.mult, [T, CS], [R])
        mk.tt(R4[:, :, 3, :], x2, cb, ALU.mult, [T, CS], [R])
        mk.tt(x1, R4[:, :, 0, :], R4[:, :, 1, :], ALU.subtract, [R], [T])
        mk.tt(x2, R4[:, :, 2, :], R4[:, :, 3, :], ALU.add, [R], [T])

    with mk.scope():
        Gmem = load_bcast("Gmem", P["mem_norm"], D)
        memt = mk.sb("memt", [128, 2, D], F32)
        mk.dma("sp", memt[:], mem_in.rearrange("(c p) d -> p c d", p=128), (), [memt])
        sq = mk.sb("sq0", [128, D], F32)
        ss = mk.sb("ss0", [128, 8], F32)
        mnb = mk.sb("mnb", [128, D], BF16)
        mnT = mk.sb("mnT", [128, 8, 256], BF16)
        psT = mk.ps("psT0", [128, 1024], BF16)
        psKV = mk.ps("psKV", [128, 512], F32)
        psK2 = mk.ps("psK2", [128, 1024], BF16)
        for mc in range(2):
            mk.act(sq[:], memt[:, mc, :], AF.Square, [memt], [sq])
            mk.red(ss[:, 0:1], sq[:], [sq], [ss])
            rstd_inplace(ss, ss[:, 0:1], 1.0 / D)
            mk.stt(mnb[:], memt[:, mc, :], ss[:, 0:1], Gmem[:], ALU.mult, ALU.mult, [memt, ss, Gmem], [mnb])
            for k in range(8):
                mk.tr(psT[:, k * 128:(k + 1) * 128], mnb[:, k * 128:(k + 1) * 128], identb[:], [mnb, identb], [psT])
            mk.cp(mnT[:, :, mc * 128:(mc + 1) * 128], psT[:].rearrange("p (k t) -> p k t", t=128), [psT], [mnT])
        wst = mk.sb("wkvst", [128, 8, 512], F32)
        wb = mk.sb("wkvb", [128, 8, 512], BF16)
        kv = mk.sb("kvsb", [128, 512], F32)
        knb = mk.sb("knb", [128, 256], BF16)
        for L in range(nlayers):
            pre = "l%d_" % L
            Gk = load_bcast("Gmk%d" % L, P[pre + "m_knorm"], 64)
            mk.dma("sp", wst[:], P[pre + "m_wkv"].rearrange("(k p) c -> p k c", p=128), (), [wst])
            mk.cp(wb[:], wst[:], [wst], [wb])
            mk.memset(Vm[L][:], 1.0, [Vm[L]])
            mk.memset(KmT[L][:], 0.0, [KmT[L]])
            for mc in range(2):
                for k in range(8):
                    mk.mm(psKV[:], mnT[:, k, mc * 128:(mc + 1) * 128], wb[:, k, :], k == 0, k == 7, [mnT, wb], [psKV])
                mk.act(kv[:], psKV[:], AF.Copy, [psKV], [kv])
                headnorm(kv, kv[:, 0:256], 4, 64, Gk, sq, ss)
                mk.cp(knb[:], kv[:, 0:256], [kv], [knb])
                for h in range(4):
                    mk.tr(psK2[0:64, h * 128:(h + 1) * 128], knb[:, h * 64:(h + 1) * 64], identb[:], [knb, identb], [psK2])
                mk.cp(KmT[L][0:64, :, mc * 128:(mc + 1) * 128], psK2[0:64, 0:512].rearrange("p (r t) -> p r t", t=128), [psK2], [KmT[L]])
                mk.cp(Vm[L][:, mc, :].rearrange("p (h d) -> p h d", d=VW)[:, :, 0:64],
                      kv[:, 256:512].rearrange("p (h d) -> p h d", d=64), [kv], [Vm[L]])

    if stop == 'p0':
        mk.barrier()
        return nc, mk
    def load_weights_bf16(Wb, w_ap, ncols, g_ap, gname, segs=None, st=None, dve_only=False):
        gT = mk.sb(gname, [128, 8], F32)
        mk.dma("sp", gT[:], g_ap.rearrange("(k p) -> p k", p=128), (), [gT], allow_slow_non_contiguous=True)
        if segs is None:
            segs = [(0, 0, ncols)]
        pieces = []
        for (d0, s0, n) in segs:
            o = 0
            while o < n:
                m = min(2048, n - o)
                pieces.append((d0 + o, s0 + o, m))
                o += m
        def body(st_):
            it = 0
            for k in range(8):
                for (d0, s0, m) in pieces:
                    s_ = st_[it % 2]
                    it += 1
                    mk.dma("sp", s_[:, 0:m], w_ap[k * 128:(k + 1) * 128, s0:s0 + m], (), [s_])
                    if it % 2 or dve_only:
                        mk.ts(Wb[:, k, d0:d0 + m], s_[:, 0:m], gT[:, k:k + 1], None, ALU.mult, None, [s_, gT], [Wb])
                    else:
                        mk.act(Wb[:, k, d0:d0 + m], s_[:, 0:m], AF.Copy, [s_, gT], [Wb], scale=gT[:, k:k + 1])
        if st is not None:
            body(st)
        else:
            with mk.scope():
                body([mk.sb("wst%d" % i, [128, 2048], F32) for i in range(2)])

    def load_wo(Wo, L, st, dve_only=False):
        for k in range(10):
            s_ = st[k % 2]
            mk.dma("sp", s_[:, 0:D], P["l%d_w_out" % L][k * 128:(k + 1) * 128, :], (), [s_])
            mk.cp(Wo[:, k, :], s_[:, 0:D], [s_], [Wo], e=("act" if (k % 2 and not dve_only) else "dve"))

    def mem_attn(L, MQ, mqb, mqT, psQ, psS, psO, PTm, OM, RL, GM, GMb, Ym_ap, Ym_buf):
        mk.cp(mqb[:], MQ[:, 0:256], [MQ], [mqb])
        for h in range(4):
            mk.tr(psQ[0:64, h * 128:(h + 1) * 128], mqb[:, h * 64:(h + 1) * 64], identb[:], [mqb, identb], [psQ])
        mk.cp(mqT[0:64, :, :], psQ[0:64, 0:512].rearrange("p (r t) -> p r t", t=128), [psQ], [mqT])
        for pr in range(2):
            for hh in range(2):
                h = pr * 2 + hh
                for mc in range(2):
                    blk = hh * 2 + mc
                    mk.mm(psS[:, blk * 128:(blk + 1) * 128], KmT[L][:, h, mc * 128:(mc + 1) * 128],
                          mqT[:, h, :], True, True, [KmT[L], mqT], [psS])
            mk.act(PTm[:], psS[:], AF.Exp, [psS], [PTm], scale=0.125)
            for hh in range(2):
                h = pr * 2 + hh
                for mc in range(2):
                    blk = hh * 2 + mc
                    mk.mm(psO[:, h * VW:(h + 1) * VW], PTm[:, blk * 128:(blk + 1) * 128],
                          Vm[L][:, mc, h * VW:(h + 1) * VW], mc == 0, mc == 1, [PTm, Vm[L]], [psO])
        o3 = psO[:, 0:4 * VW].rearrange("p (h d) -> p h d", d=VW)
        mk.recip(RL[:, 0:4], o3[:, :, 64], [psO], [RL])
        mk.tt(OM[:, 0:256].rearrange("p (h d) -> p h d", d=64), o3[:, :, 0:64], bc(RL[:, 0:4], [128, 4, 64]),
              ALU.mult, [psO, RL], [OM])
        mk.tt(Ym_ap, OM[:, 0:256], GM, ALU.mult, [OM, GMb], [Ym_buf])

    def mem_attn_parts(L, mqb, mqT, psQ, psS2, psO, PTm2, OM, RL, GMb, Ym_ap, Ym_buf, tail=None):
        def c0():
            for h in range(4):
                mk.tr(psQ[0:64, h * 128:(h + 1) * 128], mqb[:, h * 64:(h + 1) * 64], identb[:], [mqb, identb], [psQ])
            mk.cp(mqT[0:64, :, :], psQ[0:64, 0:512].rearrange("p (r t) -> p r t", t=128), [psQ], [mqT], e="act")
            for pr in range(2):
                for hh in range(2):
                    h = pr * 2 + hh
                    for mc in range(2):
                        blk = hh * 2 + mc
                        mk.mm(psS2[pr][:, blk * 128:(blk + 1) * 128], KmT[L][:, h, mc * 128:(mc + 1) * 128],
                              mqT[:, h, :], True, True, [KmT[L], mqT], [psS2[pr]])
                mk.act(PTm2[pr][:], psS2[pr][:], AF.Exp, [psS2[pr]], [PTm2[pr]], scale=0.125)

        def c1():
            for pr in range(2):
                for hh in range(2):
                    h = pr * 2 + hh
                    for mc in range(2):
                        blk = hh * 2 + mc
                        mk.mm(psO[:, h * VW:(h + 1) * VW], PTm2[pr][:, blk * 128:(blk + 1) * 128],
                              Vm[L][:, mc, h * VW:(h + 1) * VW], mc == 0, mc == 1, [PTm2[pr], Vm[L]], [psO])

        def c2():
            o3 = psO[:, 0:4 * VW].rearrange("p (h d) -> p h d", d=VW)
            mk.recip(RL[:, 0:4], o3[:, :, 64], [psO], [RL])
            mk.tt(OM[:, 0:256].rearrange("p (h d) -> p h d", d=64), o3[:, :, 0:64], bc(RL[:, 0:4], [128, 4, 64]),
                  ALU.mult, [psO, RL], [OM])
            mk.tt(Ym_ap, OM[:, 0:256], GMb[:], ALU.mult, [OM, GMb], [Ym_buf])
            if tail is not None:
                tail()
        return [c0, c1, c2]

    def xnorm_a(xt, sq, ss, xn):
        mk.act(sq[:], xt[:], AF.Square, [xt], [sq])
        mk.red(ss[:, 0:1], sq[:], [sq], [ss])
        rstd_inplace(ss, ss[:, 0:1], 1.0 / D, ln=True)
        mk.act(xn[:], xt[:], AF.Copy, [xt, ss], [xn], scale=ss[:, 0:1])

    def xnorm_b(xn, psT, hT):
        for k in range(8):
            mk.tr(psT[:, k * 128:(k + 1) * 128], xn[:, k * 128:(k + 1) * 128], identb[:], [xn, identb], [psT])
        mk.cp(hT[:], psT[:].rearrange("p (k t) -> p k t", t=128), [psT], [hT])

    def xnorm_to_hT(src_ap, src_deps, xt, sq, ss, xn, psT, hT):
        xnorm_a(xt, sq, ss, xn)
        xnorm_b(xn, psT, hT)

    def out_proj(L, Xres_ap, Xres_blks, dst_ap, dst_blks, Wo_pre=None):
        with mk.scope():
            if Wo_pre is not None:
                Wo = Wo_pre
            else:
                Wo = mk.sb("Wo", [128, 10, D], BF16)
                with mk.scope():
                    load_wo(Wo, L, [mk.sb("wost%d" % i, [128, D], F32) for i in range(2)])
            yt = [mk.sb("yt%d" % i, [128, 1280], BF16) for i in range(3)]
            xr = [mk.sb("xr%d" % i, [128, D], F32) for i in range(3)]
            xo = [mk.sb("xo%d" % i, [128, D], F32) for i in range(2)]
            yTs = [mk.sb("yT%d" % i, [128, 10, 128], BF16) for i in range(2)]
            psYs = [[mk.ps("psY%d_%d" % (j, i), [128, 1024], BF16) for i in range(2)] for j in range(2)]
            psP = [mk.ps("psP%d" % i, [128, 512], F32) for i in range(4)]

            def loads(tb):
                mk.dma("sp", yt[tb % 3][:], Y[L].ap[tb * 128:(tb + 1) * 128, :], [Y[L].blk[tb]], [yt[tb % 3]])
                mk.dma("sp", xr[tb % 3][:], Xres_ap[tb * 128:(tb + 1) * 128, :],
                       [Xres_blks[tb]] if Xres_blks else [], [xr[tb % 3]])

            def front(tb):
                y_ = yt[tb % 3]
                psY = psYs[tb % 2]
                yT = yTs[tb % 2]
                for k in range(10):
                    pT = psY[k // 8]
                    kk = k % 8
                    mk.tr(pT[:, kk * 128:(kk + 1) * 128], y_[:, k * 128:(k + 1) * 128], identb[:], [y_, identb], [pT])
                mk.cp(yT[:, 0:8, :], psY[0][:].rearrange("p (k t) -> p k t", t=128), [psY[0]], [yT])
                mk.cp(yT[:, 8:10, :], psY[1][:, 0:256].rearrange("p (k t) -> p k t", t=128), [psY[1]], [yT], e="act")

            loads(0)
            loads(1)
            front(0)
            pc = 0
            for tb in range(NTB):
                if tb + 2 < NTB:
                    loads(tb + 2)
                if tb + 1 < NTB:
                    front(tb + 1)
                yT = yTs[tb % 2]
                for cg in range(2):
                    pp = psP[pc % 4]
                    pc += 1
                    for k in range(10):
                        mk.mm(pp[:], yT[:, k, :], Wo[:, k, cg * 512:(cg + 1) * 512], k == 0, k == 9, [yT, Wo], [pp])
                    mk.tt(xo[tb % 2][:, cg * 512:(cg + 1) * 512], pp[:], xr[tb % 3][:, cg * 512:(cg + 1) * 512],
                          ALU.add, [pp, xr[tb % 3]], [xo[tb % 2]])
                mk.dma("sp", dst_ap[tb * 128:(tb + 1) * 128, :], xo[tb % 2][:], [xo[tb % 2]],
                       [dst_blks[tb]] if dst_blks else [])

    lam_init0 = 0.8 - 0.6 * float(np.exp(-0.3 * 1))
    LAMN = mk.sb("LAMN", [128, 1], F32)
    with mk.scope():
        Wb = mk.sb("Wb0", [128, 8, 4096], BF16)
        load_weights_bf16(Wb, P["l0_w_in"], 4096, P["l0_norm"], "gT0")
        Gav = load_bcast("Gav", P["l0_a_vnorm"], 512)
        Gq = load_bcast("Gq0", P["l0_b_qnorm"], 32)
        Gk = load_bcast("Gk0", P["l0_b_knorm"], 32)
        Gmq = load_bcast("Gmq0", P["l0_m_qnorm"], 64)
        BsT = mk.sb("BsT", [128, 4], F32)
        mk.dma("sp", BsT[:], P["l0_a_bs"].rearrange("g t -> t g"), (), [BsT], allow_slow_non_contiguous=True)
        lq = [load_bcast("lq%d" % i, P["l0_b_" + n], 32) for i, n in enumerate(["lq1", "lk1", "lq2", "lk2"])]
        ltmp = mk.sb("ltmp", [128, 32], F32)
        lsum = mk.sb("lsum", [128, 2], F32)
        for i in range(2):
            mk.tt(ltmp[:], lq[2 * i][:], lq[2 * i + 1][:], ALU.mult, [lq[2 * i], lq[2 * i + 1]], [ltmp])
            mk.red(lsum[:, i:i + 1], ltmp[:], [ltmp], [lsum])
        mk.act(lsum[:], lsum[:], AF.Exp, [lsum], [lsum])
        mk.tt(LAMN[:], lsum[:, 1:2], lsum[:, 0:1], ALU.subtract, [lsum], [LAMN])
        mk.ts(LAMN[:], LAMN[:], -lam_init0, None, ALU.add, None, [LAMN], [LAMN])
        WsT = mk.sb("WsT", [128, 4, 128], BF16)
        with mk.scope():
            wsl = mk.sb("wsl", [128, 4, 128], F32)
            mk.dma("sp", wsl[:], P["l0_a_ws"].rearrange("g t s -> t g s"), (), [wsl])
            psW = mk.ps("psW", [128, 512], F32)
            for g in range(4):
                mk.tr(psW[:, g * 128:(g + 1) * 128], wsl[:, g, :], identf[:], [wsl, identf], [psW])
            mk.tt(WsT[:], psW[:].rearrange("p (g t) -> p g t", t=128), bcm(tri[:], [128, 4, 128]), ALU.mult,
                  [psW, tri], [WsT])

        if stop == 'p1a':
            mk.barrier()
            return nc, mk
        xt = [mk.sb("xt%d" % i, [128, D], F32) for i in range(3)]
        cs = [mk.sb("cs%d" % i, [128, 8], F32) for i in range(3)]
        sq = mk.sb("sq", [128, D], F32)
        ss = mk.sb("ss", [128, 16], F32)
        sqx = mk.sb("sqx", [128, D], F32)
        ssx = mk.sb("ssx", [128, 8], F32)
        xns = [mk.sb("xn%d" % i, [128, D], BF16) for i in range(2)]
        hTs = [mk.sb("hT%d" % i, [128, 8, 128], BF16) for i in range(2)]
        Us = [mk.sb("U%d" % i, [128, 512], F32) for i in range(2)]
        Vv = mk.sb("Vv", [128, 512], F32)
        Vbs = [mk.sb("Vb%d" % i, [128, 512], BF16) for i in range(2)]
        Gas = [mk.sb("Ga%d" % i, [128, 512], F32) for i in range(2)]
        TA = mk.sb("TA", [128, 512], F32)
        T = mk.sb("T", [128, 512], F32)
        Rr = mk.sb("Rr", [128, 512], F32)
        TbQs = [mk.sb("TbQ%d" % i, [128, 512], BF16) for i in range(2)]
        TbKs = [mk.sb("TbK%d" % i, [128, 512], BF16) for i in range(2)]
        QKs = [mk.sb("QKs%d" % i, [128, 4, 128], BF16) for i in range(2)]
        VS = [mk.sb("VS%d" % i, [128, 8 * VW], BF16) for i in range(2)]
        GBs = [mk.sb("GBs%d" % i, [128, 512], F32) for i in range(2)]
        Ys = [mk.sb("Ys%d" % i, [128, 768], BF16) for i in range(3)]
        MQ = mk.sb("MQ", [128, 512], F32)
        GMs = [mk.sb("GM%d" % i, [128, 256], F32) for i in range(2)]
        mqbs = [mk.sb("mqb%d" % i, [128, 256], BF16) for i in range(2)]
        mqT = mk.sb("mqT", [128, 4, 128], BF16)
        mk.memset(mqT[:], 0.0, [mqT])
        PTm2 = [mk.sb("PTm%d" % i, [128, 512], BF16) for i in range(2)]
        OM = mk.sb("OM", [128, 256], F32)
        RL = mk.sb("RL", [128, 8], F32)
        psT = mk.ps("psT", [128, 1024], BF16)
        psA = [mk.ps("psA%d" % i, [128, 512], F32) for i in range(2)]
        psZ = mk.ps("psZ", [128, 512], F32)
        psQ = mk.ps("psQ", [128, 1024], BF16)
        psS2 = [mk.ps("psS%d" % i, [128, 512], F32) for i in range(2)]
        psO = mk.ps("psO", [128, 512], F32)
        for i in range(2):
            mk.memset(VS[i][:], 1.0, [VS[i]])

        def loads0(tb):
            mk.dma("sp", xt[tb % 3][:], x_in[tb * 128:(tb + 1) * 128, :], (), [xt[tb % 3]])
            mk.dma("sp", cs[tb % 3][:, 0:4], C["c_cos0"][tb * 128:(tb + 1) * 128, :], (), [cs[tb % 3]])
            mk.dma("sp", cs[tb % 3][:, 4:8], C["c_sin0"][tb * 128:(tb + 1) * 128, :], (), [cs[tb % 3]])

        def make_B0(tb):
            par = tb % 2
            rows = slice(tb * 128, (tb + 1) * 128)
            ys = Ys[tb % 3]
            U, Vb, Ga, TbQ, TbK = Us[par], Vbs[par], Gas[par], TbQs[par], TbKs[par]

            def bz():
                for g in range(4):
                    mk.mm(psZ[:, g * 128:(g + 1) * 128], WsT[:, g, :], Vb[:, g * 128:(g + 1) * 128], True, True,
                          [WsT, Vb], [psZ])
                for g in range(4):
                    mk.stt(TA[:, g * 128:(g + 1) * 128], psZ[:, g * 128:(g + 1) * 128], BsT[:, g:g + 1],
                           U[:, g * 128:(g + 1) * 128], ALU.add, ALU.mult, [psZ, BsT, U], [TA])
                mk.tt(ys[:, 0:512], TA[:], Ga[:], ALU.mult, [TA, Ga], [ys])
                mk.dma("sp", Y[0].ap[rows, 0:512], ys[:, 0:512], [ys], [Y[0].blk[tb]])

            def bqk(Tb_, qs, dst):
                def f():
                    for j in range(4):
                        mk.tr(psQ[:, j * 128:(j + 1) * 128], Tb_[:, j * 128:(j + 1) * 128], identb[:], [Tb_, identb], [psQ])
                    mk.cp(qs[:], psQ[:, 0:512].rearrange("p (j t) -> p j t", t=128), [psQ], [qs], e="act")
                    mk.dma("sp", dst.ap.rearrange("(j p) t -> p j t", p=128)[:, :, rows], qs[:], [qs], [dst.blk[tb]])
                return f

            def tail():
                mk.dma("sp", Y[0].ap[rows, 1024:1280], ys[:, 512:768], [ys], [Y[0].blk[tb]])

            mparts = mem_attn_parts(0, mqbs[par], mqT, psQ, psS2, psO, PTm2, OM, RL, GMs[par], ys[:, 512:768], ys, tail)
            return [bz, bqk(TbQ, QKs[0], QT0), bqk(TbK, QKs[1], KT0)] + mparts

        import os as _os
        _ntb = int(_os.environ.get('P1_NTB', NTB))
        loads0(0)
        if _ntb > 1:
            loads0(1)
        xnorm_a(xt[0], sqx, ssx, xns[0])
        xnorm_b(xns[0], psT, hTs[0])
        pi = 0
        pending = []
        for tb in range(_ntb):
            if tb + 2 < _ntb:
                loads0(tb + 2)
            if tb + 1 < _ntb:
                xnorm_a(xt[(tb + 1) % 3], sqx, ssx, xns[(tb + 1) % 2])
            par = tb % 2
            cs_ = cs[tb % 3]
            hT = hTs[par]
            rows = slice(tb * 128, (tb + 1) * 128)
            for cg in range(8):
                pp = psA[pi % 2]
                pi += 1
                for k in range(8):
                    mk.mm(pp[:], hT[:, k, :], Wb[:, k, cg * 512:(cg + 1) * 512], k == 0, k == 7, [hT, Wb], [pp])
                if cg == 0:
                    mk.act(Us[par][:], pp[:], AF.Gelu_apprx_tanh, [pp], [Us[par]])
                elif cg == 1:
                    mk.act(Vv[:], pp[:], AF.Gelu_apprx_tanh, [pp], [Vv])
                    headnorm(Vv, Vv[:], 4, 128, Gav, sq, ss, gfull=True)
                    mk.cp(Vbs[par][:], Vv[:], [Vv], [Vbs[par]])
                elif cg == 2:
                    mk.act(Gas[par][:], pp[:], AF.Silu, [pp], [Gas[par]])
                elif cg in (3, 4):
                    mk.act(T[:], pp[:], AF.Copy, [pp], [T])
                    headnorm(T, T[:], 16, 32, Gq if cg == 3 else Gk, sq, ss)
                    rope(T, T[:], 16, 32, 4, cs_, cs_[:, 0:4], cs_[:, 4:8], Rr)
                    Tb_ = TbQs[par] if cg == 3 else TbKs[par]
                    mk.cp(Tb_[:], T[:], [T], [Tb_])
                elif cg == 5:
                    vs = VS[par]
                    mk.act(vs[:].rearrange("p (h d) -> p h d", d=VW)[:, :, 0:64],
                           pp[:].rearrange("p (h d) -> p h d", d=64), AF.Copy, [pp], [vs])
                    mk.dma("sp", V0e.ap[rows, :], vs[:], [vs], [V0e.blk[tb]])
                elif cg == 6:
                    gb = GBs[par]
                    mk.act(gb[:], pp[:], AF.Silu, [pp], [gb])
                    mk.dma("sp", GB0.ap[rows, :], gb[:], [gb], [GB0.blk[tb]])
                else:
                    mk.act(MQ[:, 0:256], pp[:, 0:256], AF.Copy, [pp], [MQ])
                    mk.act(GMs[par][:], pp[:, 256:512], AF.Silu, [pp], [GMs[par]])
                    headnorm(MQ, MQ[:, 0:256], 4, 64, Gmq, sq, ss)
                    mk.cp(mqbs[par][:], MQ[:, 0:256], [MQ], [mqbs[par]])
                if pending:
                    pending.pop(0)()
            while pending:
                pending.pop(0)()
            if tb + 1 < _ntb:
                xnorm_b(xns[(tb + 1) % 2], psT, hTs[(tb + 1) % 2])
            pending = make_B0(tb)
        while pending:
            pending.pop(0)()

    if stop == 'p1':
        mk.barrier()
        return nc, mk
    Wb1 = mk.sb("Wb1", [128, 8, 3864], BF16) if nlayers == 2 else None
    sc23 = mk.scope()
    sc23.__enter__()
    Wo0 = mk.sb("Wo0", [128, 10, D], BF16)
    with mk.scope():
        Gsub = load_bcast("Gsub", P["l0_b_subln"], 64)
        mk.ts(Gsub[:], Gsub[:], 1.0 - lam_init0, None, ALU.mult, None, [Gsub], [Gsub])
        Ve = mk.sb("Ve", [128, NTB, 8 * VW], BF16)
        Vsrc = V0e.ap.rearrange("(c p) f -> p c f", p=128)
        for c8 in range(8):
            mk.dma("sp", Ve[:, c8 * 4:(c8 + 1) * 4, :], Vsrc[:, c8 * 4:(c8 + 1) * 4, :], V0e.bl(c8 * 4, c8 * 4 + 4), [Ve])
        QTh = [mk.sb("QTh%d" % i, [128, S], BF16) for i in range(2)]
        KTh = [[mk.sb("KTh%d_%d" % (i, m), [128, S], BF16) for m in range(2)] for i in range(2)]
        for i in range(2):
            mk.memset(QTh[i][:], 0.0, [QTh[i]])
            for m in range(2):
                mk.memset(KTh[i][m][:], 0.0, [KTh[i][m]])
        PT = [mk.sb("PT%d" % i, [128, 1024], BF16) for i in range(4)]
        GBt = [mk.sb("GBt%d" % i, [128, 4, 64], F32) for i in range(2)]
        YB = [mk.sb("YB%d" % i, [128, 4, 64], BF16) for i in range(2)]
        R0 = mk.sb("R0", [128, 8], F32)
        Aa = mk.sb("Aa", [128, 256], F32)
        Bb = mk.sb("Bb", [128, 256], F32)
        sq = mk.sb("sqd", [128, 256], F32)
        ss = mk.sb("ssd", [128, 8], F32)
        psS2 = [mk.ps("psS2_%d" % i, [128, 1024], F32) for i in range(3)]
        psO2 = [mk.ps("psO2_%d" % m, [128, 512], F32) for m in range(2)]
        OT = [mk.sb("OT%d" % m, [VW, 512], F32) for m in range(2)]
        sc = 32 ** -0.5

        def loadh(h):
            mk.dma("sp", QTh[h % 2][0:64, :], QT0.ap[h * 64:(h + 1) * 64, :], QT0.bl(), [QTh[h % 2]])
            for m in range(2):
                mk.dma("sp", KTh[h % 2][m][m * 32:(m + 1) * 32, :], KT0.ap[h * 64 + m * 32:h * 64 + (m + 1) * 32, :], KT0.bl(),
                       [KTh[h % 2][m]])

        loadh(0)
        wst_pf = [mk.sb("wstpf%d" % i, [128, 2048], F32) for i in range(2)]
        load_wo(Wo0, 0, wst_pf, dve_only=True)
        if nlayers == 2:
            load_weights_bf16(Wb1, P["l1_w_in"], 3864, P["l1_norm"], "gT1",
                              segs=[(0, 0, 2432), (2432, 2560, 128), (2560, 2432, 128), (2688, 2688, 3864 - 2688)],
                              st=wst_pf, dve_only=True)
        it = 0
        ep = 0
        pipe = Pipe(2)

        cnt2 = {"it": 0}

        def nxt2():
            i = cnt2["it"]
            cnt2["it"] += 1
            return psS2[i % 3], PT[i % 4]

        def p2_A(pS, k_, q_, j, G, c0):
            for m in range(2):
                mk.mm(pS[:, m * 512 + c0:(m + 1) * 512], k_[m][:, j * 128:(j + 1) * 128],
                      q_[:, G * 512 + c0:(G + 1) * 512], True, True, [k_[m], q_], [pS])

        def p2_B(pS, pt, h, j, G, c0, epi):
            pS3 = pS[:].rearrange("p (m c) -> p m c", m=2)
            pt3 = pt[:].rearrange("p (m c) -> p m c", m=2)
            mk.act(pt3[:, :, c0:512], pS3[:, :, c0:512], AF.Exp, [pS], [pt], scale=sc)
            if j >= 4 * G:
                mk.tt(pt3[:, :, c0:c0 + 128], pt3[:, :, c0:c0 + 128], bcm(tri[:], [128, 2, 128]), ALU.mult, [pt, tri], [pt])
            for m in range(2):
                mk.mm(psO2[m][0:VW, c0:512], Ve[:, j, h * VW:(h + 1) * VW], pt[:, m * 512 + c0:(m + 1) * 512],
                      j == 0, j == 4 * G + 3, [pt, Ve], [psO2[m]])
            if epi is not None:
                epi(pS)

        def p2_epi(par, gbt, h, G, pE):
            for m in range(2):
                mk.act(OT[m][:], psO2[m][0:VW, :], AF.Copy, [psO2[m]], [OT[m]])
                for i in range(4):
                    mk.tr(pE[:, m * 512 + i * VW:m * 512 + (i + 1) * VW], OT[m][:, i * 128:(i + 1) * 128], identf[0:VW, 0:VW],
                          [OT[m], identf], [pE])
            o0 = pE[:, 0:4 * VW].rearrange("p (i d) -> p i d", d=VW)
            o1 = pE[:, 512:512 + 4 * VW].rearrange("p (i d) -> p i d", d=VW)
            mk.recip(R0[:, 0:4], o0[:, :, 64], [pE], [R0])
            mk.recip(R0[:, 4:8], o1[:, :, 64], [pE], [R0])
            mk.ts(R0[:, 4:8], R0[:, 4:8], LAMN[:, 0:1], None, ALU.mult, None, [R0, LAMN], [R0])
            A3 = Aa[:].rearrange("p (i d) -> p i d", d=64)
            B3 = Bb[:].rearrange("p (i d) -> p i d", d=64)
            mk.tt(A3, o0[:, :, 0:64], bc(R0[:, 0:4], [128, 4, 64]), ALU.mult, [pE, R0], [Aa])
            mk.tt(B3, o1[:, :, 0:64], bc(R0[:, 4:8], [128, 4, 64]), ALU.mult, [pE, R0], [Bb])
            mk.tt(Aa[:], Aa[:], Bb[:], ALU.add, [Aa, Bb], [Aa])
            headnorm(Aa, Aa[:], 4, 64, Gsub, sq, ss, ln=True)
            yb = YB[par]
            mk.tt(yb[:], A3, gbt[:], ALU.mult, [Aa, gbt], [yb])
            mk.dma("sp", Y[0].ap[G * 512:(G + 1) * 512, 512 + h * 64:512 + (h + 1) * 64].rearrange("(i p) d -> p i d", p=128),
                   yb[:], [yb], Y[0].bl(G * 4, G * 4 + 4))

        import functools as _ft
        for h in range(8):
            if h + 1 < 8:
                loadh(h + 1)
            q_ = QTh[h % 2]
            k_ = KTh[h % 2]
            for G in range(8):
                par = ep % 2
                ep += 1
                gbt = GBt[par]
                mk.dma("sp", gbt[:], GB0.ap[G * 512:(G + 1) * 512, h * 64:(h + 1) * 64].rearrange("(i p) d -> p i d", p=128),
                       GB0.bl(G * 4, G * 4 + 4), [gbt])
                nj = 4 * G + 4
                for j in range(nj):
                    imin = max(0, j - 4 * G)
                    c0 = imin * 128
                    pS, pt = nxt2()
                    epi = _ft.partial(p2_epi, par, gbt, h, G) if j == nj - 1 else None
                    pipe.push(_ft.partial(p2_A, pS, k_, q_, j, G, c0),
                              _ft.partial(p2_B, pS, pt, h, j, G, c0, epi))
        pipe.flush()

    if stop == 'p2':
        mk.barrier()
        return nc, mk
    if nlayers == 1:
        out_proj(0, x_in, None, out_ap, None, Wo_pre=Wo0)
        sc23.__exit__(None, None, None)
    else:
        out_proj(0, x_in, None, X1.ap, X1.blk, Wo_pre=Wo0)
        sc23.__exit__(None, None, None)
        build_layer1(mk, nc, P, C, locals())
    mk.barrier()
    return nc, mk


def build_layer1(mk, nc, P, C, env):
    g_ = env
    (identb, identf, tri, ntri, epsT, KmT, Vm, X1, Y, QT1, CRT, KST, KWT, VSe, VWe, GT1, GD1, out_ap) = [g_[k] for k in (
        "identb", "identf", "tri", "ntri", "epsT", "KmT", "Vm", "X1", "Y", "QT1", "CRT", "KST", "KWT", "VSe", "VWe",
        "GT1", "GD1", "out_ap")]
    GC1 = g_["GC1"]
    (headnorm, rope, load_bcast, load_weights_bf16, mem_attn, xnorm_to_hT, out_proj, rstd_inplace, mem_attn_parts,
     xnorm_a, xnorm_b) = [g_[k] for k in (
        "headnorm", "rope", "load_bcast", "load_weights_bf16", "mem_attn", "xnorm_to_hT", "out_proj", "rstd_inplace",
        "mem_attn_parts", "xnorm_a", "xnorm_b")]
    stop = g_["stop"]
    NC1 = 3864
    KcT = [mk.sb("KcT%d" % g, [128, 256], BF16) for g in range(2)]
    for g in range(2):
        mk.memset(KcT[g][:], 0.0, [KcT[g]])
    Vc = [mk.sb("Vc%d" % g, [128, 2, VW], BF16) for g in range(2)]
    Gdk = load_bcast("Gdk", P["l1_d_knorm"], 64)

    with mk.scope():
        Wb = g_["Wb1"]
        Gcn = load_bcast("Gcn", P["l1_c_norm"], 512)
        Gdq = load_bcast("Gdq", P["l1_d_qnorm"], 64)
        Gmq = load_bcast("Gmq1", P["l1_m_qnorm"], 64)
        cwT = mk.sb("cwT", [128, 4, 32], F32)
        with mk.scope():
            psX = mk.ps("psX", [128, 512], F32)
            cw = mk.sb("cw", [32, 512], F32)
            mk.dma("sp", cw[0:31, :], P["l1_c_conv_w"], (), [cw])
            for ch in range(4):
                mk.tr(psX[:, ch * 32:ch * 32 + 31], cw[0:31, ch * 128:(ch + 1) * 128], identf[0:31, 0:31], [cw, identf], [psX])
            mk.cp(cwT[:, :, 0:31], psX[:, 0:128].rearrange("p (c j) -> p c j", j=32)[:, :, 0:31], [psX], [cwT])
        Dg = mk.sb("Dg", [128, 4, 31, 128], BF16)
        for ch in range(4):
            for j in range(31):
                mk.ts(Dg[:, ch, j, :], identf[:], cwT[:, ch, j:j + 1], None, ALU.mult, None, [identf, cwT], [Dg],
                      e="dve")
        cbT = mk.sb("cbT", [128, 4], F32)
        mk.dma("sp", cbT[:], P["l1_c_conv_b"].rearrange("(c p) -> p c", p=128), (), [cbT], allow_slow_non_contiguous=True)
        HTr = [mk.sb("HTr%d" % i, [128, 4, 30 + 512], BF16) for i in range(2)]
        mk.memset(HTr[0][:, :, 0:30], 0.0, [HTr[0]])

        xt = [mk.sb("xt%d" % i, [128, D], F32) for i in range(3)]
        cs = [mk.sb("cs%d" % i, [128, 16], F32) for i in range(3)]
        sq = mk.sb("sq", [128, 512], F32)
        ss = mk.sb("ss", [128, 16], F32)
        sqx = mk.sb("sqx", [128, D], F32)
        ssx = mk.sb("ssx", [128, 8], F32)
        xns = [mk.sb("xn%d" % i, [128, D], BF16) for i in range(2)]
        hTs = [mk.sb("hT%d" % i, [128, 8, 128], BF16) for i in range(2)]
        SG = mk.sb("SG", [128, 512], F32)
        Hbs = [mk.sb("Hb%d" % i, [128, 512], BF16) for i in range(2)]
        YC4 = mk.sb("YC4", [128, 4, 512], F32)
        Gcl = [mk.sb("Gcl%d" % i, [128, 512], F32) for i in range(2)]
        ycs = [mk.sb("ycs%d" % i, [128, 512], BF16) for i in range(2)]
        YN = mk.sb("YN", [128, 512], F32)
        Gcs = [mk.sb("Gc%d" % i, [128, 512], F32) for i in range(2)]
        T = mk.sb("T", [128, 512], F32)
        Rr = mk.sb("Rr", [128, 512], F32)
        TbQs = [mk.sb("TbQ%d" % i, [128, 512], BF16) for i in range(2)]
        TbCs = [mk.sb("TbC%d" % i, [128, 512], BF16) for i in range(2)]
        QTs = [mk.sb("QTs%d" % i, [64, 8, 128], BF16) for i in range(2)]
        CRs = [mk.sb("CRs%d" % i, [64, 4, 128], BF16) for i in range(2)]
        KSs = [mk.sb("KSs%d" % i, [64, 2, 128], BF16) for i in range(2)]
        KWs = [mk.sb("KWs%d" % i, [64, 2, 128], BF16) for i in range(2)]
        VSs = [mk.sb("VSs%d" % i, [128, 2 * VW], BF16) for i in range(2)]
        VWs = [mk.sb("VWs%d" % i, [128, 2 * VW], BF16) for i in range(2)]
        GTs = [mk.sb("GTs%d" % i, [128, 24], F32) for i in range(2)]
        GDs = [mk.sb("GDs%d" % i, [128, 512], F32) for i in range(2)]
        Ys = [mk.sb("Ys%d" % i, [128, 768], BF16) for i in range(2)]
        MQ = mk.sb("MQ", [128, 512], F32)
        GMs = [mk.sb("GM%d" % i, [128, 256], F32) for i in range(2)]
        mqbs = [mk.sb("mqb%d" % i, [128, 256], BF16) for i in range(2)]
        mqT = mk.sb("mqT", [128, 4, 128], BF16)
        mk.memset(mqT[:], 0.0, [mqT])
        PTm2 = [mk.sb("PTm%d" % i, [128, 512], BF16) for i in range(2)]
        OM = mk.sb("OM", [128, 256], F32)
        RL = mk.sb("RL", [128, 8], F32)
        psT = mk.ps("psT", [128, 1024], BF16)
        psA = [mk.ps("psA%d" % i, [128, 512], F32) for i in range(4)]
        pctr = {"i": 0}

        def nbank():
            b_ = psA[pctr["i"] % 4]
            pctr["i"] += 1
            return b_

        psQ = mk.ps("psQ", [128, 1024], BF16)
        psS = mk.ps("psS", [128, 512], F32)
        psO = mk.ps("psO", [128, 512], F32)
        psC = psS
        psCt = psS
        for i in range(2):
            mk.memset(VSs[i][:], 1.0, [VSs[i]])
            mk.memset(VWs[i][:], 1.0, [VWs[i]])

        def loads1(tb):
            mk.dma("sp", xt[tb % 3][:], X1.ap[tb * 128:(tb + 1) * 128, :], [X1.blk[tb]], [xt[tb % 3]])
            mk.dma("sp", cs[tb % 3][:, 0:8], C["c_cos1"][tb * 128:(tb + 1) * 128, :], (), [cs[tb % 3]])
            mk.dma("sp", cs[tb % 3][:, 8:16], C["c_sin1"][tb * 128:(tb + 1) * 128, :], (), [cs[tb % 3]])

        def tr_heads(src, ncols_heads, dstbuf, col0=0):
            for h in range(ncols_heads):
                mk.tr(psQ[0:64, h * 128:(h + 1) * 128], src[:, col0 + h * 64:col0 + (h + 1) * 64], identb[:], [src, identb], [psQ])
            mk.cp(dstbuf[:], psQ[0:64, 0:ncols_heads * 128].rearrange("p (r t) -> p r t", t=128), [psQ], [dstbuf], e="act")

        def make_B1(tb):
            par = tb % 2
            rows = slice(tb * 128, (tb + 1) * 128)
            ys = Ys[tb % 2]
            TbQ, TbC, Gc = TbQs[par], TbCs[par], Gcs[par]

            sb_i = tb // 4
            sub = tb % 4
            HTc = HTr[sb_i % 2]

            def b_conv():
                Hb = Hbs[par]
                if sub == 0 and sb_i >= 1:
                    mk.cp(HTc[:, :, 0:30], HTr[(sb_i - 1) % 2][:, :, 512:542], [HTr[(sb_i - 1) % 2]], [HTc])
                for ch in range(4):
                    mk.tr(psQ[:, ch * 128:(ch + 1) * 128], Hb[:, ch * 128:(ch + 1) * 128], identb[:], [Hb, identb], [psQ])
                mk.cp(HTc[:, :, 30 + sub * 128:30 + (sub + 1) * 128], psQ[:, 0:512].rearrange("p (c t) -> p c t", t=128),
                      [psQ], [HTc], e="act")

            def conv_ch(ch):
                def f():
                    for j in range(31):
                        mk.mm(psC[:, 0:512], Dg[:, ch, j, :], HTc[:, ch, j:j + 512], j == 0, j == 30, [Dg, HTc], [psC])
                    mk.ts(YC4[:, ch, :], psC[:, 0:512], cbT[:, ch:ch + 1], None, ALU.add, None, [psC, cbT], [YC4])
                return f

            def post_i(i):
                def f():
                    tbi = sb_i * 4 + i
                    rws = slice(tbi * 128, (tbi + 1) * 128)
                    gl = Gcl[i % 2]
                    yo = ycs[i % 2]
                    mk.dma("sp", gl[:], GC1.ap[rws, :], [GC1.blk[tbi]], [gl])
                    for ch in range(4):
                        mk.tr(psCt[:, ch * 128:(ch + 1) * 128], YC4[:, ch, i * 128:(i + 1) * 128], identf[:], [YC4, identf], [psCt])
                    mk.act(YN[:], psCt[:], AF.Copy, [psCt], [YN])
                    headnorm(YN, YN[:], 1, 512, Gcn, sq2, ss2, gfull=True)
                    mk.act(YN[:], YN[:], AF.Silu, [YN], [YN])
                    mk.tt(yo[:], YN[:], gl[:], ALU.mult, [YN, gl], [yo])
                    mk.dma("sp", Y[1].ap[rws, 0:512], yo[:], [yo], [Y[1].blk[tbi]])
                return f

            def b_q():
                qs = QTs[par]
                tr_heads(TbQ, 8, qs)
                mk.dma("sp", QT1.ap[:, :, rows], qs[:], [qs], [QT1.blk[tb]])

            def b_kv():
                cr = CRs[par]
                tr_heads(TbC, 4, cr)
                mk.dma("sp", CRT.ap[:, :, rows], cr[:], [cr], [CRT.blk[tb]])
                ks = KSs[par]
                tr_heads(TbC, 2, ks, col0=256)
                mk.dma("sp", KST.ap[:, :, rows], ks[:], [ks], [KST.blk[tb]])
                kw = KWs[par]
                tr_heads(TbC, 2, kw, col0=384)
                mk.dma("sp", KWT.ap[:, :, rows], kw[:], [kw], [KWT.blk[tb]])

            def tail():
                mk.dma("sp", Y[1].ap[rows, 1024:1280], ys[:, 512:768], [ys], [Y[1].blk[tb]])

            mparts = mem_attn_parts(1, mqbs[par], mqT, psQ, [psS, psS], psO, PTm2, OM, RL, GMs[par], ys[:, 512:768], ys, tail)
            slow_new = ([conv_ch(c) for c in range(4)] + [post_i(i) for i in range(4)]) if sub == 3 else []
            return [b_conv, b_q, b_kv] + mparts, slow_new

        sq2 = sq
        ss2 = mk.sb("ss2", [128, 8], F32)
        loads1(0)
        loads1(1)
        xnorm_a(xt[0], sqx, ssx, xns[0])
        xnorm_b(xns[0], psT, hTs[0])
        pending = []
        slow = []

        def pop():
            if pending:
                pending.pop(0)()
            elif slow:
                slow.pop(0)()

        for tb in range(NTB):
            if tb + 2 < NTB:
                loads1(tb + 2)
            if tb + 1 < NTB:
                xnorm_a(xt[(tb + 1) % 3], sqx, ssx, xns[(tb + 1) % 2])
            par = tb % 2
            cs_ = cs[tb % 3]
            hT = hTs[par]
            rows = slice(tb * 128, (tb + 1) * 128)

            def proj(pp, c0, n):
                for k in range(8):
                    mk.mm(pp[:, 0:n], hT[:, k, :], Wb[:, k, c0:c0 + n], k == 0, k == 7, [hT, Wb], [pp])

            pa = nbank()
            pb = nbank()
            proj(pa, 0, 512)
            proj(pb, 512, 512)
            mk.act(SG[:], pb[:], AF.Sigmoid, [pb], [SG])
            mk.tt(Hbs[par][:], pa[:], SG[:], ALU.mult, [pa, SG], [Hbs[par]])
            pop()
            pp = nbank()
            proj(pp, 1024, 512)
            mk.act(Gcs[par][:], pp[:], AF.Silu, [pp], [Gcs[par]])
            mk.dma("sp", GC1.ap[rows, :], Gcs[par][:], [Gcs[par]], [GC1.blk[tb]])
            pop()
            pop()
            pp = nbank()
            proj(pp, 1536, 512)
            mk.act(T[:], pp[:], AF.Copy, [pp], [T])
            headnorm(T, T[:], 8, 64, Gdq, sq, ss)
            rope(T, T[:], 8, 64, 8, cs_, cs_[:, 0:8], cs_[:, 8:16], Rr)
            mk.cp(TbQs[par][:], T[:], [T], [TbQs[par]])
            pop()
            pp = nbank()
            proj(pp, 2048, 512)
            mk.act(T[:], pp[:], AF.Copy, [pp], [T])
            TbC = TbCs[par]
            mk.cp(TbC[:, 0:256], T[:, 0:256], [T], [TbC])
            headnorm(T, T[:, 256:512], 4, 64, Gdk, sq, ss)
            rope(T, T[:, 256:512], 4, 64, 8, cs_, cs_[:, 0:8], cs_[:, 8:16], Rr)
            mk.cp(TbC[:, 256:512], T[:, 256:512], [T], [TbC])
            pop()
            pp = nbank()
            proj(pp, 2560, 280)
            vs = VSs[par]
            mk.act(vs[:].rearrange("p (g d) -> p g d", d=VW)[:, :, 0:64], pp[:, 0:128].rearrange("p (g d) -> p g d", d=64),
                   AF.Copy, [pp], [vs])
            mk.dma("sp", VSe.ap[rows, :], vs[:], [vs], [VSe.blk[tb]])
            vw = VWs[par]
            mk.act(vw[:].rearrange("p (g d) -> p g d", d=VW)[:, :, 0:64], pp[:, 128:256].rearrange("p (g d) -> p g d", d=64),
                   AF.Copy, [pp], [vw])
            mk.dma("sp", VWe.ap[rows, :], vw[:], [vw], [VWe.blk[tb]])
            gt = GTs[par]
            mk.act(gt[:], pp[:, 256:280], AF.Sigmoid, [pp], [gt])
            mk.dma("sp", GT1.ap[rows, :], gt[:], [gt], [GT1.blk[tb]])
            pop()
            pp = nbank()
            proj(pp, 2840, 512)
            gd = GDs[par]
            mk.act(gd[:], pp[:], AF.Silu, [pp], [gd])
            mk.dma("sp", GD1.ap[rows, :], gd[:], [gd], [GD1.blk[tb]])
            pop()
            pop()
            pp = nbank()
            proj(pp, 3352, 512)
            mk.act(MQ[:, 0:256], pp[:, 0:256], AF.Copy, [pp], [MQ])
            mk.act(GMs[par][:], pp[:, 256:512], AF.Silu, [pp], [GMs[par]])
            headnorm(MQ, MQ[:, 0:256], 4, 64, Gmq, sq, ss)
            mk.cp(mqbs[par][:], MQ[:, 0:256], [MQ], [mqbs[par]])
            pop()
            while pending:
                pending.pop(0)()
            if tb + 1 < NTB:
                xnorm_b(xns[(tb + 1) % 2], psT, hTs[(tb + 1) % 2])
            fast_new, slow_new = make_B1(tb)
            if slow_new:
                while slow:
                    slow.pop(0)()
                slow.extend(slow_new)
            pending.extend(fast_new)
        while pending:
            pending.pop(0)()
        while slow:
            slow.pop(0)()
    if stop == 'p4':
        return

    with mk.scope():
        Gd1 = Gdk
        cc = mk.sb("cc", [128, 2, 16], F32)
        mk.dma("sp", cc[:, :, 0:8], C["c_cosc"].rearrange("(c p) d -> p c d", p=128), (), [cc])
        mk.dma("sp", cc[:, :, 8:16], C["c_sinc"].rearrange("(c p) d -> p c d", p=128), (), [cc])
        crt = [mk.sb("crt%d" % i, [64, S], BF16) for i in range(2)]
        w1s = mk.sb("w1s", [64, 32, 128], F32)
        w1b = mk.sb("w1b", [128, 32, 128], BF16)
        mk.memset(w1b[:], 0.0, [w1b])
        w2s = mk.sb("w2s", [128, 64], F32)
        w2b = mk.sb("w2b", [128, 64], BF16)
        pl = mk.sb("pl", [32, 64], F32)
        posT = mk.sb("posT", [64, 32], F32)
        A = mk.sb("A", [128, 32, 256], BF16)
        hs = mk.sb("hs", [128, 256], BF16)
        T = mk.sb("Tc", [128, 64], F32)
        Tb = mk.sb("Tcb", [128, 64], BF16)
        Rr = mk.sb("Rrc", [128, 64], F32)
        sq = mk.sb("sqc", [128, 64], F32)
        ss = mk.sb("ssc", [128, 8], F32)
        psH = mk.ps("psH", [128, 512], F32)
        psC2 = mk.ps("psC2", [128, 512], F32)
        psP = mk.ps("psP", [128, 512], F32)
        psK = mk.ps("psK", [128, 1024], BF16)
        mk.memset(A[:], 0.0, [A])
        for g in range(2):
            mk.memset(Vc[g][:], 1.0, [Vc[g]])
        it = 0
        for kv in range(2):
            nm = "k" if kv == 0 else "v"
            mk.dma("sp", w1s[:], P["l1_d_cmp_w1_" + nm].rearrange("(l d) h -> d l h", d=64), (), [w1s])
            mk.cp(w1b[0:64, :, :], w1s[:], [w1s], [w1b], e="act")
            mk.dma("sp", w2s[:], P["l1_d_cmp_w2_" + nm], (), [w2s])
            mk.cp(w2b[:], w2s[:], [w2s], [w2b])
            mk.dma("sp", pl[:], P["l1_d_cmp_pos_" + nm], (), [pl])
            mk.tr(psP[0:64, 0:32], pl[:], identf[0:32, 0:32], [pl, identf], [psP])
            mk.cp(posT[:], psP[0:64, 0:32], [psP], [posT])
            for g in range(2):
                c_ = crt[it % 2]
                it += 1
                mk.dma("sp", c_[:], CRT.ap[:, kv * 2 + g, :], CRT.bl(), [c_])
                for l in range(32):
                    mk.ts(A[0:64, l, 0:255], c_[:, l:l + 16 * 254 + 1:16], posT[:, l:l + 1], None, ALU.add, None,
                          [c_, posT], [A])
                for l in range(32):
                    mk.mm(psH[:, 0:256], w1b[:, l, :], A[:, l, :], l == 0, l == 31, [w1b, A], [psH])
                mk.act(hs[:], psH[:, 0:256], AF.Silu, [psH], [hs])
                for c in range(2):
                    mk.mm(psC2[:, c * 64:(c + 1) * 64], hs[:, c * 128:(c + 1) * 128], w2b[:], True, True, [hs, w2b], [psC2])
                for c in range(2):
                    if kv == 0:
                        mk.act(T[:], psC2[:, c * 64:(c + 1) * 64], AF.Copy, [psC2], [T])
                        headnorm(T, T[:], 1, 64, Gd1, sq, ss)
                        rope(T, T[:], 1, 64, 8, cc, cc[:, c, 0:8], cc[:, c, 8:16], Rr)
                        mk.cp(Tb[:], T[:], [T], [Tb])
                        mk.tr(psK[0:64, 0:128], Tb[:], identb[:], [Tb, identb], [psK])
                        mk.cp(KcT[g][0:64, c * 128:(c + 1) * 128], psK[0:64, 0:128], [psK], [KcT[g]])
                    else:
                        mk.cp(Vc[g][:, c, 0:64], psC2[:, c * 64:(c + 1) * 64], [psC2], [Vc[g]])
    if stop == 'p5':
        return

    with mk.scope():
        import functools as _ft
        cmask = mk.sb("cmask", [128, 2, S], BF16)
        mk.dma("sp", cmask[:], C["c_cmask"].rearrange("(c p) q -> p c q", p=128), (), [cmask])
        OVL = mk.sb("OVL", [128, 2, 64], BF16)
        mk.dma("sp", OVL[:], C["c_ovl"].rearrange("(c p) j -> p c j", p=128), (), [OVL])
        QTg = mk.sb("QTg", [128, 4, S], BF16)
        KsT = mk.sb("KsT", [128, S], BF16)
        KwT = mk.sb("KwT", [128, S], BF16)
        mk.memset(QTg[:], 0.0, [QTg])
        mk.dma("sp", KsT[64:128, :], C["c_E"], (), [KsT])
        mk.memset(KwT[:], 0.0, [KwT])
        Vs = mk.sb("Vs", [128, NTB, VW], BF16)
        Vw = mk.sb("Vw", [128, NTB, VW], BF16)
        PT = [mk.sb("PT%d" % i, [128, 512], BF16) for i in range(4)]
        GTt = [mk.sb("GTt%d" % i, [128, 4, 24], F32) for i in range(2)]
        GDt = [mk.sb("GDt%d" % i, [128, 4, 256], F32) for i in range(2)]
        KA = [mk.sb("KA%d" % i, [128, 4, 128], F32) for i in range(2)]
        OC = mk.sb("OC", [128, 4, 4, 64], F32)
        IMP = mk.sb("IMP", [128, 4, 64], F32)
        IF = mk.sb("IF", [128, 64], F32)
        IW = mk.sb("IW", [128, 64], F32)
        M8 = mk.sb("M8", [128, 16], F32)
        NEG = mk.sb("NEG", [128, 128], BF16)
        mk.memset(NEG[:], 0.0, [NEG])
        RL = mk.sb("RLn", [128, 16], F32)
        YD = mk.sb("YD", [128, 4, 64], F32)
        YT = mk.sb("YT", [128, 4, 64], F32)
        YDb = [mk.sb("YDb%d" % i, [128, 4, 64], BF16) for i in range(2)]
        OTa = mk.sb("OTa", [VW, 512], F32)
        OTb = mk.sb("OTb", [VW, 512], F32)
        tiny = mk.sb("tiny", [128, 1], F32)
        mk.memset(tiny[:], 1e-30, [tiny])
        psS = [mk.ps("psSn%d" % i, [128, 512], F32) for i in range(3)]
        psOa = mk.ps("psOa", [128, 512], F32)
        psOb = mk.ps("psOb", [128, 512], F32)
        psN = mk.ps("psN", [128, 1024], BF16)
        psEa = mk.ps("psEa", [128, 512], F32)
        psEb = mk.ps("psEb", [128, 512], F32)
        cnt = {"it": 0, "ep": 0}
        pipe = Pipe(2)

        def nxt():
            i = cnt["it"]
            cnt["it"] += 1
            return psS[i % 3], PT[i % 4]

        def tr_back(src_ps, OTx, psE, nrow, wout):
            mk.act(OTx[0:nrow, :], src_ps[0:nrow, :], AF.Copy, [src_ps], [OTx])
            for i in range(4):
                mk.tr(psE[:, i * wout:i * wout + nrow], OTx[0:nrow, i * 128:(i + 1) * 128], identf[0:nrow, 0:nrow],
                      [OTx, identf], [psE])

        def cmp_A(pS, g, c, r, qrows):
            mk.mm(pS[:, 0:512], KcT[g][:, c * 128:(c + 1) * 128], QTg[:, r, qrows], True, True, [KcT[g], QTg], [pS])

        def cmp_B(pS, pt, g, c, qrows, first, last, epi):
            mk.act(pt[:], pS[:], AF.Exp, [pS], [pt], scale=0.125)
            mk.tt(pt[:], pt[:], cmask[:, c, qrows], ALU.mult, [pt, cmask], [pt])
            mk.mm(psOa[0:VW, :], Vc[g][:, c, :], pt[:], first, last, [pt, Vc[g]], [psOa])
            mk.mm(psOb[0:64, :], OVL[:, c, :], pt[:], first, last, [pt, OVL], [psOb])
            if epi is not None:
                epi()

        def cmp_epi(r, h, gtt):
            tr_back(psOa, OTa, psEa, VW, VW)
            tr_back(psOb, OTb, psEb, 64, 64)
            oc3 = psEa[:, 0:4 * VW].rearrange("p (i d) -> p i d", d=VW)
            mk.ts(RL[:, 0:4], oc3[:, :, 64], tiny[:, 0:1], None, ALU.max, None, [psEa, tiny], [RL])
            mk.recip(RL[:, 0:4], RL[:, 0:4], [RL], [RL])
            mk.tt(RL[:, 4:8], RL[:, 0:4], gtt[:, :, h * 3], ALU.mult, [RL, gtt], [RL])
            mk.tt(OC[:, r, :, :], oc3[:, :, 0:64], bc(RL[:, 4:8], [128, 4, 64]), ALU.mult, [psEa, RL], [OC])
            im3 = psEb[:, 0:256].rearrange("p (i d) -> p i d", d=64)
            if r == 0:
                mk.tt(IMP[:], im3, bc(RL[:, 0:4], [128, 4, 64]), ALU.mult, [psEb, RL], [IMP])
            else:
                mk.tt(YT[:], im3, bc(RL[:, 0:4], [128, 4, 64]), ALU.mult, [psEb, RL], [YT])
                mk.tt(IMP[:], IMP[:], YT[:], ALU.add, [IMP, YT], [IMP])

        def sel_A(pS, j, r, G, c0):
            mk.mm(pS[:, c0:512], KsT[:, j * 128:(j + 1) * 128], QTg[:, r, G * 512 + c0:(G + 1) * 512], True, True,
                  [KsT, QTg], [pS])

        def sel_B(pS, pt, j, G, c0):
            mk.act(pt[:, c0:512], pS[:, c0:512], AF.Exp, [pS], [pt], scale=0.125)
            if j >= 4 * G:
                mk.tt(pt[:, c0:c0 + 128], pt[:, c0:c0 + 128], tri[:], ALU.mult, [pt, tri], [pt])
            mk.mm(psOa[0:VW, c0:512], Vs[:, j, :], pt[:, c0:512], j == 0, j == 4 * G + 3, [pt, Vs], [psOa])

        def win_A(pS, j, r, G, c0, c1):
            mk.mm(pS[:, c0:c1], KwT[:, j * 128:(j + 1) * 128], QTg[:, r, G * 512 + c0:G * 512 + c1], True, True,
                  [KwT, QTg], [pS])

        def win_B(pS, pt, j, G, c0, c1, rel, first, last, epi):
            mk.act(pt[:, c0:c1], pS[:, c0:c1], AF.Exp, [pS], [pt], scale=0.125)
            if rel >= 0:
                mk.tt(pt[:, rel * 128:(rel + 1) * 128], pt[:, rel * 128:(rel + 1) * 128], tri[:], ALU.mult,
                      [pt, tri], [pt])
            if rel + 4 <= 3:
                a = rel + 4
                mk.tt(pt[:, a * 128:(a + 1) * 128], pt[:, a * 128:(a + 1) * 128], ntri[:], ALU.mult,
                      [pt, ntri], [pt])
            mk.mm(psOb[0:VW, c0:c1], Vw[:, j, :], pt[:, c0:c1], first, last, [pt, Vw], [psOb])
            if epi is not None:
                epi()

        def fin_epi(r, h, g, G, gtt, gdt, yb, qrows):
            tr_back(psOa, OTa, psEa, VW, VW)
            tr_back(psOb, OTb, psEb, VW, VW)
            os3 = psEa[:, 0:4 * VW].rearrange("p (i d) -> p i d", d=VW)
            ow3 = psEb[:, 0:4 * VW].rearrange("p (i d) -> p i d", d=VW)
            mk.recip(RL[:, 8:12], os3[:, :, 64], [psEa], [RL])
            mk.recip(RL[:, 12:16], ow3[:, :, 64], [psEb], [RL])
            mk.tt(RL[:, 8:12], RL[:, 8:12], gtt[:, :, h * 3 + 1], ALU.mult, [RL, gtt], [RL])
            mk.tt(RL[:, 12:16], RL[:, 12:16], gtt[:, :, h * 3 + 2], ALU.mult, [RL, gtt], [RL])
            mk.tt(YD[:], os3[:, :, 0:64], bc(RL[:, 8:12], [128, 4, 64]), ALU.mult, [psEa, RL], [YD])
            mk.tt(YT[:], ow3[:, :, 0:64], bc(RL[:, 12:16], [128, 4, 64]), ALU.mult, [psEb, RL], [YT])
            mk.tt(YD[:], YD[:], YT[:], ALU.add, [YD, YT], [YD])
            mk.tt(YD[:], YD[:], OC[:, r, :, :], ALU.add, [YD, OC], [YD])
            mk.tt(yb[:], YD[:], gdt[:, :, r * 64:(r + 1) * 64], ALU.mult, [YD, gdt], [yb])
            mk.dma("sp", Y[1].ap[qrows, 512 + h * 64:512 + (h + 1) * 64].rearrange("(i p) d -> p i d", p=128),
                   yb[:], [yb], Y[1].bl(G * 4, G * 4 + 4))

        for g in range(2):
            mk.dma("sp", QTg[0:64, :, :], QT1.ap[:, g * 4:(g + 1) * 4, :], QT1.bl(), [QTg])
            mk.dma("sp", KsT[0:64, :], KST.ap[:, g, :], KST.bl(), [KsT])
            mk.dma("sp", KwT[0:64, :], KWT.ap[:, g, :], KWT.bl(), [KwT])
            mk.dma("sp", Vs[:], VSe.ap[:, g * VW:(g + 1) * VW].rearrange("(c p) f -> p c f", p=128), VSe.bl(), [Vs])
            mk.dma("sp", Vw[:], VWe.ap[:, g * VW:(g + 1) * VW].rearrange("(c p) f -> p c f", p=128), VWe.bl(), [Vw])
            for G in range(8):
                par = cnt["ep"] % 2
                cnt["ep"] += 1
                gtt = GTt[par]
                gdt = GDt[par]
                ka = KA[par]
                qrows = slice(G * 512, (G + 1) * 512)
                mk.dma("sp", gtt[:], GT1.ap[qrows, :].rearrange("(i p) d -> p i d", p=128), GT1.bl(G * 4, G * 4 + 4), [gtt])
                mk.dma("sp", gdt[:], GD1.ap[qrows, g * 256:(g + 1) * 256].rearrange("(i p) d -> p i d", p=128),
                       GD1.bl(G * 4, G * 4 + 4), [gdt])
                mk.dma("sp", ka[:, :, 0:64], C["c_keep"][qrows, :].rearrange("(i p) d -> p i d", p=128), (), [ka])
                mk.dma("sp", ka[:, :, 64:128], C["c_addm"][qrows, :].rearrange("(i p) d -> p i d", p=128), (), [ka])
                chunks = [0] if G < 4 else [0, 1]
                for r in range(4):
                    h = g * 4 + r
                    for ci, c in enumerate(chunks):
                        pS, pt = nxt()
                        last = ci == len(chunks) - 1
                        epi = _ft.partial(cmp_epi, r, h, gtt) if last else None
                        pipe.push(_ft.partial(cmp_A, pS, g, c, r, qrows),
                                  _ft.partial(cmp_B, pS, pt, g, c, qrows, ci == 0, last, epi))
                pipe.flush()
                for i in range(4):
                    mk.tt(IF[:], IMP[:, i, :], ka[:, i, 0:64], ALU.mult, [IMP, ka], [IF])
                    mk.tt(IF[:], IF[:], ka[:, i, 64:128], ALU.add, [IF, ka], [IF])
                    mk.op("dve", lambda e: e.max(out=M8[:, 0:8], in_=IF[:]), [IF], [M8])
                    mk.op("dve", lambda e: e.match_replace(out=IW[:], in_to_replace=M8[:, 0:8], in_values=IF[:], imm_value=-9.0),
                          [M8, IF], [IW])
                    mk.op("dve", lambda e: e.max(out=M8[:, 8:16], in_=IW[:]), [IW], [M8])
                    mk.ts(IW[:], IF[:], M8[:, 15:16], None, ALU.is_ge, None, [IF, M8], [IW])
                    mk.ts(NEG[:, 64:128], IW[:], -1.0, 30000.0, ALU.add, ALU.mult, [IW], [NEG])
                    mk.tr(psN[:, i * 128:(i + 1) * 128], NEG[:], identb[:], [NEG, identb], [psN])
                for r in range(4):
                    mk.cp(QTg[64:128, r, qrows], psN[64:128, 0:512], [psN], [QTg], e=("act" if r % 2 else "dve"))
                for r in range(4):
                    h = g * 4 + r
                    for j in range(4 * G + 4):
                        imin = max(0, j - 4 * G)
                        c0 = imin * 128
                        pS, pt = nxt()
                        pipe.push(_ft.partial(sel_A, pS, j, r, G, c0), _ft.partial(sel_B, pS, pt, j, G, c0))
                    j0 = max(0, 4 * G - 4)
                    for j in range(j0, 4 * G + 4):
                        rel = j - 4 * G
                        ilo = max(0, rel)
                        ihi = min(3, rel + 4)
                        c0, c1 = ilo * 128, (ihi + 1) * 128
                        pS, pt = nxt()
                        last = j == 4 * G + 3
                        yb = YDb[r % 2]
                        epi = _ft.partial(fin_epi, r, h, g, G, gtt, gdt, yb, qrows) if last else None
                        pipe.push(_ft.partial(win_A, pS, j, r, G, c0, c1),
                                  _ft.partial(win_B, pS, pt, j, G, c0, c1, rel, j == j0, last, epi))
                pipe.flush()
    if stop == 'p6':
        return
    out_proj(1, X1.ap, X1.blk, out_ap, None)


def host_consts():
    c = {}
    c["c_identb"] = np.eye(128, dtype=np.float32).astype(ml_dtypes.bfloat16)
    c["c_identf"] = np.eye(128, dtype=np.float32)
    k = np.arange(128)[:, None]
    q = np.arange(128)[None, :]
    c["c_tri"] = (k <= q).astype(np.float32).astype(ml_dtypes.bfloat16)
    c["c_ntri"] = (k > q).astype(np.float32).astype(ml_dtypes.bfloat16)

    def cs(pos, half):
        inv = (np.float32(THETA) ** (-np.arange(half, dtype=np.float32) / np.float32(half))).astype(np.float32)
        ang = pos.astype(np.float32)[:, None] * inv[None, :]
        return np.cos(ang).astype(np.float32), np.sin(ang).astype(np.float32)

    pos = np.arange(S)
    c["c_cos0"], c["c_sin0"] = cs(pos, 4)
    c["c_cos1"], c["c_sin1"] = cs(pos, 8)
    n_cmp = (S - 32) // 16 + 1
    cend = np.arange(256) * 16 + 31
    c["c_cosc"], c["c_sinc"] = cs(cend, 8)
    cm = (cend[:, None] <= pos[None, :]).astype(np.float32)
    cm[n_cmp:, :] = 0.0
    c["c_cmask"] = cm.astype(ml_dtypes.bfloat16)
    start = np.arange(256) * 16
    s0 = np.arange(64) * 64
    lo = np.maximum(start[:, None], s0[None, :])
    hi = np.minimum(start[:, None] + 32, s0[None, :] + 64)
    ov = (np.clip(hi - lo, 0, None) / 32.0).astype(np.float32)
    ov[n_cmp:, :] = 0.0
    c["c_ovl"] = ov.astype(ml_dtypes.bfloat16)
    cur = (pos // 64)[:, None]
    j = np.arange(64)[None, :]
    forced = (j == 0) | (j == cur) | (j == cur - 1)
    future = j > cur
    keep = (~forced & ~future).astype(np.float32)
    addm = np.where(forced, 10.0 + j / 64.0, np.where(future, -1.0 - j / 64.0, 0.0)).astype(np.float32)
    c["c_keep"] = keep
    c["c_addm"] = addm
    E = np.zeros((64, 32, 128), np.float32)
    for ch in range(32):
        for kk in range(128):
            E[2 * ch + kk // 64, ch, kk] = 1.0
    c["c_E"] = E.reshape(64, 32 * 128).astype(ml_dtypes.bfloat16)
    return c


_CACHE = {}


def kernel(**inputs):
    nl = 2
    if "nc" not in _CACHE:
        _CACHE["nc"] = build_program(nl)[0]
    nc = _CACHE["nc"]
    consts = host_consts()
    in_maps = []
    for c in range(8):
        b = c % 4
        m = {"x": np.ascontiguousarray(inputs["x"][b]), "mem": np.ascontiguousarray(inputs["mem"][b])}
        for k, v in inputs.items():
            if k not in ("x", "mem"):
                m[k] = np.ascontiguousarray(v)
        m.update(consts)
        in_maps.append(m)
    res = run_bass_kernel_spmd(nc, in_maps, core_ids=list(range(8)))
    out = np.stack([np.asarray(res.results[b]["out"]) for b in range(4)], axis=0)
    return out.astype(np.float32)
```

```python
import contextlib
import numpy as np
import ml_dtypes
import concourse.bass as bass
import concourse.mybir as mybir
from concourse.bass_utils import run_bass_kernel_spmd

F32 = mybir.dt.float32
BF16 = mybir.dt.bfloat16
AF = mybir.ActivationFunctionType
ALU = mybir.AluOpType
AX = mybir.AxisListType

S = 4096
D = 1024
NTB = S // 128
EPS = 1e-6
THETA = 500000.0
VW = 72


class Buf:
    def __init__(self, t, name):
        self.t = t
        self.name = name
        self.w = {}
        self.r = {}

    def __getitem__(self, k):
        return self.t[k]


class MK:
    EPOCH = 60000
    NSLOT = 16

    def __init__(self, nc):
        self.nc = nc
        self.es = contextlib.ExitStack()
        self.scopes = []
        self.eng = {"pe": nc.tensor, "act": nc.scalar, "dve": nc.vector, "pool": nc.gpsimd, "sp": nc.sync}
        self.cnt = {e: 0 for e in self.eng}
        self.dcnt = {"sp": 0, "pool": 0, "act": 0}
        self.sems = {}
        self.known = {e: {} for e in self.eng}
        self.latest = {}
        self.uid = 0

    def sem(self, key):
        if key not in self.sems:
            self.sems[key] = self.es.enter_context(self.nc.semaphore("s_%s_%d" % key))
        return self.sems[key]

    def _stack(self):
        return self.scopes[-1] if self.scopes else self.es

    def sb(self, name, shape, dtype):
        self.uid += 1
        t = self._stack().enter_context(self.nc.sbuf_tensor("%s_%d" % (name, self.uid), list(shape), dtype))
        return Buf(t, name)

    def ps(self, name, shape, dtype):
        self.uid += 1
        t = self._stack().enter_context(self.nc.psum_tensor("%s_%d" % (name, self.uid), list(shape), dtype))
        return Buf(t, name)

    @contextlib.contextmanager
    def scope(self):
        st = contextlib.ExitStack()
        self.scopes.append(st)
        try:
            yield
        finally:
            self.barrier()
            self.scopes.pop()
            st.close()

    def _wait(self, e, deps):
        for key, val in deps.items():
            if key[0] == "pe" and e == "pe":
                continue
            if self.known[e].get(key, 0) >= val:
                continue
            self.eng[e].wait_ge(self.sem(key), val)
            self.known[e][key] = val

    @staticmethod
    def _deps(rd, wr):
        d = {}
        for b in rd:
            for k, v in b.w.items():
                if d.get(k, 0) < v:
                    d[k] = v
        for b in wr:
            for k, v in b.w.items():
                if d.get(k, 0) < v:
                    d[k] = v
            for k, v in b.r.items():
                if d.get(k, 0) < v:
                    d[k] = v
        return d

    def _mark(self, rd, wr, key, val):
        self.latest[key] = val
        for b in wr:
            b.w = {key: val}
            b.r = {}
        for b in rd:
            if b in wr:
                continue
            if b.r.get(key, 0) < val:
                b.r[key] = val

    def op(self, e, fn, rd=(), wr=()):
        self._wait(e, self._deps(rd, wr))
        n = self.cnt[e]
        key = (e, n // self.EPOCH)
        val = n % self.EPOCH + 1
        ins = fn(self.eng[e])
        ins.then_inc(self.sem(key), 1)
        self.cnt[e] = n + 1
        self._mark(rd, wr, key, val)

    def dma(self, q, out, in_, rd=(), wr=(), **kw):
        if q == "sp" and "DRAM" in str(out.space).upper() and "DRAM" not in str(in_.space).upper():
            q = "pool"
        self._wait(q, self._deps(rd, wr))
        n = self.dcnt[q]
        slot = n % self.NSLOT
        key = ("d" + q, slot)
        val = 16 * (n // self.NSLOT + 1)
        if n >= self.NSLOT:
            self._wait(q, {key: val - 16})
        self.eng[q].dma_start(out=out, in_=in_, **kw).then_inc(self.sem(key), 16)
        self.dcnt[q] = n + 1
        self._mark(rd, wr, key, val)

    def barrier(self):
        lt = dict(self.latest)
        for e in self.eng:
            self._wait(e, lt)

    def mm(self, out, lhsT, rhs, start, stop, rd, wr):
        self.op("pe", lambda e: e.matmul(out, lhsT=lhsT, rhs=rhs, start=start, stop=stop, skip_group_check=True), rd, wr)

    def tr(self, out, in_, ident, rd, wr):
        self.op("pe", lambda e: e.transpose(out, in_, ident), rd, wr)

    def act(self, out, in_, func, rd, wr, **kw):
        self.op("act", lambda e: e.activation(out=out, in_=in_, func=func, **kw), rd, wr)

    def tt(self, out, in0, in1, op, rd, wr, e="dve"):
        self.op(e, lambda g: g.tensor_tensor(out=out, in0=in0, in1=in1, op=op), rd, wr)

    def ts(self, out, in0, s1, s2, op0, op1, rd, wr, e="dve"):
        if op1 is None:
            self.op(e, lambda g: g.tensor_scalar(out=out, in0=in0, scalar1=s1, scalar2=None, op0=op0), rd, wr)
        else:
            self.op(e, lambda g: g.tensor_scalar(out=out, in0=in0, scalar1=s1, scalar2=s2, op0=op0, op1=op1), rd, wr)

    def stt(self, out, in0, scalar, in1, op0, op1, rd, wr, e="dve"):
        self.op(e, lambda g: g.scalar_tensor_tensor(out=out, in0=in0, scalar=scalar, in1=in1, op0=op0, op1=op1), rd, wr)

    def red(self, out, in_, rd, wr, op=ALU.add):
        self.op("dve", lambda g: g.tensor_reduce(out=out, in_=in_, axis=AX.X, op=op), rd, wr)

    def recip(self, out, in_, rd, wr):
        self.op("dve", lambda g: g.reciprocal(out=out, in_=in_), rd, wr)

    def cp(self, out, in_, rd, wr, e="dve"):
        if e == "act":
            self.op(e, lambda g: g.activation(out=out, in_=in_, func=AF.Copy), rd, wr)
        else:
            self.op(e, lambda g: g.tensor_copy(out=out, in_=in_), rd, wr)

    def memset(self, ap, val, wr, e="dve"):
        self.op(e, lambda g: g.memset(ap, val), (), wr)


class Pipe:
    def __init__(self, depth):
        self.depth = depth
        self.q = []

    def push(self, A, B):
        A()
        self.q.append(B)
        if len(self.q) > self.depth:
            self.q.pop(0)()

    def flush(self):
        while self.q:
            self.q.pop(0)()


class DS:
    def __init__(self, nc, name, shape, dtype, nblk=NTB):
        self.ap = nc.dram_tensor(name, list(shape), dtype, kind="Internal").ap()
        self.blk = [Buf(None, "%s_%d" % (name, i)) for i in range(nblk)]

    def bl(self, lo=0, hi=None):
        return self.blk[lo:(len(self.blk) if hi is None else hi)]


def bc(ap, shape):
    return ap.unsqueeze(2).to_broadcast(list(shape))


def bcm(ap, shape):
    return ap.unsqueeze(1).to_broadcast(list(shape))


def build_program(nlayers=2, stop=None):
    nc = bass.Bass("TRN2", target_bir_lowering=False)
    mk = MK(nc)

    def din(name, shape, dtype=F32):
        return nc.dram_tensor(name, list(shape), dtype, kind="ExternalInput").ap()

    x_in = din("x", [S, D])
    mem_in = din("mem", [256, D])
    P = {}
    specs = {
        "mem_norm": [D], "l0_norm": [D], "l0_w_in": [D, 4096], "l0_a_vnorm": [512], "l0_a_ws": [4, 128, 128],
        "l0_a_bs": [4, 128], "l0_b_qnorm": [32], "l0_b_knorm": [32], "l0_b_lq1": [32], "l0_b_lk1": [32],
        "l0_b_lq2": [32], "l0_b_lk2": [32], "l0_b_subln": [64], "l0_m_wkv": [D, 512], "l0_m_qnorm": [64],
        "l0_m_knorm": [64], "l0_w_out": [1280, D],
        "l1_norm": [D], "l1_w_in": [D, 3864], "l1_c_conv_w": [31, 512], "l1_c_conv_b": [512], "l1_c_norm": [512],
        "l1_d_qnorm": [64], "l1_d_knorm": [64], "l1_d_cmp_pos_k": [32, 64], "l1_d_cmp_w1_k": [2048, 128],
        "l1_d_cmp_w2_k": [128, 64], "l1_d_cmp_pos_v": [32, 64], "l1_d_cmp_w1_v": [2048, 128],
        "l1_d_cmp_w2_v": [128, 64], "l1_m_wkv": [D, 512], "l1_m_qnorm": [64], "l1_m_knorm": [64],
        "l1_w_out": [1280, D],
    }
    for k, shp in specs.items():
        P[k] = din(k, shp)
    C = {}
    cspecs = {
        "c_identb": ([128, 128], BF16), "c_identf": ([128, 128], F32), "c_tri": ([128, 128], BF16),
        "c_ntri": ([128, 128], BF16), "c_cos0": ([S, 4], F32), "c_sin0": ([S, 4], F32),
        "c_cos1": ([S, 8], F32), "c_sin1": ([S, 8], F32), "c_cosc": ([256, 8], F32), "c_sinc": ([256, 8], F32),
        "c_cmask": ([256, S], BF16), "c_ovl": ([256, 64], BF16), "c_keep": ([S, 64], F32),
        "c_addm": ([S, 64], F32), "c_E": ([64, 32 * 128], BF16),
    }
    for k, (shp, dt) in cspecs.items():
        C[k] = din(k, shp, dt)
    out_ap = nc.dram_tensor("out", [S, D], F32, kind="ExternalOutput").ap()

    X1 = DS(nc, "X1", [S, D], F32)
    Y = [DS(nc, "Y0", [S, 1280], BF16), DS(nc, "Y1", [S, 1280], BF16)]
    QT0 = DS(nc, "QT0", [512, S], BF16)
    KT0 = DS(nc, "KT0", [512, S], BF16)
    V0e = DS(nc, "V0e", [S, 8 * VW], BF16)
    GB0 = DS(nc, "GB0", [S, 512], F32)
    QT1 = DS(nc, "QT1", [64, 8, S], BF16)
    CRT = DS(nc, "CRT", [64, 4, S], BF16)
    KST = DS(nc, "KST", [64, 2, S], BF16)
    KWT = DS(nc, "KWT", [64, 2, S], BF16)
    VSe = DS(nc, "VSe", [S, 2 * VW], BF16)
    VWe = DS(nc, "VWe", [S, 2 * VW], BF16)
    GT1 = DS(nc, "GT1", [S, 24], F32)
    GD1 = DS(nc, "GD1", [S, 512], F32)
    GC1 = DS(nc, "GC1", [S, 512], F32)

    identb = mk.sb("identb", [128, 128], BF16)
    identf = mk.sb("identf", [128, 128], F32)
    tri = mk.sb("tri", [128, 128], BF16)
    ntri = mk.sb("ntri", [128, 128], BF16)
    epsT = mk.sb("epsT", [128, 1], F32)
    mk.dma("sp", identb[:], C["c_identb"], (), [identb])
    mk.dma("sp", identf[:], C["c_identf"], (), [identf])
    mk.dma("sp", tri[:], C["c_tri"], (), [tri])
    mk.dma("sp", ntri[:], C["c_ntri"], (), [ntri])
    mk.memset(epsT[:], EPS, [epsT])

    def load_bcast(name, src, n):
        b = mk.sb(name, [128, n], F32)
        mk.dma("sp", b[:], src.partition_broadcast(128), (), [b])
        return b

    KmT = [mk.sb("KmT%d" % L, [128, 4, 256], BF16) for L in range(2)]
    Vm = [mk.sb("Vm%d" % L, [128, 2, 4 * VW], BF16) for L in range(2)]

    def rstd_inplace(ss_buf, ss_ap, inv_d, ln=False):
        if ln:
            mk.act(ss_ap, ss_ap, AF.Ln, [ss_buf, epsT], [ss_buf], bias=epsT[:, 0:1], scale=inv_d)
            mk.act(ss_ap, ss_ap, AF.Exp, [ss_buf], [ss_buf], scale=-0.5)
            return
        mk.act(ss_ap, ss_ap, AF.Sqrt, [ss_buf, epsT], [ss_buf], bias=epsT[:, 0:1], scale=inv_d)
        mk.recip(ss_ap, ss_ap, [ss_buf], [ss_buf])

    def headnorm(T, Tap, H, hd, G, SQ, SS, gfull=False, ln=True):
        n = H * hd
        T3 = Tap.rearrange("p (h d) -> p h d", d=hd)
        mk.tt(SQ[:, 0:n], Tap, Tap, ALU.mult, [T], [SQ])
        mk.red(SS[:, 0:H], SQ[:, 0:n].rearrange("p (h d) -> p h d", d=hd), [SQ], [SS])
        rstd_inplace(SS, SS[:, 0:H], 1.0 / hd, ln=ln)
        mk.tt(T3, T3, bc(SS[:, 0:H], [128, H, hd]), ALU.mult, [T, SS], [T])
        if gfull:
            mk.tt(Tap, Tap, G[:, 0:n], ALU.mult, [T, G], [T])
        else:
            mk.tt(T3, T3, bcm(G[:, 0:hd], [128, H, hd]), ALU.mult, [T, G], [T])

    def rope(T, Tap, H, hd, half, CS, csap_c, csap_s, R):
        T3 = Tap.rearrange("p (h d) -> p h d", d=hd)
        x1 = T3[:, :, 0:half]
        x2 = T3[:, :, half:2 * half]
        R4 = R[:, 0:H * 4 * half].rearrange("p (h f d) -> p h f d", f=4, d=half)
        cb = bcm(csap_c, [128, H, half])
        sb_ = bcm(csap_s, [128, H, half])
        mk.tt(R4[:, :, 0, :], x1, cb, ALU.mult, [T, CS], [R])
        mk.tt(R4[:, :, 1, :], x2, sb_, ALU.mult, [T, CS], [R])
        mk.tt(R4[:, :, 2, :], x1, sb_, ALU.mult, [T, CS], [R])
        mk.tt(R4[:, :, 3, :], x2, cb, ALU.mult, [T, CS], [R])
        mk.tt(x1, R4[:, :, 0, :], R4[:, :, 1, :], ALU.subtract, [R], [T])
        mk.tt(x2, R4[:, :, 2, :], R4[:, :, 3, :], ALU.add, [R], [T])

    with mk.scope():
        Gmem = load_bcast("Gmem", P["mem_norm"], D)
        memt = mk.sb("memt", [128, 2, D], F32)
        mk.dma("sp", memt[:], mem_in.rearrange("(c p) d -> p c d", p=128), (), [memt])
        sq = mk.sb("sq0", [128, D], F32)
        ss = mk.sb("ss0", [128, 8], F32)
        mnb = mk.sb("mnb", [128, D], BF16)
        mnT = mk.sb("mnT", [128, 8, 256], BF16)
        psT = mk.ps("psT0", [128, 1024], BF16)
        psKV = mk.ps("psKV", [128, 512], F32)
        psK2 = mk.ps("psK2", [128, 1024], BF16)
        for mc in range(2):
            mk.act(sq[:], memt[:, mc, :], AF.Square, [memt], [sq])
            mk.red(ss[:, 0:1], sq[:], [sq], [ss])
            rstd_inplace(ss, ss[:, 0:1], 1.0 / D)
            mk.stt(mnb[:], memt[:, mc, :], ss[:, 0:1], Gmem[:], ALU.mult, ALU.mult, [memt, ss, Gmem], [mnb])
            for k in range(8):
                mk.tr(psT[:, k * 128:(k + 1) * 128], mnb[:, k * 128:(k + 1) * 128], identb[:], [mnb, identb], [psT])
            mk.cp(mnT[:, :, mc * 128:(mc + 1) * 128], psT[:].rearrange("p (k t) -> p k t", t=128), [psT], [mnT])
        wst = mk.sb("wkvst", [128, 8, 512], F32)
        wb = mk.sb("wkvb", [128, 8, 512], BF16)
        kv = mk.sb("kvsb", [128, 512], F32)
        knb = mk.sb("knb", [128, 256], BF16)
        for L in range(nlayers):
            pre = "l%d_" % L
            Gk = load_bcast("Gmk%d" % L, P[pre + "m_knorm"], 64)
            mk.dma("sp", wst[:], P[pre + "m_wkv"].rearrange("(k p) c -> p k c", p=128), (), [wst])
            mk.cp(wb[:], wst[:], [wst], [wb])
            mk.memset(Vm[L][:], 1.0, [Vm[L]])
            mk.memset(KmT[L][:], 0.0, [KmT[L]])
            for mc in range(2):
                for k in range(8):
                    mk.mm(psKV[:], mnT[:, k, mc * 128:(mc + 1) * 128], wb[:, k, :], k == 0, k == 7, [mnT, wb], [psKV])
                mk.act(kv[:], psKV[:], AF.Copy, [psKV], [kv])
                headnorm(kv, kv[:, 0:256], 4, 64, Gk, sq, ss)
                mk.cp(knb[:], kv[:, 0:256], [kv], [knb])
                for h in range(4):
                    mk.tr(psK2[0:64, h * 128:(h + 1) * 128], knb[:, h * 64:(h + 1) * 64], identb[:], [knb, identb], [psK2])
                mk.cp(KmT[L][0:64, :, mc * 128:(mc + 1) * 128], psK2[0:64, 0:512].rearrange("p (r t) -> p r t", t=128), [psK2], [KmT[L]])
                mk.cp(Vm[L][:, mc, :].rearrange("p (h d) -> p h d", d=VW)[:, :, 0:64],
                      kv[:, 256:512].rearrange("p (h d) -> p h d", d=64), [kv], [Vm[L]])

    if stop == 'p0':
        mk.barrier()
        return nc, mk
    def load_weights_bf16(Wb, w_ap, ncols, g_ap, gname, segs=None, st=None, dve_only=False):
        gT = mk.sb(gname, [128, 8], F32)
        mk.dma("sp", gT[:], g_ap.rearrange("(k p) -> p k", p=128), (), [gT], allow_slow_non_contiguous=True)
        if segs is None:
            segs = [(0, 0, ncols)]
        pieces = []
        for (d0, s0, n) in segs:
            o = 0
            while o < n:
                m = min(2048, n - o)
                pieces.append((d0 + o, s0 + o, m))
                o += m
        def body(st_):
            it = 0
            for k in range(8):
                for (d0, s0, m) in pieces:
                    s_ = st_[it % 2]
                    it += 1
                    mk.dma("sp", s_[:, 0:m], w_ap[k * 128:(k + 1) * 128, s0:s0 + m], (), [s_])
                    if it % 2 or dve_only:
                        mk.ts(Wb[:, k, d0:d0 + m], s_[:, 0:m], gT[:, k:k + 1], None, ALU.mult, None, [s_, gT], [Wb])
                    else:
                        mk.act(Wb[:, k, d0:d0 + m], s_[:, 0:m], AF.Copy, [s_, gT], [Wb], scale=gT[:, k:k + 1])
        if st is not None:
            body(st)
        else:
            with mk.scope():
                body([mk.sb("wst%d" % i, [128, 2048], F32) for i in range(2)])

    def load_wo(Wo, L, st, dve_only=False):
        for k in range(10):
            s_ = st[k % 2]
            mk.dma("sp", s_[:, 0:D], P["l%d_w_out" % L][k * 128:(k + 1) * 128, :], (), [s_])
            mk.cp(Wo[:, k, :], s_[:, 0:D], [s_], [Wo], e=("act" if (k % 2 and not dve_only) else "dve"))

    def mem_attn(L, MQ, mqb, mqT, psQ, psS, psO, PTm, OM, RL, GM, GMb, Ym_ap, Ym_buf):
        mk.cp(mqb[:], MQ[:, 0:256], [MQ], [mqb])
        for h in range(4):
            mk.tr(psQ[0:64, h * 128:(h + 1) * 128], mqb[:, h * 64:(h + 1) * 64], identb[:], [mqb, identb], [psQ])
        mk.cp(mqT[0:64, :, :], psQ[0:64, 0:512].rearrange("p (r t) -> p r t", t=128), [psQ], [mqT])
        for pr in range(2):
            for hh in range(2):
                h = pr * 2 + hh
                for mc in range(2):
                    blk = hh * 2 + mc
                    mk.mm(psS[:, blk * 128:(blk + 1) * 128], KmT[L][:, h, mc * 128:(mc + 1) * 128],
                          mqT[:, h, :], True, True, [KmT[L], mqT], [psS])
            mk.act(PTm[:], psS[:], AF.Exp, [psS], [PTm], scale=0.125)
            for hh in range(2):
                h = pr * 2 + hh
                for mc in range(2):
                    blk = hh * 2 + mc
                    mk.mm(psO[:, h * VW:(h + 1) * VW], PTm[:, blk * 128:(blk + 1) * 128],
                          Vm[L][:, mc, h * VW:(h + 1) * VW], mc == 0, mc == 1, [PTm, Vm[L]], [psO])
        o3 = psO[:, 0:4 * VW].rearrange("p (h d) -> p h d", d=VW)
        mk.recip(RL[:, 0:4], o3[:, :, 64], [psO], [RL])
        mk.tt(OM[:, 0:256].rearrange("p (h d) -> p h d", d=64), o3[:, :, 0:64], bc(RL[:, 0:4], [128, 4, 64]),
              ALU.mult, [psO, RL], [OM])
        mk.tt(Ym_ap, OM[:, 0:256], GM, ALU.mult, [OM, GMb], [Ym_buf])

    def mem_attn_parts(L, mqb, mqT, psQ, psS2, psO, PTm2, OM, RL, GMb, Ym_ap, Ym_buf, tail=None):
        def c0():
            for h in range(4):
                mk.tr(psQ[0:64, h * 128:(h + 1) * 128], mqb[:, h * 64:(h + 1) * 64], identb[:], [mqb, identb], [psQ])
            mk.cp(mqT[0:64, :, :], psQ[0:64, 0:512].rearrange("p (r t) -> p r t", t=128), [psQ], [mqT], e="act")
            for pr in range(2):
                for hh in range(2):
                    h = pr * 2 + hh
                    for mc in range(2):
                        blk = hh * 2 + mc
                        mk.mm(psS2[pr][:, blk * 128:(blk + 1) * 128], KmT[L][:, h, mc * 128:(mc + 1) * 128],
                              mqT[:, h, :], True, True, [KmT[L], mqT], [psS2[pr]])
                mk.act(PTm2[pr][:], psS2[pr][:], AF.Exp, [psS2[pr]], [PTm2[pr]], scale=0.125)

        def c1():
            for pr in range(2):
                for hh in range(2):
                    h = pr * 2 + hh
                    for mc in range(2):
                        blk = hh * 2 + mc
                        mk.mm(psO[:, h * VW:(h + 1) * VW], PTm2[pr][:, blk * 128:(blk + 1) * 128],
                              Vm[L][:, mc, h * VW:(h + 1) * VW], mc == 0, mc == 1, [PTm2[pr], Vm[L]], [psO])

        def c2():
            o3 = psO[:, 0:4 * VW].rearrange("p (h d) -> p h d", d=VW)
            mk.recip(RL[:, 0:4], o3[:, :, 64], [psO], [RL])
            mk.tt(OM[:, 0:256].rearrange("p (h d) -> p h d", d=64), o3[:, :, 0:64], bc(RL[:, 0:4], [128, 4, 64]),
                  ALU.mult, [psO, RL], [OM])
            mk.tt(Ym_ap, OM[:, 0:256], GMb[:], ALU.mult, [OM, GMb], [Ym_buf])
            if tail is not None:
                tail()
        return [c0, c1, c2]

    def xnorm_a(xt, sq, ss, xn):
        mk.act(sq[:], xt[:], AF.Square, [xt], [sq])
        mk.red(ss[:, 0:1], sq[:], [sq], [ss])
        rstd_inplace(ss, ss[:, 0:1], 1.0 / D, ln=True)
        mk.act(xn[:], xt[:], AF.Copy, [xt, ss], [xn], scale=ss[:, 0:1])

    def xnorm_b(xn, psT, hT):
        for k in range(8):
            mk.tr(psT[:, k * 128:(k + 1) * 128], xn[:, k * 128:(k + 1) * 128], identb[:], [xn, identb], [psT])
        mk.cp(hT[:], psT[:].rearrange("p (k t) -> p k t", t=128), [psT], [hT])

    def xnorm_to_hT(src_ap, src_deps, xt, sq, ss, xn, psT, hT):
        xnorm_a(xt, sq, ss, xn)
        xnorm_b(xn, psT, hT)

    def out_proj(L, Xres_ap, Xres_blks, dst_ap, dst_blks, Wo_pre=None):
        with mk.scope():
            if Wo_pre is not None:
                Wo = Wo_pre
            else:
                Wo = mk.sb("Wo", [128, 10, D], BF16)
                with mk.scope():
                    load_wo(Wo, L, [mk.sb("wost%d" % i, [128, D], F32) for i in range(2)])
            yt = [mk.sb("yt%d" % i, [128, 1280], BF16) for i in range(3)]
            xr = [mk.sb("xr%d" % i, [128, D], F32) for i in range(3)]
            xo = [mk.sb("xo%d" % i, [128, D], F32) for i in range(2)]
            yTs = [mk.sb("yT%d" % i, [128, 10, 128], BF16) for i in range(2)]
            psYs = [[mk.ps("psY%d_%d" % (j, i), [128, 1024], BF16) for i in range(2)] for j in range(2)]
            psP = [mk.ps("psP%d" % i, [128, 512], F32) for i in range(4)]

            def loads(tb):
                mk.dma("sp", yt[tb % 3][:], Y[L].ap[tb * 128:(tb + 1) * 128, :], [Y[L].blk[tb]], [yt[tb % 3]])
                mk.dma("sp", xr[tb % 3][:], Xres_ap[tb * 128:(tb + 1) * 128, :],
                       [Xres_blks[tb]] if Xres_blks else [], [xr[tb % 3]])

            def front(tb):
                y_ = yt[tb % 3]
                psY = psYs[tb % 2]
                yT = yTs[tb % 2]
                for k in range(10):
                    pT = psY[k // 8]
                    kk = k % 8
                    mk.tr(pT[:, kk * 128:(kk + 1) * 128], y_[:, k * 128:(k + 1) * 128], identb[:], [y_, identb], [pT])
                mk.cp(yT[:, 0:8, :], psY[0][:].rearrange("p (k t) -> p k t", t=128), [psY[0]], [yT])
                mk.cp(yT[:, 8:10, :], psY[1][:, 0:256].rearrange("p (k t) -> p k t", t=128), [psY[1]], [yT], e="act")

            loads(0)
            loads(1)
            front(0)
            pc = 0
            for tb in range(NTB):
                if tb + 2 < NTB:
                    loads(tb + 2)
                if tb + 1 < NTB:
                    front(tb + 1)
                yT = yTs[tb % 2]
                for cg in range(2):
                    pp = psP[pc % 4]
                    pc += 1
                    for k in range(10):
                        mk.mm(pp[:], yT[:, k, :], Wo[:, k, cg * 512:(cg + 1) * 512], k == 0, k == 9, [yT, Wo], [pp])
                    mk.tt(xo[tb % 2][:, cg * 512:(cg + 1) * 512], pp[:], xr[tb % 3][:, cg * 512:(cg + 1) * 512],
                          ALU.add, [pp, xr[tb % 3]], [xo[tb % 2]])
                mk.dma("sp", dst_ap[tb * 128:(tb + 1) * 128, :], xo[tb % 2][:], [xo[tb % 2]],
                       [dst_blks[tb]] if dst_blks else [])

    lam_init0 = 0.8 - 0.6 * float(np.exp(-0.3 * 1))
    LAMN = mk.sb("LAMN", [128, 1], F32)
    with mk.scope():
        Wb = mk.sb("Wb0", [128, 8, 4096], BF16)
        load_weights_bf16(Wb, P["l0_w_in"], 4096, P["l0_norm"], "gT0")
        Gav = load_bcast("Gav", P["l0_a_vnorm"], 512)
        Gq = load_bcast("Gq0", P["l0_b_qnorm"], 32)
        Gk = load_bcast("Gk0", P["l0_b_knorm"], 32)
        Gmq = load_bcast("Gmq0", P["l0_m_qnorm"], 64)
        BsT = mk.sb("BsT", [128, 4], F32)
        mk.dma("sp", BsT[:], P["l0_a_bs"].rearrange("g t -> t g"), (), [BsT], allow_slow_non_contiguous=True)
        lq = [load_bcast("lq%d" % i, P["l0_b_" + n], 32) for i, n in enumerate(["lq1", "lk1", "lq2", "lk2"])]
        ltmp = mk.sb("ltmp", [128, 32], F32)
        lsum = mk.sb("lsum", [128, 2], F32)
        for i in range(2):
            mk.tt(ltmp[:], lq[2 * i][:], lq[2 * i + 1][:], ALU.mult, [lq[2 * i], lq[2 * i + 1]], [ltmp])
            mk.red(lsum[:, i:i + 1], ltmp[:], [ltmp], [lsum])
        mk.act(lsum[:], lsum[:], AF.Exp, [lsum], [lsum])
        mk.tt(LAMN[:], lsum[:, 1:2], lsum[:, 0:1], ALU.subtract, [lsum], [LAMN])
        mk.ts(LAMN[:], LAMN[:], -lam_init0, None, ALU.add, None, [LAMN], [LAMN])
        WsT = mk.sb("WsT", [128, 4, 128], BF16)
        with mk.scope():
            wsl = mk.sb("wsl", [128, 4, 128], F32)
            mk.dma("sp", wsl[:], P["l0_a_ws"].rearrange("g t s -> t g s"), (), [wsl])
            psW = mk.ps("psW", [128, 512], F32)
            for g in range(4):
                mk.tr(psW[:, g * 128:(g + 1) * 128], wsl[:, g, :], identf[:], [wsl, identf], [psW])
            mk.tt(WsT[:], psW[:].rearrange("p (g t) -> p g t", t=128), bcm(tri[:], [128, 4, 128]), ALU.mult,
                  [psW, tri], [WsT])

        if stop == 'p1a':
            mk.barrier()
            return nc, mk
        xt = [mk.sb("xt%d" % i, [128, D], F32) for i in range(3)]
        cs = [mk.sb("cs%d" % i, [128, 8], F32) for i in range(3)]
        sq = mk.sb("sq", [128, D], F32)
        ss = mk.sb("ss", [128, 16], F32)
        sqx = mk.sb("sqx", [128, D], F32)
        ssx = mk.sb("ssx", [128, 8], F32)
        xns = [mk.sb("xn%d" % i, [128, D], BF16) for i in range(2)]
        hTs = [mk.sb("hT%d" % i, [128, 8, 128], BF16) for i in range(2)]
        Us = [mk.sb("U%d" % i, [128, 512], F32) for i in range(2)]
        Vv = mk.sb("Vv", [128, 512], F32)
        Vbs = [mk.sb("Vb%d" % i, [128, 512], BF16) for i in range(2)]
        Gas = [mk.sb("Ga%d" % i, [128, 512], F32) for i in range(2)]
        TA = mk.sb("TA", [128, 512], F32)
        T = mk.sb("T", [128, 512], F32)
        Rr = mk.sb("Rr", [128, 512], F32)
        TbQs = [mk.sb("TbQ%d" % i, [128, 512], BF16) for i in range(2)]
        TbKs = [mk.sb("TbK%d" % i, [128, 512], BF16) for i in range(2)]
        QKs = [mk.sb("QKs%d" % i, [128, 4, 128], BF16) for i in range(2)]
        VS = [mk.sb("VS%d" % i, [128, 8 * VW], BF16) for i in range(2)]
        GBs = [mk.sb("GBs%d" % i, [128, 512], F32) for i in range(2)]
        Ys = [mk.sb("Ys%d" % i, [128, 768], BF16) for i in range(3)]
        MQ = mk.sb("MQ", [128, 512], F32)
        GMs = [mk.sb("GM%d" % i, [128, 256], F32) for i in range(2)]
        mqbs = [mk.sb("mqb%d" % i, [128, 256], BF16) for i in range(2)]
        mqT = mk.sb("mqT", [128, 4, 128], BF16)
        mk.memset(mqT[:], 0.0, [mqT])
        PTm2 = [mk.sb("PTm%d" % i, [128, 512], BF16) for i in range(2)]
        OM = mk.sb("OM", [128, 256], F32)
        RL = mk.sb("RL", [128, 8], F32)
        psT = mk.ps("psT", [128, 1024], BF16)
        psA = [mk.ps("psA%d" % i, [128, 512], F32) for i in range(2)]
        psZ = mk.ps("psZ", [128, 512], F32)
        psQ = mk.ps("psQ", [128, 1024], BF16)
        psS2 = [mk.ps("psS%d" % i, [128, 512], F32) for i in range(2)]
        psO = mk.ps("psO", [128, 512], F32)
        for i in range(2):
            mk.memset(VS[i][:], 1.0, [VS[i]])

        def loads0(tb):
            mk.dma("sp", xt[tb % 3][:], x_in[tb * 128:(tb + 1) * 128, :], (), [xt[tb % 3]])
            mk.dma("sp", cs[tb % 3][:, 0:4], C["c_cos0"][tb * 128:(tb + 1) * 128, :], (), [cs[tb % 3]])
            mk.dma("sp", cs[tb % 3][:, 4:8], C["c_sin0"][tb * 128:(tb + 1) * 128, :], (), [cs[tb % 3]])

        def make_B0(tb):
            par = tb % 2
            rows = slice(tb * 128, (tb + 1) * 128)
            ys = Ys[tb % 3]
            U, Vb, Ga, TbQ, TbK = Us[par], Vbs[par], Gas[par], TbQs[par], TbKs[par]

            def bz():
                for g in range(4):
                    mk.mm(psZ[:, g * 128:(g + 1) * 128], WsT[:, g, :], Vb[:, g * 128:(g + 1) * 128], True, True,
                          [WsT, Vb], [psZ])
                for g in range(4):
                    mk.stt(TA[:, g * 128:(g + 1) * 128], psZ[:, g * 128:(g + 1) * 128], BsT[:, g:g + 1],
                           U[:, g * 128:(g + 1) * 128], ALU.add, ALU.mult, [psZ, BsT, U], [TA])
                mk.tt(ys[:, 0:512], TA[:], Ga[:], ALU.mult, [TA, Ga], [ys])
                mk.dma("sp", Y[0].ap[rows, 0:512], ys[:, 0:512], [ys], [Y[0].blk[tb]])

            def bqk(Tb_, qs, dst):
                def f():
                    for j in range(4):
                        mk.tr(psQ[:, j * 128:(j + 1) * 128], Tb_[:, j * 128:(j + 1) * 128], identb[:], [Tb_, identb], [psQ])
                    mk.cp(qs[:], psQ[:, 0:512].rearrange("p (j t) -> p j t", t=128), [psQ], [qs], e="act")
                    mk.dma("sp", dst.ap.rearrange("(j p) t -> p j t", p=128)[:, :, rows], qs[:], [qs], [dst.blk[tb]])
                return f

            def tail():
                mk.dma("sp", Y[0].ap[rows, 1024:1280], ys[:, 512:768], [ys], [Y[0].blk[tb]])

            mparts = mem_attn_parts(0, mqbs[par], mqT, psQ, psS2, psO, PTm2, OM, RL, GMs[par], ys[:, 512:768], ys, tail)
            return [bz, bqk(TbQ, QKs[0], QT0), bqk(TbK, QKs[1], KT0)] + mparts

        import os as _os
        _ntb = int(_os.environ.get('P1_NTB', NTB))
        loads0(0)
        if _ntb > 1:
            loads0(1)
        xnorm_a(xt[0], sqx, ssx, xns[0])
        xnorm_b(xns[0], psT, hTs[0])
        pi = 0
        pending = []
        for tb in range(_ntb):
            if tb + 2 < _ntb:
                loads0(tb + 2)
            if tb + 1 < _ntb:
                xnorm_a(xt[(tb + 1) % 3], sqx, ssx, xns[(tb + 1) % 2])
            par = tb % 2
            cs_ = cs[tb % 3]
            hT = hTs[par]
            rows = slice(tb * 128, (tb + 1) * 128)
            for cg in range(8):
                pp = psA[pi % 2]
                pi += 1
                for k in range(8):
                    mk.mm(pp[:], hT[:, k, :], Wb[:, k, cg * 512:(cg + 1) * 512], k == 0, k == 7, [hT, Wb], [pp])
                if cg == 0:
                    mk.act(Us[par][:], pp[:], AF.Gelu_apprx_tanh, [pp], [Us[par]])
                elif cg == 1:
                    mk.act(Vv[:], pp[:], AF.Gelu_apprx_tanh, [pp], [Vv])
                    headnorm(Vv, Vv[:], 4, 128, Gav, sq, ss, gfull=True)
                    mk.cp(Vbs[par][:], Vv[:], [Vv], [Vbs[par]])
                elif cg == 2:
                    mk.act(Gas[par][:], pp[:], AF.Silu, [pp], [Gas[par]])
                elif cg in (3, 4):
                    mk.act(T[:], pp[:], AF.Copy, [pp], [T])
                    headnorm(T, T[:], 16, 32, Gq if cg == 3 else Gk, sq, ss)
                    rope(T, T[:], 16, 32, 4, cs_, cs_[:, 0:4], cs_[:, 4:8], Rr)
                    Tb_ = TbQs[par] if cg == 3 else TbKs[par]
                    mk.cp(Tb_[:], T[:], [T], [Tb_])
                elif cg == 5:
                    vs = VS[par]
                    mk.act(vs[:].rearrange("p (h d) -> p h d", d=VW)[:, :, 0:64],
                           pp[:].rearrange("p (h d) -> p h d", d=64), AF.Copy, [pp], [vs])
                    mk.dma("sp", V0e.ap[rows, :], vs[:], [vs], [V0e.blk[tb]])
                elif cg == 6:
                    gb = GBs[par]
                    mk.act(gb[:], pp[:], AF.Silu, [pp], [gb])
                    mk.dma("sp", GB0.ap[rows, :], gb[:], [gb], [GB0.blk[tb]])
                else:
                    mk.act(MQ[:, 0:256], pp[:, 0:256], AF.Copy, [pp], [MQ])
                    mk.act(GMs[par][:], pp[:, 256:512], AF.Silu, [pp], [GMs[par]])
                    headnorm(MQ, MQ[:, 0:256], 4, 64, Gmq, sq, ss)
                    mk.cp(mqbs[par][:], MQ[:, 0:256], [MQ], [mqbs[par]])
                if pending:
                    pending.pop(0)()
            while pending:
                pending.pop(0)()
            if tb + 1 < _ntb:
                xnorm_b(xns[(tb + 1) % 2], psT, hTs[(tb + 1) % 2])
            pending = make_B0(tb)
        while pending:
            pending.pop(0)()

    if stop == 'p1':
        mk.barrier()
        return nc, mk
    Wb1 = mk.sb("Wb1", [128, 8, 3864], BF16) if nlayers == 2 else None
    sc23 = mk.scope()
    sc23.__enter__()
    Wo0 = mk.sb("Wo0", [128, 10, D], BF16)
    with mk.scope():
        Gsub = load_bcast("Gsub", P["l0_b_subln"], 64)
        mk.ts(Gsub[:], Gsub[:], 1.0 - lam_init0, None, ALU.mult, None, [Gsub], [Gsub])
        Ve = mk.sb("Ve", [128, NTB, 8 * VW], BF16)
        Vsrc = V0e.ap.rearrange("(c p) f -> p c f", p=128)
        for c8 in range(8):
            mk.dma("sp", Ve[:, c8 * 4:(c8 + 1) * 4, :], Vsrc[:, c8 * 4:(c8 + 1) * 4, :], V0e.bl(c8 * 4, c8 * 4 + 4), [Ve])
        QTh = [mk.sb("QTh%d" % i, [128, S], BF16) for i in range(2)]
        KTh = [[mk.sb("KTh%d_%d" % (i, m), [128, S], BF16) for m in range(2)] for i in range(2)]
        for i in range(2):
            mk.memset(QTh[i][:], 0.0, [QTh[i]])
            for m in range(2):
                mk.memset(KTh[i][m][:], 0.0, [KTh[i][m]])
        PT = [mk.sb("PT%d" % i, [128, 1024], BF16) for i in range(4)]
        GBt = [mk.sb("GBt%d" % i, [128, 4, 64], F32) for i in range(2)]
        YB = [mk.sb("YB%d" % i, [128, 4, 64], BF16) for i in range(2)]
        R0 = mk.sb("R0", [128, 8], F32)
        Aa = mk.sb("Aa", [128, 256], F32)
        Bb = mk.sb("Bb", [128, 256], F32)
        sq = mk.sb("sqd", [128, 256], F32)
        ss = mk.sb("ssd", [128, 8], F32)
        psS2 = [mk.ps("psS2_%d" % i, [128, 1024], F32) for i in range(3)]
        psO2 = [mk.ps("psO2_%d" % m, [128, 512], F32) for m in range(2)]
        OT = [mk.sb("OT%d" % m, [VW, 512], F32) for m in range(2)]
        sc = 32 ** -0.5

        def loadh(h):
            mk.dma("sp", QTh[h % 2][0:64, :], QT0.ap[h * 64:(h + 1) * 64, :], QT0.bl(), [QTh[h % 2]])
            for m in range(2):
                mk.dma("sp", KTh[h % 2][m][m * 32:(m + 1) * 32, :], KT0.ap[h * 64 + m * 32:h * 64 + (m + 1) * 32, :], KT0.bl(),
                       [KTh[h % 2][m]])

        loadh(0)
        wst_pf = [mk.sb("wstpf%d" % i, [128, 2048], F32) for i in range(2)]
        load_wo(Wo0, 0, wst_pf, dve_only=True)
        if nlayers == 2:
            load_weights_bf16(Wb1, P["l1_w_in"], 3864, P["l1_norm"], "gT1",
                              segs=[(0, 0, 2432), (2432, 2560, 128), (2560, 2432, 128), (2688, 2688, 3864 - 2688)],
                              st=wst_pf, dve_only=True)
        it = 0
        ep = 0
        pipe = Pipe(2)

        cnt2 = {"it": 0}

        def nxt2():
            i = cnt2["it"]
            cnt2["it"] += 1
            return psS2[i % 3], PT[i % 4]

        def p2_A(pS, k_, q_, j, G, c0):
            for m in range(2):
                mk.mm(pS[:, m * 512 + c0:(m + 1) * 512], k_[m][:, j * 128:(j + 1) * 128],
                      q_[:, G * 512 + c0:(G + 1) * 512], True, True, [k_[m], q_], [pS])

        def p2_B(pS, pt, h, j, G, c0, epi):
            pS3 = pS[:].rearrange("p (m c) -> p m c", m=2)
            pt3 = pt[:].rearrange("p (m c) -> p m c", m=2)
            mk.act(pt3[:, :, c0:512], pS3[:, :, c0:512], AF.Exp, [pS], [pt], scale=sc)
            if j >= 4 * G:
                mk.tt(pt3[:, :, c0:c0 + 128], pt3[:, :, c0:c0 + 128], bcm(tri[:], [128, 2, 128]), ALU.mult, [pt, tri], [pt])
            for m in range(2):
                mk.mm(psO2[m][0:VW, c0:512], Ve[:, j, h * VW:(h + 1) * VW], pt[:, m * 512 + c0:(m + 1) * 512],
                      j == 0, j == 4 * G + 3, [pt, Ve], [psO2[m]])
            if epi is not None:
                epi(pS)

        def p2_epi(par, gbt, h, G, pE):
            for m in range(2):
                mk.act(OT[m][:], psO2[m][0:VW, :], AF.Copy, [psO2[m]], [OT[m]])
                for i in range(4):
                    mk.tr(pE[:, m * 512 + i * VW:m * 512 + (i + 1) * VW], OT[m][:, i * 128:(i + 1) * 128], identf[0:VW, 0:VW],
                          [OT[m], identf], [pE])
            o0 = pE[:, 0:4 * VW].rearrange("p (i d) -> p i d", d=VW)
            o1 = pE[:, 512:512 + 4 * VW].rearrange("p (i d) -> p i d", d=VW)
            mk.recip(R0[:, 0:4], o0[:, :, 64], [pE], [R0])
            mk.recip(R0[:, 4:8], o1[:, :, 64], [pE], [R0])
            mk.ts(R0[:, 4:8], R0[:, 4:8], LAMN[:, 0:1], None, ALU.mult, None, [R0, LAMN], [R0])
            A3 = Aa[:].rearrange("p (i d) -> p i d", d=64)
            B3 = Bb[:].rearrange("p (i d) -> p i d", d=64)
            mk.tt(A3, o0[:, :, 0:64], bc(R0[:, 0:4], [128, 4, 64]), ALU.mult, [pE, R0], [Aa])
            mk.tt(B3, o1[:, :, 0:64], bc(R0[:, 4:8], [128, 4, 64]), ALU.mult, [pE, R0], [Bb])
            mk.tt(Aa[:], Aa[:], Bb[:], ALU.add, [Aa, Bb], [Aa])
            headnorm(Aa, Aa[:], 4, 64, Gsub, sq, ss, ln=True)
            yb = YB[par]
            mk.tt(yb[:], A3, gbt[:], ALU.mult, [Aa, gbt], [yb])
            mk.dma("sp", Y[0].ap[G * 512:(G + 1) * 512, 512 + h * 64:512 + (h + 1) * 64].rearrange("(i p) d -> p i d", p=128),
                   yb[:], [yb], Y[0].bl(G * 4, G * 4 + 4))

        import functools as _ft
        for h in range(8):
            if h + 1 < 8:
                loadh(h + 1)
            q_ = QTh[h % 2]
            k_ = KTh[h % 2]
            for G in range(8):
                par = ep % 2
                ep += 1
                gbt = GBt[par]
                mk.dma("sp", gbt[:], GB0.ap[G * 512:(G + 1) * 512, h * 64:(h + 1) * 64].rearrange("(i p) d -> p i d", p=128),
                       GB0.bl(G * 4, G * 4 + 4), [gbt])
                nj = 4 * G + 4
                for j in range(nj):
                    imin = max(0, j - 4 * G)
                    c0 = imin * 128
                    pS, pt = nxt2()
                    epi = _ft.partial(p2_epi, par, gbt, h, G) if j == nj - 1 else None
                    pipe.push(_ft.partial(p2_A, pS, k_, q_, j, G, c0),
                              _ft.partial(p2_B, pS, pt, h, j, G, c0, epi))
        pipe.flush()

    if stop == 'p2':
        mk.barrier()
        return nc, mk
    if nlayers == 1:
        out_proj(0, x_in, None, out_ap, None, Wo_pre=Wo0)
        sc23.__exit__(None, None, None)
    else:
        out_proj(0, x_in, None, X1.ap, X1.blk, Wo_pre=Wo0)
        sc23.__exit__(None, None, None)
        build_layer1(mk, nc, P, C, locals())
    mk.barrier()
    return nc, mk


def build_layer1(mk, nc, P, C, env):
    g_ = env
    (identb, identf, tri, ntri, epsT, KmT, Vm, X1, Y, QT1, CRT, KST, KWT, VSe, VWe, GT1, GD1, out_ap) = [g_[k] for k in (
        "identb", "identf", "tri", "ntri", "epsT", "KmT", "Vm", "X1", "Y", "QT1", "CRT", "KST", "KWT", "VSe", "VWe",
        "GT1", "GD1", "out_ap")]
    GC1 = g_["GC1"]
    (headnorm, rope, load_bcast, load_weights_bf16, mem_attn, xnorm_to_hT, out_proj, rstd_inplace, mem_attn_parts,
     xnorm_a, xnorm_b) = [g_[k] for k in (
        "headnorm", "rope", "load_bcast", "load_weights_bf16", "mem_attn", "xnorm_to_hT", "out_proj", "rstd_inplace",
        "mem_attn_parts", "xnorm_a", "xnorm_b")]
    stop = g_["stop"]
    NC1 = 3864
    KcT = [mk.sb("KcT%d" % g, [128, 256], BF16) for g in range(2)]
    for g in range(2):
        mk.memset(KcT[g][:], 0.0, [KcT[g]])
    Vc = [mk.sb("Vc%d" % g, [128, 2, VW], BF16) for g in range(2)]
    Gdk = load_bcast("Gdk", P["l1_d_knorm"], 64)

    with mk.scope():
        Wb = g_["Wb1"]
        Gcn = load_bcast("Gcn", P["l1_c_norm"], 512)
        Gdq = load_bcast("Gdq", P["l1_d_qnorm"], 64)
        Gmq = load_bcast("Gmq1", P["l1_m_qnorm"], 64)
        cwT = mk.sb("cwT", [128, 4, 32], F32)
        with mk.scope():
            psX = mk.ps("psX", [128, 512], F32)
            cw = mk.sb("cw", [32, 512], F32)
            mk.dma("sp", cw[0:31, :], P["l1_c_conv_w"], (), [cw])
            for ch in range(4):
                mk.tr(psX[:, ch * 32:ch * 32 + 31], cw[0:31, ch * 128:(ch + 1) * 128], identf[0:31, 0:31], [cw, identf], [psX])
            mk.cp(cwT[:, :, 0:31], psX[:, 0:128].rearrange("p (c j) -> p c j", j=32)[:, :, 0:31], [psX], [cwT])
        Dg = mk.sb("Dg", [128, 4, 31, 128], BF16)
        for ch in range(4):
            for j in range(31):
                mk.ts(Dg[:, ch, j, :], identf[:], cwT[:, ch, j:j + 1], None, ALU.mult, None, [identf, cwT], [Dg],
                      e="dve")
        cbT = mk.sb("cbT", [128, 4], F32)
        mk.dma("sp", cbT[:], P["l1_c_conv_b"].rearrange("(c p) -> p c", p=128), (), [cbT], allow_slow_non_contiguous=True)
        HTr = [mk.sb("HTr%d" % i, [128, 4, 30 + 512], BF16) for i in range(2)]
        mk.memset(HTr[0][:, :, 0:30], 0.0, [HTr[0]])

        xt = [mk.sb("xt%d" % i, [128, D], F32) for i in range(3)]
        cs = [mk.sb("cs%d" % i, [128, 16], F32) for i in range(3)]
        sq = mk.sb("sq", [128, 512], F32)
        ss = mk.sb("ss", [128, 16], F32)
        sqx = mk.sb("sqx", [128, D], F32)
        ssx = mk.sb("ssx", [128, 8], F32)
        xns = [mk.sb("xn%d" % i, [128, D], BF16) for i in range(2)]
        hTs = [mk.sb("hT%d" % i, [128, 8, 128], BF16) for i in range(2)]
        SG = mk.sb("SG", [128, 512], F32)
        Hbs = [mk.sb("Hb%d" % i, [128, 512], BF16) for i in range(2)]
        YC4 = mk.sb("YC4", [128, 4, 512], F32)
        Gcl = [mk.sb("Gcl%d" % i, [128, 512], F32) for i in range(2)]
        ycs = [mk.sb("ycs%d" % i, [128, 512], BF16) for i in range(2)]
        YN = mk.sb("YN", [128, 512], F32)
        Gcs = [mk.sb("Gc%d" % i, [128, 512], F32) for i in range(2)]
        T = mk.sb("T", [128, 512], F32)
        Rr = mk.sb("Rr", [128, 512], F32)
        TbQs = [mk.sb("TbQ%d" % i, [128, 512], BF16) for i in range(2)]
        TbCs = [mk.sb("TbC%d" % i, [128, 512], BF16) for i in range(2)]
        QTs = [mk.sb("QTs%d" % i, [64, 8, 128], BF16) for i in range(2)]
        CRs = [mk.sb("CRs%d" % i, [64, 4, 128], BF16) for i in range(2)]
        KSs = [mk.sb("KSs%d" % i, [64, 2, 128], BF16) for i in range(2)]
        KWs = [mk.sb("KWs%d" % i, [64, 2, 128], BF16) for i in range(2)]
        VSs = [mk.sb("VSs%d" % i, [128, 2 * VW], BF16) for i in range(2)]
        VWs = [mk.sb("VWs%d" % i, [128, 2 * VW], BF16) for i in range(2)]
        GTs = [mk.sb("GTs%d" % i, [128, 24], F32) for i in range(2)]
        GDs = [mk.sb("GDs%d" % i, [128, 512], F32) for i in range(2)]
        Ys = [mk.sb("Ys%d" % i, [128, 768], BF16) for i in range(2)]
        MQ = mk.sb("MQ", [128, 512], F32)
        GMs = [mk.sb("GM%d" % i, [128, 256], F32) for i in range(2)]
        mqbs = [mk.sb("mqb%d" % i, [128, 256], BF16) for i in range(2)]
        mqT = mk.sb("mqT", [128, 4, 128], BF16)
        mk.memset(mqT[:], 0.0, [mqT])
        PTm2 = [mk.sb("PTm%d" % i, [128, 512], BF16) for i in range(2)]
        OM = mk.sb("OM", [128, 256], F32)
        RL = mk.sb("RL", [128, 8], F32)
        psT = mk.ps("psT", [128, 1024], BF16)
        psA = [mk.ps("psA%d" % i, [128, 512], F32) for i in range(4)]
        pctr = {"i": 0}

        def nbank():
            b_ = psA[pctr["i"] % 4]
            pctr["i"] += 1
            return b_

        psQ = mk.ps("psQ", [128, 1024], BF16)
        psS = mk.ps("psS", [128, 512], F32)
        psO = mk.ps("psO", [128, 512], F32)
        psC = psS
        psCt = psS
        for i in range(2):
            mk.memset(VSs[i][:], 1.0, [VSs[i]])
            mk.memset(VWs[i][:], 1.0, [VWs[i]])

        def loads1(tb):
            mk.dma("sp", xt[tb % 3][:], X1.ap[tb * 128:(tb + 1) * 128, :], [X1.blk[tb]], [xt[tb % 3]])
            mk.dma("sp", cs[tb % 3][:, 0:8], C["c_cos1"][tb * 128:(tb + 1) * 128, :], (), [cs[tb % 3]])
            mk.dma("sp", cs[tb % 3][:, 8:16], C["c_sin1"][tb * 128:(tb + 1) * 128, :], (), [cs[tb % 3]])

        def tr_heads(src, ncols_heads, dstbuf, col0=0):
            for h in range(ncols_heads):
                mk.tr(psQ[0:64, h * 128:(h + 1) * 128], src[:, col0 + h * 64:col0 + (h + 1) * 64], identb[:], [src, identb], [psQ])
            mk.cp(dstbuf[:], psQ[0:64, 0:ncols_heads * 128].rearrange("p (r t) -> p r t", t=128), [psQ], [dstbuf], e="act")

        def make_B1(tb):
            par = tb % 2
            rows = slice(tb * 128, (tb + 1) * 128)
            ys = Ys[tb % 2]
            TbQ, TbC, Gc = TbQs[par], TbCs[par], Gcs[par]

            sb_i = tb // 4
            sub = tb % 4
            HTc = HTr[sb_i % 2]

            def b_conv():
                Hb = Hbs[par]
                if sub == 0 and sb_i >= 1:
                    mk.cp(HTc[:, :, 0:30], HTr[(sb_i - 1) % 2][:, :, 512:542], [HTr[(sb_i - 1) % 2]], [HTc])
                for ch in range(4):
                    mk.tr(psQ[:, ch * 128:(ch + 1) * 128], Hb[:, ch * 128:(ch + 1) * 128], identb[:], [Hb, identb], [psQ])
                mk.cp(HTc[:, :, 30 + sub * 128:30 + (sub + 1) * 128], psQ[:, 0:512].rearrange("p (c t) -> p c t", t=128),
                      [psQ], [HTc], e="act")

            def conv_ch(ch):
                def f():
                    for j in range(31):
                        mk.mm(psC[:, 0:512], Dg[:, ch, j, :], HTc[:, ch, j:j + 512], j == 0, j == 30, [Dg, HTc], [psC])
                    mk.ts(YC4[:, ch, :], psC[:, 0:512], cbT[:, ch:ch + 1], None, ALU.add, None, [psC, cbT], [YC4])
                return f

            def post_i(i):
                def f():
                    tbi = sb_i * 4 + i
                    rws = slice(tbi * 128, (tbi + 1) * 128)
                    gl = Gcl[i % 2]
                    yo = ycs[i % 2]
                    mk.dma("sp", gl[:], GC1.ap[rws, :], [GC1.blk[tbi]], [gl])
                    for ch in range(4):
                        mk.tr(psCt[:, ch * 128:(ch + 1) * 128], YC4[:, ch, i * 128:(i + 1) * 128], identf[:], [YC4, identf], [psCt])
                    mk.act(YN[:], psCt[:], AF.Copy, [psCt], [YN])
                    headnorm(YN, YN[:], 1, 512, Gcn, sq2, ss2, gfull=True)
                    mk.act(YN[:], YN[:], AF.Silu, [YN], [YN])
                    mk.tt(yo[:], YN[:], gl[:], ALU.mult, [YN, gl], [yo])
                    mk.dma("sp", Y[1].ap[rws, 0:512], yo[:], [yo], [Y[1].blk[tbi]])
                return f

            def b_q():
                qs = QTs[par]
                tr_heads(TbQ, 8, qs)
                mk.dma("sp", QT1.ap[:, :, rows], qs[:], [qs], [QT1.blk[tb]])

            def b_kv():
                cr = CRs[par]
                tr_heads(TbC, 4, cr)
                mk.dma("sp", CRT.ap[:, :, rows], cr[:], [cr], [CRT.blk[tb]])
                ks = KSs[par]
                tr_heads(TbC, 2, ks, col0=256)
                mk.dma("sp", KST.ap[:, :, rows], ks[:], [ks], [KST.blk[tb]])
                kw = KWs[par]
                tr_heads(TbC, 2, kw, col0=384)
                mk.dma("sp", KWT.ap[:, :, rows], kw[:], [kw], [KWT.blk[tb]])

            def tail():
                mk.dma("sp", Y[1].ap[rows, 1024:1280], ys[:, 512:768], [ys], [Y[1].blk[tb]])

            mparts = mem_attn_parts(1, mqbs[par], mqT, psQ, [psS, psS], psO, PTm2, OM, RL, GMs[par], ys[:, 512:768], ys, tail)
            slow_new = ([conv_ch(c) for c in range(4)] + [post_i(i) for i in range(4)]) if sub == 3 else []
            return [b_conv, b_q, b_kv] + mparts, slow_new

        sq2 = sq
        ss2 = mk.sb("ss2", [128, 8], F32)
        loads1(0)
        loads1(1)
        xnorm_a(xt[0], sqx, ssx, xns[0])
        xnorm_b(xns[0], psT, hTs[0])
        pending = []
        slow = []

        def pop():
            if pending:
                pending.pop(0)()
            elif slow:
                slow.pop(0)()

        for tb in range(NTB):
            if tb + 2 < NTB:
                loads1(tb + 2)
            if tb + 1 < NTB:
                xnorm_a(xt[(tb + 1) % 3], sqx, ssx, xns[(tb + 1) % 2])
            par = tb % 2
            cs_ = cs[tb % 3]
            hT = hTs[par]
            rows = slice(tb * 128, (tb + 1) * 128)

            def proj(pp, c0, n):
                for k in range(8):
                    mk.mm(pp[:, 0:n], hT[:, k, :], Wb[:, k, c0:c0 + n], k == 0, k == 7, [hT, Wb], [pp])

            pa = nbank()
            pb = nbank()
            proj(pa, 0, 512)
            proj(pb, 512, 512)
            mk.act(SG[:], pb[:], AF.Sigmoid, [pb], [SG])
            mk.tt(Hbs[par][:], pa[:], SG[:], ALU.mult, [pa, SG], [Hbs[par]])
            pop()
            pp = nbank()
            proj(pp, 1024, 512)
            mk.act(Gcs[par][:], pp[:], AF.Silu, [pp], [Gcs[par]])
            mk.dma("sp", GC1.ap[rows, :], Gcs[par][:], [Gcs[par]], [GC1.blk[tb]])
            pop()
            pop()
            pp = nbank()
            proj(pp, 1536, 512)
            mk.act(T[:], pp[:], AF.Copy, [pp], [T])
            headnorm(T, T[:], 8, 64, Gdq, sq, ss)
            rope(T, T[:], 8, 64, 8, cs_, cs_[:, 0:8], cs_[:, 8:16], Rr)
            mk.cp(TbQs[par][:], T[:], [T], [TbQs[par]])
            pop()
            pp = nbank()
            proj(pp, 2048, 512)
            mk.act(T[:], pp[:], AF.Copy, [pp], [T])
            TbC = TbCs[par]
            mk.cp(TbC[:, 0:256], T[:, 0:256], [T], [TbC])
            headnorm(T, T[:, 256:512], 4, 64, Gdk, sq, ss)
            rope(T, T[:, 256:512], 4, 64, 8, cs_, cs_[:, 0:8], cs_[:, 8:16], Rr)
            mk.cp(TbC[:, 256:512], T[:, 256:512], [T], [TbC])
            pop()
            pp = nbank()
            proj(pp, 2560, 280)
            vs = VSs[par]
            mk.act(vs[:].rearrange("p (g d) -> p g d", d=VW)[:, :, 0:64], pp[:, 0:128].rearrange("p (g d) -> p g d", d=64),
                   AF.Copy, [pp], [vs])
            mk.dma("sp", VSe.ap[rows, :], vs[:], [vs], [VSe.blk[tb]])
            vw = VWs[par]
            mk.act(vw[:].rearrange("p (g d) -> p g d", d=VW)[:, :, 0:64], pp[:, 128:256].rearrange("p (g d) -> p g d", d=64),
                   AF.Copy, [pp], [vw])
            mk.dma("sp", VWe.ap[rows, :], vw[:], [vw], [VWe.blk[tb]])
            gt = GTs[par]
            mk.act(gt[:], pp[:, 256:280], AF.Sigmoid, [pp], [gt])
            mk.dma("sp", GT1.ap[rows, :], gt[:], [gt], [GT1.blk[tb]])
            pop()
            pp = nbank()
            proj(pp, 2840, 512)
            gd = GDs[par]
            mk.act(gd[:], pp[:], AF.Silu, [pp], [gd])
            mk.dma("sp", GD1.ap[rows, :], gd[:], [gd], [GD1.blk[tb]])
            pop()
            pop()
            pp = nbank()
            proj(pp, 3352, 512)
            mk.act(MQ[:, 0:256], pp[:, 0:256], AF.Copy, [pp], [MQ])
            mk.act(GMs[par][:], pp[:, 256:512], AF.Silu, [pp], [GMs[par]])
            headnorm(MQ, MQ[:, 0:256], 4, 64, Gmq, sq, ss)
            mk.cp(mqbs[par][:], MQ[:, 0:256], [MQ], [mqbs[par]])
            pop()
            while pending:
                pending.pop(0)()
            if tb + 1 < NTB:
                xnorm_b(xns[(tb + 1) % 2], psT, hTs[(tb + 1) % 2])
            fast_new, slow_new = make_B1(tb)
            if slow_new:
                while slow:
                    slow.pop(0)()
                slow.extend(slow_new)
            pending.extend(fast_new)
        while pending:
            pending.pop(0)()
        while slow:
            slow.pop(0)()
    if stop == 'p4':
        return

    with mk.scope():
        Gd1 = Gdk
        cc = mk.sb("cc", [128, 2, 16], F32)
        mk.dma("sp", cc[:, :, 0:8], C["c_cosc"].rearrange("(c p) d -> p c d", p=128), (), [cc])
        mk.dma("sp", cc[:, :, 8:16], C["c_sinc"].rearrange("(c p) d -> p c d", p=128), (), [cc])
        crt = [mk.sb("crt%d" % i, [64, S], BF16) for i in range(2)]
        w1s = mk.sb("w1s", [64, 32, 128], F32)
        w1b = mk.sb("w1b", [128, 32, 128], BF16)
        mk.memset(w1b[:], 0.0, [w1b])
        w2s = mk.sb("w2s", [128, 64], F32)
        w2b = mk.sb("w2b", [128, 64], BF16)
        pl = mk.sb("pl", [32, 64], F32)
        posT = mk.sb("posT", [64, 32], F32)
        A = mk.sb("A", [128, 32, 256], BF16)
        hs = mk.sb("hs", [128, 256], BF16)
        T = mk.sb("Tc", [128, 64], F32)
        Tb = mk.sb("Tcb", [128, 64], BF16)
        Rr = mk.sb("Rrc", [128, 64], F32)
        sq = mk.sb("sqc", [128, 64], F32)
        ss = mk.sb("ssc", [128, 8], F32)
        psH = mk.ps("psH", [128, 512], F32)
        psC2 = mk.ps("psC2", [128, 512], F32)
        psP = mk.ps("psP", [128, 512], F32)
        psK = mk.ps("psK", [128, 1024], BF16)
        mk.memset(A[:], 0.0, [A])
        for g in range(2):
            mk.memset(Vc[g][:], 1.0, [Vc[g]])
        it = 0
        for kv in range(2):
            nm = "k" if kv == 0 else "v"
            mk.dma("sp", w1s[:], P["l1_d_cmp_w1_" + nm].rearrange("(l d) h -> d l h", d=64), (), [w1s])
            mk.cp(w1b[0:64, :, :], w1s[:], [w1s], [w1b], e="act")
            mk.dma("sp", w2s[:], P["l1_d_cmp_w2_" + nm], (), [w2s])
            mk.cp(w2b[:], w2s[:], [w2s], [w2b])
            mk.dma("sp", pl[:], P["l1_d_cmp_pos_" + nm], (), [pl])
            mk.tr(psP[0:64, 0:32], pl[:], identf[0:32, 0:32], [pl, identf], [psP])
            mk.cp(posT[:], psP[0:64, 0:32], [psP], [posT])
            for g in range(2):
                c_ = crt[it % 2]
                it += 1
                mk.dma("sp", c_[:], CRT.ap[:, kv * 2 + g, :], CRT.bl(), [c_])
                for l in range(32):
                    mk.ts(A[0:64, l, 0:255], c_[:, l:l + 16 * 254 + 1:16], posT[:, l:l + 1], None, ALU.add, None,
                          [c_, posT], [A])
                for l in range(32):
                    mk.mm(psH[:, 0:256], w1b[:, l, :], A[:, l, :], l == 0, l == 31, [w1b, A], [psH])
                mk.act(hs[:], psH[:, 0:256], AF.Silu, [psH], [hs])
                for c in range(2):
                    mk.mm(psC2[:, c * 64:(c + 1) * 64], hs[:, c * 128:(c + 1) * 128], w2b[:], True, True, [hs, w2b], [psC2])
                for c in range(2):
                    if kv == 0:
                        mk.act(T[:], psC2[:, c * 64:(c + 1) * 64], AF.Copy, [psC2], [T])
                        headnorm(T, T[:], 1, 64, Gd1, sq, ss)
                        rope(T, T[:], 1, 64, 8, cc, cc[:, c, 0:8], cc[:, c, 8:16], Rr)
                        mk.cp(Tb[:], T[:], [T], [Tb])
                        mk.tr(psK[0:64, 0:128], Tb[:], identb[:], [Tb, identb], [psK])
                        mk.cp(KcT[g][0:64, c * 128:(c + 1) * 128], psK[0:64, 0:128], [psK], [KcT[g]])
                    else:
                        mk.cp(Vc[g][:, c, 0:64], psC2[:, c * 64:(c + 1) * 64], [psC2], [Vc[g]])
    if stop == 'p5':
        return

    with mk.scope():
        import functools as _ft
        cmask = mk.sb("cmask", [128, 2, S], BF16)
        mk.dma("sp", cmask[:], C["c_cmask"].rearrange("(c p) q -> p c q", p=128), (), [cmask])
        OVL = mk.sb("OVL", [128, 2, 64], BF16)
        mk.dma("sp", OVL[:], C["c_ovl"].rearrange("(c p) j -> p c j", p=128), (), [OVL])
        QTg = mk.sb("QTg", [128, 4, S], BF16)
        KsT = mk.sb("KsT", [128, S], BF16)
        KwT = mk.sb("KwT", [128, S], BF16)
        mk.memset(QTg[:], 0.0, [QTg])
        mk.dma("sp", KsT[64:128, :], C["c_E"], (), [KsT])
        mk.memset(KwT[:], 0.0, [KwT])
        Vs = mk.sb("Vs", [128, NTB, VW], BF16)
        Vw = mk.sb("Vw", [128, NTB, VW], BF16)
        PT = [mk.sb("PT%d" % i, [128, 512], BF16) for i in range(4)]
        GTt = [mk.sb("GTt%d" % i, [128, 4, 24], F32) for i in range(2)]
        GDt = [mk.sb("GDt%d" % i, [128, 4, 256], F32) for i in range(2)]
        KA = [mk.sb("KA%d" % i, [128, 4, 128], F32) for i in range(2)]
        OC = mk.sb("OC", [128, 4, 4, 64], F32)
        IMP = mk.sb("IMP", [128, 4, 64], F32)
        IF = mk.sb("IF", [128, 64], F32)
        IW = mk.sb("IW", [128, 64], F32)
        M8 = mk.sb("M8", [128, 16], F32)
        NEG = mk.sb("NEG", [128, 128], BF16)
        mk.memset(NEG[:], 0.0, [NEG])
        RL = mk.sb("RLn", [128, 16], F32)
        YD = mk.sb("YD", [128, 4, 64], F32)
        YT = mk.sb("YT", [128, 4, 64], F32)
        YDb = [mk.sb("YDb%d" % i, [128, 4, 64], BF16) for i in range(2)]
        OTa = mk.sb("OTa", [VW, 512], F32)
        OTb = mk.sb("OTb", [VW, 512], F32)
        tiny = mk.sb("tiny", [128, 1], F32)
        mk.memset(tiny[:], 1e-30, [tiny])
        psS = [mk.ps("psSn%d" % i, [128, 512], F32) for i in range(3)]
        psOa = mk.ps("psOa", [128, 512], F32)
        psOb = mk.ps("psOb", [128, 512], F32)
        psN = mk.ps("psN", [128, 1024], BF16)
        psEa = mk.ps("psEa", [128, 512], F32)
        psEb = mk.ps("psEb", [128, 512], F32)
        cnt = {"it": 0, "ep": 0}
        pipe = Pipe(2)

        def nxt():
            i = cnt["it"]
            cnt["it"] += 1
            return psS[i % 3], PT[i % 4]

        def tr_back(src_ps, OTx, psE, nrow, wout):
            mk.act(OTx[0:nrow, :], src_ps[0:nrow, :], AF.Copy, [src_ps], [OTx])
            for i in range(4):
                mk.tr(psE[:, i * wout:i * wout + nrow], OTx[0:nrow, i * 128:(i + 1) * 128], identf[0:nrow, 0:nrow],
                      [OTx, identf], [psE])

        def cmp_A(pS, g, c, r, qrows):
            mk.mm(pS[:, 0:512], KcT[g][:, c * 128:(c + 1) * 128], QTg[:, r, qrows], True, True, [KcT[g], QTg], [pS])

        def cmp_B(pS, pt, g, c, qrows, first, last, epi):
            mk.act(pt[:], pS[:], AF.Exp, [pS], [pt], scale=0.125)
            mk.tt(pt[:], pt[:], cmask[:, c, qrows], ALU.mult, [pt, cmask], [pt])
            mk.mm(psOa[0:VW, :], Vc[g][:, c, :], pt[:], first, last, [pt, Vc[g]], [psOa])
            mk.mm(psOb[0:64, :], OVL[:, c, :], pt[:], first, last, [pt, OVL], [psOb])
            if epi is not None:
                epi()

        def cmp_epi(r, h, gtt):
            tr_back(psOa, OTa, psEa, VW, VW)
            tr_back(psOb, OTb, psEb, 64, 64)
            oc3 = psEa[:, 0:4 * VW].rearrange("p (i d) -> p i d", d=VW)
            mk.ts(RL[:, 0:4], oc3[:, :, 64], tiny[:, 0:1], None, ALU.max, None, [psEa, tiny], [RL])
            mk.recip(RL[:, 0:4], RL[:, 0:4], [RL], [RL])
            mk.tt(RL[:, 4:8], RL[:, 0:4], gtt[:, :, h * 3], ALU.mult, [RL, gtt], [RL])
            mk.tt(OC[:, r, :, :], oc3[:, :, 0:64], bc(RL[:, 4:8], [128, 4, 64]), ALU.mult, [psEa, RL], [OC])
            im3 = psEb[:, 0:256].rearrange("p (i d) -> p i d", d=64)
            if r == 0:
                mk.tt(IMP[:], im3, bc(RL[:, 0:4], [128, 4, 64]), ALU.mult, [psEb, RL], [IMP])
            else:
                mk.tt(YT[:], im3, bc(RL[:, 0:4], [128, 4, 64]), ALU.mult, [psEb, RL], [YT])
                mk.tt(IMP[:], IMP[:], YT[:], ALU.add, [IMP, YT], [IMP])

        def sel_A(pS, j, r, G, c0):
            mk.mm(pS[:, c0:512], KsT[:, j * 128:(j + 1) * 128], QTg[:, r, G * 512 + c0:(G + 1) * 512], True, True,
                  [KsT, QTg], [pS])

        def sel_B(pS, pt, j, G, c0):
            mk.act(pt[:, c0:512], pS[:, c0:512], AF.Exp, [pS], [pt], scale=0.125)
            if j >= 4 * G:
                mk.tt(pt[:, c0:c0 + 128], pt[:, c0:c0 + 128], tri[:], ALU.mult, [pt, tri], [pt])
            mk.mm(psOa[0:VW, c0:512], Vs[:, j, :], pt[:, c0:512], j == 0, j == 4 * G + 3, [pt, Vs], [psOa])

        def win_A(pS, j, r, G, c0, c1):
            mk.mm(pS[:, c0:c1], KwT[:, j * 128:(j + 1) * 128], QTg[:, r, G * 512 + c0:G * 512 + c1], True, True,
                  [KwT, QTg], [pS])

        def win_B(pS, pt, j, G, c0, c1, rel, first, last, epi):
            mk.act(pt[:, c0:c1], pS[:, c0:c1], AF.Exp, [pS], [pt], scale=0.125)
            if rel >= 0:
                mk.tt(pt[:, rel * 128:(rel + 1) * 128], pt[:, rel * 128:(rel + 1) * 128], tri[:], ALU.mult,
                      [pt, tri], [pt])
            if rel + 4 <= 3:
                a = rel + 4
                mk.tt(pt[:, a * 128:(a + 1) * 128], pt[:, a * 128:(a + 1) * 128], ntri[:], ALU.mult,
                      [pt, ntri], [pt])
            mk.mm(psOb[0:VW, c0:c1], Vw[:, j, :], pt[:, c0:c1], first, last, [pt, Vw], [psOb])
            if epi is not None:
                epi()

        def fin_epi(r, h, g, G, gtt, gdt, yb, qrows):
            tr_back(psOa, OTa, psEa, VW, VW)
            tr_back(psOb, OTb, psEb, VW, VW)
            os3 = psEa[:, 0:4 * VW].rearrange("p (i d) -> p i d", d=VW)
            ow3 = psEb[:, 0:4 * VW].rearrange("p (i d) -> p i d", d=VW)
            mk.recip(RL[:, 8:12], os3[:, :, 64], [psEa], [RL])
            mk.recip(RL[:, 12:16], ow3[:, :, 64], [psEb], [RL])
            mk.tt(RL[:, 8:12], RL[:, 8:12], gtt[:, :, h * 3 + 1], ALU.mult, [RL, gtt], [RL])
            mk.tt(RL[:, 12:16], RL[:, 12:16], gtt[:, :, h * 3 + 2], ALU.mult, [RL, gtt], [RL])
            mk.tt(YD[:], os3[:, :, 0:64], bc(RL[:, 8:12], [128, 4, 64]), ALU.mult, [psEa, RL], [YD])
            mk.tt(YT[:], ow3[:, :, 0:64], bc(RL[:, 12:16], [128, 4, 64]), ALU.mult, [psEb, RL], [YT])
            mk.tt(YD[:], YD[:], YT[:], ALU.add, [YD, YT], [YD])
            mk.tt(YD[:], YD[:], OC[:, r, :, :], ALU.add, [YD, OC], [YD])
            mk.tt(yb[:], YD[:], gdt[:, :, r * 64:(r + 1) * 64], ALU.mult, [YD, gdt], [yb])
            mk.dma("sp", Y[1].ap[qrows, 512 + h * 64:512 + (h + 1) * 64].rearrange("(i p) d -> p i d", p=128),
                   yb[:], [yb], Y[1].bl(G * 4, G * 4 + 4))

        for g in range(2):
            mk.dma("sp", QTg[0:64, :, :], QT1.ap[:, g * 4:(g + 1) * 4, :], QT1.bl(), [QTg])
            mk.dma("sp", KsT[0:64, :], KST.ap[:, g, :], KST.bl(), [KsT])
            mk.dma("sp", KwT[0:64, :], KWT.ap[:, g, :], KWT.bl(), [KwT])
            mk.dma("sp", Vs[:], VSe.ap[:, g * VW:(g + 1) * VW].rearrange("(c p) f -> p c f", p=128), VSe.bl(), [Vs])
            mk.dma("sp", Vw[:], VWe.ap[:, g * VW:(g + 1) * VW].rearrange("(c p) f -> p c f", p=128), VWe.bl(), [Vw])
            for G in range(8):
                par = cnt["ep"] % 2
                cnt["ep"] += 1
                gtt = GTt[par]
                gdt = GDt[par]
                ka = KA[par]
                qrows = slice(G * 512, (G + 1) * 512)
                mk.dma("sp", gtt[:], GT1.ap[qrows, :].rearrange("(i p) d -> p i d", p=128), GT1.bl(G * 4, G * 4 + 4), [gtt])
                mk.dma("sp", gdt[:], GD1.ap[qrows, g * 256:(g + 1) * 256].rearrange("(i p) d -> p i d", p=128),
                       GD1.bl(G * 4, G * 4 + 4), [gdt])
                mk.dma("sp", ka[:, :, 0:64], C["c_keep"][qrows, :].rearrange("(i p) d -> p i d", p=128), (), [ka])
                mk.dma("sp", ka[:, :, 64:128], C["c_addm"][qrows, :].rearrange("(i p) d -> p i d", p=128), (), [ka])
                chunks = [0] if G < 4 else [0, 1]
                for r in range(4):
                    h = g * 4 + r
                    for ci, c in enumerate(chunks):
                        pS, pt = nxt()
                        last = ci == len(chunks) - 1
                        epi = _ft.partial(cmp_epi, r, h, gtt) if last else None
                        pipe.push(_ft.partial(cmp_A, pS, g, c, r, qrows),
                                  _ft.partial(cmp_B, pS, pt, g, c, qrows, ci == 0, last, epi))
                pipe.flush()
                for i in range(4):
                    mk.tt(IF[:], IMP[:, i, :], ka[:, i, 0:64], ALU.mult, [IMP, ka], [IF])
                    mk.tt(IF[:], IF[:], ka[:, i, 64:128], ALU.add, [IF, ka], [IF])
                    mk.op("dve", lambda e: e.max(out=M8[:, 0:8], in_=IF[:]), [IF], [M8])
                    mk.op("dve", lambda e: e.match_replace(out=IW[:], in_to_replace=M8[:, 0:8], in_values=IF[:], imm_value=-9.0),
                          [M8, IF], [IW])
                    mk.op("dve", lambda e: e.max(out=M8[:, 8:16], in_=IW[:]), [IW], [M8])
                    mk.ts(IW[:], IF[:], M8[:, 15:16], None, ALU.is_ge, None, [IF, M8], [IW])
                    mk.ts(NEG[:, 64:128], IW[:], -1.0, 30000.0, ALU.add, ALU.mult, [IW], [NEG])
                    mk.tr(psN[:, i * 128:(i + 1) * 128], NEG[:], identb[:], [NEG, identb], [psN])
                for r in range(4):
                    mk.cp(QTg[64:128, r, qrows], psN[64:128, 0:512], [psN], [QTg], e=("act" if r % 2 else "dve"))
                for r in range(4):
                    h = g * 4 + r
                    for j in range(4 * G + 4):
                        imin = max(0, j - 4 * G)
                        c0 = imin * 128
                        pS, pt = nxt()
                        pipe.push(_ft.partial(sel_A, pS, j, r, G, c0), _ft.partial(sel_B, pS, pt, j, G, c0))
                    j0 = max(0, 4 * G - 4)
                    for j in range(j0, 4 * G + 4):
                        rel = j - 4 * G
                        ilo = max(0, rel)
                        ihi = min(3, rel + 4)
                        c0, c1 = ilo * 128, (ihi + 1) * 128
                        pS, pt = nxt()
                        last = j == 4 * G + 3
                        yb = YDb[r % 2]
                        epi = _ft.partial(fin_epi, r, h, g, G, gtt, gdt, yb, qrows) if last else None
                        pipe.push(_ft.partial(win_A, pS, j, r, G, c0, c1),
                                  _ft.partial(win_B, pS, pt, j, G, c0, c1, rel, j == j0, last, epi))
                pipe.flush()
    if stop == 'p6':
        return
    out_proj(1, X1.ap, X1.blk, out_ap, None)


def host_consts():
    c = {}
    c["c_identb"] = np.eye(128, dtype=np.float32).astype(ml_dtypes.bfloat16)
    c["c_identf"] = np.eye(128, dtype=np.float32)
    k = np.arange(128)[:, None]
    q = np.arange(128)[None, :]
    c["c_tri"] = (k <= q).astype(np.float32).astype(ml_dtypes.bfloat16)
    c["c_ntri"] = (k > q).astype(np.float32).astype(ml_dtypes.bfloat16)

    def cs(pos, half):
        inv = (np.float32(THETA) ** (-np.arange(half, dtype=np.float32) / np.float32(half))).astype(np.float32)
        ang = pos.astype(np.float32)[:, None] * inv[None, :]
        return np.cos(ang).astype(np.float32), np.sin(ang).astype(np.float32)

    pos = np.arange(S)
    c["c_cos0"], c["c_sin0"] = cs(pos, 4)
    c["c_cos1"], c["c_sin1"] = cs(pos, 8)
    n_cmp = (S - 32) // 16 + 1
    cend = np.arange(256) * 16 + 31
    c["c_cosc"], c["c_sinc"] = cs(cend, 8)
    cm = (cend[:, None] <= pos[None, :]).astype(np.float32)
    cm[n_cmp:, :] = 0.0
    c["c_cmask"] = cm.astype(ml_dtypes.bfloat16)
    start = np.arange(256) * 16
    s0 = np.arange(64) * 64
    lo = np.maximum(start[:, None], s0[None, :])
    hi = np.minimum(start[:, None] + 32, s0[None, :] + 64)
    ov = (np.clip(hi - lo, 0, None) / 32.0).astype(np.float32)
    ov[n_cmp:, :] = 0.0
    c["c_ovl"] = ov.astype(ml_dtypes.bfloat16)
    cur = (pos // 64)[:, None]
    j = np.arange(64)[None, :]
    forced = (j == 0) | (j == cur) | (j == cur - 1)
    future = j > cur
    keep = (~forced & ~future).astype(np.float32)
    addm = np.where(forced, 10.0 + j / 64.0, np.where(future, -1.0 - j / 64.0, 0.0)).astype(np.float32)
    c["c_keep"] = keep
    c["c_addm"] = addm
    E = np.zeros((64, 32, 128), np.float32)
    for ch in range(32):
        for kk in range(128):
            E[2 * ch + kk // 64, ch, kk] = 1.0
    c["c_E"] = E.reshape(64, 32 * 128).astype(ml_dtypes.bfloat16)
    return c


_CACHE = {}


def kernel(**inputs):
    nl = 2
    if "nc" not in _CACHE:
        _CACHE["nc"] = build_program(nl)[0]
    nc = _CACHE["nc"]
    consts = host_consts()
    in_maps = []
    for c in range(8):
        b = c % 4
        m = {"x": np.ascontiguousarray(inputs["x"][b]), "mem": np.ascontiguousarray(inputs["mem"][b])}
        for k, v in inputs.items():
            if k not in ("x", "mem"):
                m[k] = np.ascontiguousarray(v)
        m.update(consts)
        in_maps.append(m)
    res = run_bass_kernel_spmd(nc, in_maps, core_ids=list(range(8)))
    out = np.stack([np.asarray(res.results[b]["out"]) for b in range(4)], axis=0)
    return out.astype(np.float32)
```
